# Optimizing a Trainium2 kernel written in Bass

```python
import functools
import jax, jax.numpy as jnp
from jax import lax
import numpy as np

D_MODEL = 1024
BATCH = 32
SEQ = 2048
DEPTH = 4
DEC_BATCH = 8
DEC_SEQ = 32
PAST_LEN = 1024

CHUNK = 64
N_META = 16
H_GLA = 4
DK_GLA = D_MODEL // 2 // H_GLA
DV_GLA = D_MODEL // H_GLA
GATE_RANK = 16
GATE_TAU = 16.0
H_RET = 4
DK_RET = D_MODEL // 2 // H_RET
DV_RET = D_MODEL // H_RET
D_FF = 2816
ROPE_BASE = 10000.0
LN_EPS = 1e-5
GN_EPS = 1e-5
DN_ALPHA = (2.0 * DEPTH) ** 0.25
DN_BETA = (8.0 * DEPTH) ** -0.25
IN_SIZES = (H_GLA * DK_GLA, H_GLA * DK_GLA, H_GLA * DV_GLA, H_GLA * DV_GLA, GATE_RANK,
            H_RET * DK_RET, H_RET * DK_RET, H_RET * DV_RET, H_RET * DV_RET, 2 * D_MODEL)
N_IN = sum(IN_SIZES)

kernel_name = 'gla_retnet_macaron_deepnorm_meta_stream'

F32 = jnp.float32


def layer_norm(x, g, b):
    xf = x.astype(F32)
    mu = xf.mean(-1, keepdims=True)
    var = jnp.mean(jnp.square(xf - mu), -1, keepdims=True)
    return ((xf - mu) * lax.rsqrt(var + LN_EPS) * g.astype(F32) + b.astype(F32)).astype(x.dtype)


def head_norm(o, g):
    mu = o.mean(-1, keepdims=True)
    var = jnp.mean(jnp.square(o - mu), -1, keepdims=True)
    on = (o - mu) * lax.rsqrt(var + GN_EPS)
    return on.reshape(o.shape[0], o.shape[1], -1) * g.astype(F32)


def swiglu(x, w_up, w_down):
    a, b = jnp.split(x @ w_up, 2, axis=-1)
    return (jax.nn.silu(a) * b) @ w_down


def rotary(x, pos):
    half = x.shape[-1] // 2
    inv = ROPE_BASE ** (-jnp.arange(half, dtype=F32) / half)
    ang = pos.astype(F32)[:, None] * inv[None, :]
    cos = jnp.cos(ang)[None, :, None, :]
    sin = jnp.sin(ang)[None, :, None, :]
    x1, x2 = x[..., :half], x[..., half:]
    return jnp.concatenate([x1 * cos - x2 * sin, x2 * cos + x1 * sin], axis=-1)


def gla_block(s0, q, k, v, lg):
    L = q.shape[1]
    b = jnp.cumsum(lg, axis=1)
    o_inter = jnp.einsum('blhd,bhde->blhe', q * jnp.exp(b), s0)
    causal = (jnp.arange(L)[:, None] >= jnp.arange(L)[None, :])[None, :, :, None, None]
    dec = jnp.exp(jnp.where(causal, b[:, :, None] - b[:, None, :], -jnp.inf))
    att = jnp.einsum('bthd,bshd,btshd->bhts', q, k, dec)
    o = o_inter + jnp.einsum('bhts,bshe->bthe', att, v)
    b_last = b[:, -1]
    s_new = jnp.exp(b_last)[..., None] * s0 + jnp.einsum(
        'bshd,bshe->bhde', k * jnp.exp(b_last[:, None] - b), v)
    return o, s_new


def ret_block(s0, q, k, v, log_gamma):
    L = q.shape[1]
    idx = jnp.arange(L, dtype=F32)
    inter = jnp.exp((idx + 1.0)[:, None] * log_gamma[None, :])
    o_inter = jnp.einsum('blhd,bhde->blhe', q, s0) * inter[None, :, :, None]
    rel = idx[:, None] - idx[None, :]
    dmat = jnp.exp(jnp.where(rel[..., None] >= 0, rel[..., None] * log_gamma, -jnp.inf))
    att = jnp.einsum('bthd,bshd->bhts', q, k) * jnp.transpose(dmat, (2, 0, 1))[None]
    o = o_inter + jnp.einsum('bhts,bshe->bthe', att, v)
    tail = jnp.exp((L - 1.0 - idx)[:, None] * log_gamma[None, :])
    s_new = jnp.exp(L * log_gamma)[None, :, None, None] * s0 + jnp.einsum(
        'bshd,bshe->bhde', k * tail[None, :, :, None], v)
    return o, s_new


def run_mixer(step, s0, xs, prompt):
    if not prompt:
        return step(s0, *xs)
    o_meta, s = step(s0, *tuple(a[:, :N_META] for a in xs))
    B, T = xs[0].shape[:2]
    t_rest = T - N_META
    n = t_rest // CHUNK
    chunks = tuple(a[:, N_META:].reshape(B, n, CHUNK, *a.shape[2:]).swapaxes(0, 1) for a in xs)

    def body(carry, c):
        o, carry = step(carry, *c)
        return carry, o

    s, o = lax.scan(body, s, chunks)
    o = o.swapaxes(0, 1).reshape(B, t_rest, *o.shape[3:])
    return jnp.concatenate([o_meta, o], axis=1), s


def layer(h, pos, s_gla, s_ret, prompt, ln_g, ln_b, w1u, w1d, w_in, w_a2, b_a, b_m,
          g_gla, g_ret, w_oa, w_ob, w_o, w2u, w2d):
    B, T, _ = h.shape
    h = layer_norm(DN_ALPHA * h + 0.5 * swiglu(h, w1u, w1d), ln_g[0], ln_b[0])
    z = h @ w_in
    cuts = [int(c) for c in np.cumsum(IN_SIZES)[:-1]]
    qg, kg, vg, rg, ag, qr, kr, vr, gr, mg = jnp.split(z, cuts, axis=-1)
    q = qg.astype(F32).reshape(B, T, H_GLA, DK_GLA) * DK_GLA ** -0.5
    k = kg.astype(F32).reshape(B, T, H_GLA, DK_GLA)
    v = vg.astype(F32).reshape(B, T, H_GLA, DV_GLA)
    lg = (jax.nn.log_sigmoid((ag @ w_a2 + b_a).astype(F32)) / GATE_TAU).reshape(B, T, H_GLA, DK_GLA)
    o_a, s_gla = run_mixer(gla_block, s_gla.astype(F32), (q, k, v, lg), prompt)
    ya = (jax.nn.silu(rg) * head_norm(o_a, g_gla).astype(h.dtype)) @ w_oa
    log_gamma = jnp.log1p(-(2.0 ** (-5.0 - jnp.arange(H_RET, dtype=F32))))
    q = rotary(qr.astype(F32).reshape(B, T, H_RET, DK_RET), pos)
    k = rotary(kr.astype(F32).reshape(B, T, H_RET, DK_RET), pos) * DK_RET ** -0.5
    v = vr.astype(F32).reshape(B, T, H_RET, DV_RET)
    step_r = functools.partial(ret_block, log_gamma=log_gamma)
    o_b, s_ret = run_mixer(step_r, s_ret.astype(F32), (q, k, v), prompt)
    yb = (jax.nn.silu(gr) * head_norm(o_b, g_ret).astype(h.dtype)) @ w_ob
    ga, gb = jnp.split(jax.nn.sigmoid(mg + b_m), 2, axis=-1)
    h = layer_norm(DN_ALPHA * h + (ga * ya + gb * yb) @ w_o, ln_g[1], ln_b[1])
    h = layer_norm(DN_ALPHA * h + 0.5 * swiglu(h, w2u, w2d), ln_g[2], ln_b[2])
    return h, s_gla, s_ret


def setup_inputs(seed: int = 0) -> dict:
    key = jax.random.key(seed)
    ks = jax.random.split(key, 24)
    n = jax.random.normal
    d = D_MODEL
    return {
        'x_prompt': n(ks[0], (BATCH, SEQ, d), F32),
        'x_sample': n(ks[1], (DEC_BATCH, DEC_SEQ, d), F32),
        'state_gla': n(ks[2], (DEPTH, DEC_BATCH, H_GLA, DK_GLA, DV_GLA), F32),
        'state_ret': n(ks[3], (DEPTH, DEC_BATCH, H_RET, DK_RET, DV_RET), F32),
        'meta': n(ks[4], (N_META, d), F32),
        'ln_g': 1.0 + 0.02 * n(ks[5], (DEPTH, 3, d), F32),
        'ln_b': 0.02 * n(ks[6], (DEPTH, 3, d), F32),
        'w_ffn1_up': n(ks[7], (DEPTH, d, 2 * D_FF), F32) * d ** -0.5,
        'w_ffn1_down': n(ks[8], (DEPTH, D_FF, d), F32) * (D_FF ** -0.5 * DN_BETA),
        'w_in': n(ks[9], (DEPTH, d, N_IN), F32) * d ** -0.5,
        'w_alpha2': n(ks[10], (DEPTH, GATE_RANK, H_GLA * DK_GLA), F32) * GATE_RANK ** -0.5,
        'b_alpha': 0.1 * n(ks[11], (DEPTH, H_GLA * DK_GLA), F32),
        'b_merge': 0.1 * n(ks[12], (DEPTH, 2 * d), F32),
        'gn_gla': 1.0 + 0.02 * n(ks[13], (DEPTH, H_GLA * DV_GLA), F32),
        'gn_ret': 1.0 + 0.02 * n(ks[14], (DEPTH, H_RET * DV_RET), F32),
        'w_o_gla': n(ks[15], (DEPTH, H_GLA * DV_GLA, d), F32) * ((H_GLA * DV_GLA) ** -0.5 * DN_BETA),
        'w_o_ret': n(ks[16], (DEPTH, H_RET * DV_RET, d), F32) * ((H_RET * DV_RET) ** -0.5 * DN_BETA),
        'w_out': n(ks[17], (DEPTH, d, d), F32) * (d ** -0.5 * DN_BETA),
        'w_ffn2_up': n(ks[18], (DEPTH, d, 2 * D_FF), F32) * d ** -0.5,
        'w_ffn2_down': n(ks[19], (DEPTH, D_FF, d), F32) * (D_FF ** -0.5 * DN_BETA),
    }


def reference(x_prompt, x_sample, state_gla, state_ret, meta, ln_g, ln_b, w_ffn1_up, w_ffn1_down,
              w_in, w_alpha2, b_alpha, b_merge, gn_gla, gn_ret, w_o_gla, w_o_ret, w_out,
              w_ffn2_up, w_ffn2_down):
    b_p = x_prompt.shape[0]
    hp = jnp.concatenate(
        [jnp.broadcast_to(meta[None].astype(x_prompt.dtype), (b_p, N_META, D_MODEL)), x_prompt], axis=1)
    hs = x_sample
    pos_p = jnp.arange(N_META + x_prompt.shape[1], dtype=jnp.int32)
    pos_s = N_META + PAST_LEN + jnp.arange(x_sample.shape[1], dtype=jnp.int32)
    zg = jnp.zeros((b_p, H_GLA, DK_GLA, DV_GLA), F32)
    zr = jnp.zeros((b_p, H_RET, DK_RET, DV_RET), F32)
    gla_p, ret_p, gla_s, ret_s = [], [], [], []
    for l in range(DEPTH):
        lw = (ln_g[l], ln_b[l], w_ffn1_up[l], w_ffn1_down[l], w_in[l], w_alpha2[l], b_alpha[l],
              b_merge[l], gn_gla[l], gn_ret[l], w_o_gla[l], w_o_ret[l], w_out[l],
              w_ffn2_up[l], w_ffn2_down[l])
        hp, sg, sr = layer(hp, pos_p, zg, zr, True, *lw)
        gla_p.append(sg.astype(x_prompt.dtype))
        ret_p.append(sr.astype(x_prompt.dtype))
        hs, sg, sr = layer(hs, pos_s, state_gla[l], state_ret[l], False, *lw)
        gla_s.append(sg.astype(state_gla.dtype))
        ret_s.append(sr.astype(state_ret.dtype))
    y_prompt = hp[:, N_META:]
    y_sample = hs
    new_gla_prompt = jnp.stack(gla_p, axis=0)
    new_ret_prompt = jnp.stack(ret_p, axis=0)
    new_gla_sample = jnp.stack(gla_s, axis=0)
    new_ret_sample = jnp.stack(ret_s, axis=0)
    return (y_prompt, y_sample, new_gla_prompt, new_ret_prompt, new_gla_sample, new_ret_sample)
```

```python
from contextlib import ExitStack
import math
import numpy as np
import concourse.bass as bass
import concourse.mybir as mybir
from concourse.bass_utils import run_bass_kernel_spmd

F32 = mybir.dt.float32
BF16 = mybir.dt.bfloat16
AF = mybir.ActivationFunctionType
ALU = mybir.AluOpType

D = 1024
DFF = 2816
NIN = 8208
DEPTH = 4
N_META = 16
PAST_LEN = 1024
LN_EPS = 1e-5
GN_EPS = 1e-5
DN_ALPHA = (2.0 * DEPTH) ** 0.25
LVARS = (128, 16, 32)
LOFF = {128: 0, 16: 128, 32: 144}
LIDX = {128: 0, 16: 1, 32: 2}
NSLOT = 8
DEBUG_STOP = 99
MIX_STOP = 99
CORE_STOP = 99
NOSTORE = 0
SAME_SYNC = True


class Sem:
    __slots__ = ("h", "val")

    def __init__(self, h):
        self.h = h
        self.val = 0


class Buf:
    __slots__ = ("w", "r", "name", "x")

    def __init__(self, name="", x=False):
        self.w = None
        self.r = {}
        self.name = name
        self.x = x


class Eng:
    def __init__(self, name, sem, same_sync):
        self.name = name
        self.sem = sem
        self.waited = {}
        self.prog = []
        self.same_sync = same_sync


class Tracker:
    def __init__(self, nc, stack):
        self.nc = nc
        self.stack = stack
        self.pe = Eng("pe", self.new_sem("s_pe"), False)
        self.act = Eng("act", self.new_sem("s_act"), SAME_SYNC)
        self.dve = Eng("dve", self.new_sem("s_dve"), SAME_SYNC)
        self.pool = Eng("pool", self.new_sem("s_pool"), SAME_SYNC)
        self.sp = Eng("sp", self.new_sem("s_sp"), False)
        self.out_sems = []

    def new_sem(self, name):
        return Sem(self.stack.enter_context(self.nc.semaphore(name)))

    def _deps(self, eng, reads, writes):
        deps = {}
        for b in reads:
            if b.w is not None:
                s, v = b.w
                if deps.get(s, 0) < v:
                    deps[s] = v
            if b.x:
                for s, v in b.r.items():
                    if deps.get(s, 0) < v:
                        deps[s] = v
        for b in writes:
            if b.w is not None:
                s, v = b.w
                if deps.get(s, 0) < v:
                    deps[s] = v
            for s, v in b.r.items():
                if deps.get(s, 0) < v:
                    deps[s] = v
        for s, v in deps.items():
            if s is eng.sem and not eng.same_sync:
                continue
            if eng.waited.get(s, 0) < v:
                eng.prog.append((0, s, v))
                eng.waited[s] = v

    def op(self, eng, fn, reads=(), writes=()):
        self._deps(eng, reads, writes)
        s = eng.sem
        s.val += 1
        tok = (s, s.val)
        eng.prog.append((1, fn, s, 1))
        for b in writes:
            b.w = tok
            b.r = {}
        for b in reads:
            b.r[s] = s.val
        return tok

    def dma(self, qeng, sem, fn, reads=(), writes=(), n=1, is_out=False):
        self._deps(qeng, reads, writes)
        sem.val += 16 * n
        tok = (sem, sem.val)
        qeng.prog.append((1, fn, sem, 16))
        for b in writes:
            b.w = tok
            b.r = {}
        for b in reads:
            b.r[sem] = sem.val
        if is_out and sem not in self.out_sems:
            self.out_sems.append(sem)
        return tok

    def finish(self):
        for s in self.out_sems:
            self.sp.prog.append((0, s, s.val))

    def replay(self):
        def run(prog, e):
            for it in prog:
                if it[0] == 0:
                    e.wait_ge(it[1].h, it[2])
                else:
                    r = it[1](e)
                    if isinstance(r, (list, tuple)):
                        for x in r:
                            x.then_inc(it[2].h, it[3])
                    else:
                        r.then_inc(it[2].h, it[3])

        with self.nc.Block() as block:
            @block.tensor
            def _(e):
                run(self.pe.prog, e)

            @block.scalar
            def _(e):
                run(self.act.prog, e)

            @block.vector
            def _(e):
                run(self.dve.prog, e)

            @block.gpsimd
            def _(e):
                run(self.pool.prog, e)

            @block.sync
            def _(e):
                run(self.sp.prog, e)


def build_program(n_seq, seq_len, depth, nsub, with_meta=True, with_sample=True):
    NT = 128 * nsub
    assert seq_len % NT == 0
    tiles_per_seq = seq_len // NT
    nc = bass.Bass("TRN2", target_bir_lowering=False)

    def din(name, shape):
        return nc.dram_tensor(name, list(shape), F32, kind="ExternalInput").ap()

    def dout(name, shape):
        return nc.dram_tensor(name, list(shape), F32, kind="ExternalOutput").ap()

    def dscr(name, shape):
        return nc.dram_tensor(name, list(shape), F32).ap()

    xp = din("xp", [n_seq, seq_len, D])
    xs = din("xs", [32, D])
    sg_in = din("sg_in", [depth, 4, 128, 256])
    sr_in = din("sr_in", [depth, 4, 128, 256])
    meta = din("meta", [N_META, D])
    ln_g = din("ln_g", [depth, 3, D])
    ln_b = din("ln_b", [depth, 3, D])
    w1u = din("w1u", [depth, D, 2 * DFF])
    w1d = din("w1d", [depth, DFF, D])
    w_in = din("w_in", [depth, D, NIN])
    w_a2 = din("w_a2", [depth, 16, 512])
    b_a = din("b_a", [depth, 512])
    bm_t = din("bm_t", [128, depth, 16])
    gng_t = din("gng_t", [128, depth, 8])
    gnr_t = din("gnr_t", [128, depth, 8])
    w_oa = din("w_oa", [depth, D, D])
    w_ob = din("w_ob", [depth, D, D])
    w_o = din("w_o", [depth, D, D])
    w2u = din("w2u", [depth, D, 2 * DFF])
    w2d = din("w2d", [depth, DFF, D])
    c_idf = din("c_idf", [128, 128])
    c_mask = din("c_mask", [128, 128])
    c_tri = din("c_tri", [128, 128])
    c_cos = din("c_cos", [N_META + 2048, 64])
    c_sin = din("c_sin", [N_META + 2048, 64])
    c_nsin = din("c_nsin", [N_META + 2048, 64])
    c_c1T = din("c_c1T", [4, 176])
    c_c2 = din("c_c2", [128, 3, 4])
    c_decr = din("c_decr", [3, 4])

    yp = dout("yp", [n_seq, seq_len, D])
    ys = dout("ys", [32, D])
    gp = dout("gp", [depth, n_seq, 4, 128, 256])
    rp = dout("rp", [depth, n_seq, 4, 128, 256])
    gs = dout("gs", [depth, 4, 128, 256])
    rs = dout("rs", [depth, 4, 128, 256])

    WSH = {"w1u": (D, 2 * DFF), "w1d": (DFF, D), "w_in": (D, NIN), "w_oa": (D, D), "w_ob": (D, D), "w_o": (D, D),
           "w2u": (D, 2 * DFF), "w2d": (DFF, D)}
    wfp = {"w1u": w1u, "w1d": w1d, "w_in": w_in, "w_oa": w_oa, "w_ob": w_ob, "w_o": w_o, "w2u": w2u, "w2d": w2d}
    wsc = {k: nc.dram_tensor(k + "_bf", [depth, r, c], BF16).ap() for k, (r, c) in WSH.items()}

    def wsrc(name, l):
        return wsc[name][l]

    smeta_g = dscr("smeta_g", [depth, 4, 128, 256])
    smeta_r = dscr("smeta_r", [depth, 4, 128, 256])
    scr_g = dscr("scr_g", [depth, 4, 128, 256])
    scr_r = dscr("scr_r", [depth, 4, 128, 256])

    with ExitStack() as st:
        T = Tracker(nc, st)
        pe, act, dve, pool, sp = T.pe, T.act, T.dve, T.pool, T.sp

        def sb(name, shape, dt=F32):
            return st.enter_context(nc.sbuf_tensor(name, list(shape), dt))

        idf = sb("idf", [128, 128])
        idb = sb("idb", [128, 128], BF16)
        mask = sb("mask", [128, 128])
        tri = sb("tri", [128, 128])
        neg16 = sb("neg16", [128, 1])
        mhalf = sb("mhalf", [128, 8])
        c1T = sb("c1T", [128, 4, 176])
        c2 = sb("c2", [128, 3, 4])
        decr = sb("decr", [128, 3, 4])
        ba_bc = sb("ba_bc", [128, 512])
        wa2 = sb("wa2", [16, 512], BF16)
        G = [sb(f"G{i}", [128, 512]) for i in range(5)]
        bm = sb("bm", [128, depth, 16])
        gng = sb("gng", [128, depth, 8])
        gnr = sb("gnr", [128, depth, 8])
        lng = sb("lng", [128, D])
        lnb = sb("lnb", [128, D])
        cosT = sb("cosT", [128, nsub, 64])
        sinT = sb("sinT", [128, nsub, 64])
        nsinT = sb("nsinT", [128, nsub, 64])
        h = sb("h", [128, nsub, D])
        hT = sb("hT", [128, 8, NT], BF16)
        ring = [sb(f"ring{i}", [128, 4096], BF16) for i in range(NSLOT)]
        gT = [sb(f"gT{i}", [128, 4, NT], BF16) for i in range(2)]
        satmp = [G[0], G[1]]
        agT = sb("agT", [16, NT], BF16)
        xb, ee, ltok, E1, E2 = G
        decg = sb("decg", [128, nsub, 4])
        qt = sb("qt", [128, nsub, 512], BF16)
        kt = sb("kt", [128, nsub, 512], BF16)
        vv = sb("vv", [128, nsub, 1024], BF16)
        sgate = sb("sgate", [128, nsub, 1024], BF16)
        zc, za, ztmp, rot = G[0:4]
        Sbf = sb("Sbf", [128, 4, 256], BF16)
        qTt = sb("qTt", [128, 4, 128], BF16)
        kTt = sb("kTt", [128, 4, 128], BF16)
        attb = sb("attb", [128, 4, 128], BF16)
        on = sb("on", [128, 4, 256])
        yin = sb("yin", [128, 1024], BF16)
        hst = sb("hst", [128, 4, 6])
        hmv = sb("hmv", [128, 4, 2])
        hve = sb("hve", [128, 4])
        hrs = sb("hrs", [128, 4])
        yinTa = sb("yinTa", [128, 8, NT], BF16)
        yinTb = sb("yinTb", [128, 8, NT], BF16)
        mT = sb("mT", [128, 8, NT], BF16)
        sgA, sgB, t1, t2 = G[0:4]
        Sg = sb("Sg", [128, 4, 256])
        Sr = sb("Sr", [128, 4, 256])
        lst = [sb(f"lst{i}", [128, 2, 6]) for i in range(2)]
        lmv = [sb(f"lmv{i}", [128, 2]) for i in range(2)]
        lve = [sb(f"lve{i}", [128, 1]) for i in range(2)]
        lrs = [sb(f"lrs{i}", [128, 1]) for i in range(2)]

        ps = st.enter_context(nc.psum_tensor("ps", [128, 8, 512], F32))
        psb = ps.bitcast(BF16)

        b_const = Buf("const")
        b_w = {k: [Buf(f"w_{k}{l}") for l in range(depth)] for k in WSH}
        b_lp = Buf("lp")
        b_ln = Buf("lnp")
        b_cs = Buf("cossin")
        b_h = [[Buf(f"h{s}{j}") for j in range(2)] for s in range(nsub)]
        b_hT = [Buf(f"hT{s}") for s in range(nsub)]
        b_ring = [Buf(f"ring{i}") for i in range(NSLOT)]
        b_gT = [[Buf(f"gT{i}{c}") for c in range(4)] for i in range(2)]
        b_G = [Buf(f"G{i}") for i in range(5)]
        b_sa = [b_G[0], b_G[1]]
        b_ps = [Buf(f"ps{i}", x=True) for i in range(8)]
        b_agT = Buf("agT")
        b_xb, b_ee, b_ltok, b_E1, b_E2 = b_G
        b_decg = [Buf(f"decg{s}") for s in range(nsub)]
        b_qt = [Buf(f"qt{s}") for s in range(nsub)]
        b_kt = [Buf(f"kt{s}") for s in range(nsub)]
        b_vv = [[Buf(f"vv{s}{j}") for j in range(2)] for s in range(nsub)]
        b_sg = [[Buf(f"sg{s}{j}") for j in range(2)] for s in range(nsub)]
        b_zc, b_za, b_ztmp, b_rot = b_G[0:4]
        b_Sbf, b_qTt, b_kTt, b_attb, b_on, b_yin = Buf("Sbf"), Buf("qTt"), Buf("kTt"), Buf("attb"), Buf("on"), Buf("yin")
        b_hst, b_hmv, b_hve, b_hrs = Buf("hst"), Buf("hmv"), Buf("hve"), Buf("hrs")
        b_yTa = [Buf(f"yTa{s}") for s in range(nsub)]
        b_yTb = [Buf(f"yTb{s}") for s in range(nsub)]
        b_mT = [Buf(f"mT{c}") for c in range(8)]
        b_sgA, b_sgB, b_t1, b_t2 = b_G[0:4]
        b_Sg, b_Sr = Buf("Sg"), Buf("Sr")
        b_lst = [Buf(f"lst{i}") for i in range(2)]
        b_lmv = [Buf(f"lmv{i}") for i in range(2)]
        b_lve = [Buf(f"lve{i}") for i in range(2)]
        b_lrs = [Buf(f"lrs{i}") for i in range(2)]
        b_smg = [Buf(f"smg{l}") for l in range(depth)]
        b_smr = [Buf(f"smr{l}") for l in range(depth)]
        b_scg = [Buf(f"scg{l}") for l in range(depth)]
        b_scr = [Buf(f"scr{l}") for l in range(depth)]

        s_ring = [T.new_sem(f"d_ring{i}") for i in range(NSLOT)]
        s_const = T.new_sem("d_const")
        s_cv = {k: [T.new_sem(f"d_cv_{k}{l}") for l in range(depth)] for k in WSH}
        s_lp = T.new_sem("d_lp")
        s_x = T.new_sem("d_x")
        s_y = T.new_sem("d_y")
        s_ln = T.new_sem("d_ln")
        s_cs = T.new_sem("d_cs")
        s_stl_g, s_stl_r = T.new_sem("d_stlg"), T.new_sem("d_stlr")
        s_sts_g, s_sts_r = T.new_sem("d_stsg"), T.new_sem("d_stsr")

        pstate = {"i": 0}

        def palloc():
            i = pstate["i"]
            pstate["i"] = (i + 1) % 8
            return i

        rstate = {"i": 0}

        def wload(src, a, b, dep):
            i = rstate["i"]
            rstate["i"] = (i + 1) % NSLOT
            view = ring[i][:, 0:a * b].rearrange("p (a b) -> p a b", a=a)
            T.dma(sp, s_ring[i], lambda e, view=view, src=src: e.dma_start(out=view, in_=src),
                  reads=[dep], writes=[b_ring[i]])
            return view, b_ring[i]

        def load_consts():
            def f(e):
                r = [
                    e.dma_start(out=idf[:], in_=c_idf[:, :]),
                    e.dma_start(out=mask[:], in_=c_mask[:, :]),
                    e.dma_start(out=tri[:], in_=c_tri[:, :]),
                    e.dma_start(out=c1T[:], in_=c_c1T.partition_broadcast(128)),
                    e.dma_start(out=c2[:], in_=c_c2[:, :, :]),
                    e.dma_start(out=decr[:], in_=c_decr.partition_broadcast(128)),
                    e.dma_start(out=bm[:], in_=bm_t[:, :, :]),
                    e.dma_start(out=gng[:], in_=gng_t[:, :, :]),
                    e.dma_start(out=gnr[:], in_=gnr_t[:, :, :]),
                ]
                return r
            T.dma(pool, s_const, f, writes=[b_const], n=9)
            T.op(dve, lambda e: e.tensor_copy(out=idb[:], in_=idf[:]), reads=[b_const], writes=[b_const])
            T.op(dve, lambda e: e.memset(neg16[:], -1.0 / 16.0), writes=[b_const])
            T.op(dve, lambda e: e.memset(mhalf[:], -0.5), writes=[b_const])

        def convert_weights():
            for l in range(depth):
                for name in ("w1u", "w1d", "w_in", "w_oa", "w_ob", "w_o", "w2u", "w2d"):
                    src, dst = wfp[name][l], wsc[name][l]
                    nchunk = WSH[name][0] // 128

                    def f(e, src=src, dst=dst, nchunk=nchunk):
                        return [e.dma_start(out=dst[c * 128:(c + 1) * 128, :], in_=src[c * 128:(c + 1) * 128, :])
                                for c in range(nchunk)]
                    T.dma(pool, s_cv[name][l], f, writes=[b_w[name][l]], n=nchunk)

        def rstd_pow(out_ap, in_ap, nrow, ncol, rb, wb):
            T.op(pool, lambda e: e.tensor_tensor(out=out_ap, in0=in_ap, in1=mhalf[0:nrow, 0:ncol], op=ALU.pow),
                 reads=[rb, b_const], writes=[wb])

        def transpose_to_hT(s, L):
            p0, p1 = palloc(), palloc()

            def f(e):
                r = None
                for k in range(8):
                    bank = p0 if k < 4 else p1
                    r = e.transpose(ps[:, bank, (k % 4) * 128:(k % 4) * 128 + L],
                                    h[0:L, s, k * 128:(k + 1) * 128], idf[0:L, 0:L])
                return r
            T.op(pe, f, reads=[b_h[s][0], b_h[s][1], b_const], writes=[b_ps[p0], b_ps[p1]])
            T.op(act, lambda e: e.copy(out=hT[:, 0:4, s * 128:s * 128 + L],
                                        in_=ps[:, p0, :].rearrange("p (k t) -> p k t", k=4)[:, :, 0:L]),
                 reads=[b_ps[p0]], writes=[b_hT[s]])
            T.op(dve, lambda e: e.tensor_copy(out=hT[:, 4:8, s * 128:s * 128 + L],
                                               in_=ps[:, p1, :].rearrange("p (k t) -> p k t", k=4)[:, :, 0:L]),
                 reads=[b_ps[p1]], writes=[b_hT[s]])

        def load_ln(l, idx):
            def f(e):
                return [e.dma_start(out=lng[:], in_=ln_g[l, idx].partition_broadcast(128)),
                        e.dma_start(out=lnb[:], in_=ln_b[l, idx].partition_broadcast(128))]
            T.dma(pool, s_ln, f, writes=[b_ln], n=2)

        def layer_norm(s, L):
            i = s % 2
            hs = h[0:L, s, :]
            T.op(dve, lambda e: e.bn_stats(out=lst[i][0:L, 0, :], in_=h[0:L, s, 0:512]), reads=[b_h[s][0]], writes=[b_lst[i]])
            T.op(dve, lambda e: e.bn_stats(out=lst[i][0:L, 1, :], in_=h[0:L, s, 512:1024]), reads=[b_h[s][1]], writes=[b_lst[i]])
            T.op(dve, lambda e: e.bn_aggr(out=lmv[i][0:L, :], in_=lst[i][0:L, :, :]), reads=[b_lst[i]], writes=[b_lmv[i]])
            T.op(dve, lambda e: e.tensor_scalar_add(out=lve[i][0:L, :], in0=lmv[i][0:L, 1:2], scalar1=LN_EPS),
                 reads=[b_lmv[i]], writes=[b_lve[i]])
            rstd_pow(lrs[i][0:L, :], lve[i][0:L, :], L, 1, b_lve[i], b_lrs[i])
            T.op(dve, lambda e: e.tensor_scalar(out=hs, in0=hs, scalar1=lmv[i][0:L, 0:1], scalar2=lrs[i][0:L, :],
                                                 op0=ALU.subtract, op1=ALU.mult),
                 reads=[b_h[s][0], b_h[s][1], b_lmv[i], b_lrs[i]], writes=[b_h[s][0], b_h[s][1]])
            T.op(dve, lambda e: e.tensor_tensor(out=hs, in0=hs, in1=lng[0:L, :], op=ALU.mult),
                 reads=[b_h[s][0], b_h[s][1], b_ln], writes=[b_h[s][0], b_h[s][1]])
            T.op(dve, lambda e: e.tensor_tensor(out=hs, in0=hs, in1=lnb[0:L, :], op=ALU.add),
                 reads=[b_h[s][0], b_h[s][1], b_ln], writes=[b_h[s][0], b_h[s][1]])
            transpose_to_hT(s, L)

        def residual_add(s, L, hf, pbank, first):
            hh = h[0:L, s, hf * 512:(hf + 1) * 512]
            if first:
                T.op(dve, lambda e: e.scalar_tensor_tensor(out=hh, in0=hh, scalar=DN_ALPHA, in1=ps[0:L, pbank, :],
                                                            op0=ALU.mult, op1=ALU.add),
                     reads=[b_h[s][hf], b_ps[pbank]], writes=[b_h[s][hf]])
            else:
                T.op(dve, lambda e: e.tensor_tensor(out=hh, in0=hh, in1=ps[0:L, pbank, :], op=ALU.add),
                     reads=[b_h[s][hf], b_ps[pbank]], writes=[b_h[s][hf]])

        def ffn(tile, nu, nd, l):
            ns, L = tile["ns"], tile["L"]
            ntok = tile["ntok"]
            wu_v = wsrc(nu, l).rearrange("(k p) n -> p k n", p=128)
            wd_v = wsrc(nd, l).rearrange("(c p) n -> p c n", p=128)
            du, dd = b_w[nu][l], b_w[nd][l]
            groups = [(0, 512), (512, 512), (1024, 512), (1536, 512), (2048, 512), (2560, 256)]
            for gi, (f0, fw) in enumerate(groups):
                nch = fw // 128
                wa_v, wa_b = wload(wu_v[:, :, f0:f0 + fw], 8, fw, du)
                wb_v, wb_b = wload(wu_v[:, :, DFF + f0:DFF + f0 + fw], 8, fw, du)
                wd_s, wd_b = wload(wd_v[:, f0 // 128:f0 // 128 + nch, :], nch, 1024, dd)
                gb = gi % 2
                for c in range(nch):
                    pa, pb = palloc(), palloc()

                    def fup(e, w_v=wa_v, bank=pa, c=c):
                        r = None
                        for k in range(8):
                            r = e.matmul(ps[:, bank, 0:ntok], lhsT=w_v[:, k, c * 128:(c + 1) * 128],
                                         rhs=hT[:, k, 0:ntok], start=(k == 0), stop=(k == 7))
                        return r
                    T.op(pe, fup, reads=[wa_b] + b_hT[0:ns], writes=[b_ps[pa]])

                    def fupb(e, w_v=wb_v, bank=pb, c=c):
                        r = None
                        for k in range(8):
                            r = e.matmul(ps[:, bank, 0:ntok], lhsT=w_v[:, k, c * 128:(c + 1) * 128],
                                         rhs=hT[:, k, 0:ntok], start=(k == 0), stop=(k == 7))
                        return r
                    T.op(pe, fupb, reads=[wb_b] + b_hT[0:ns], writes=[b_ps[pb]])
                    si = c % 2
                    T.op(act, lambda e, pa=pa, si=si: e.activation(out=satmp[si][:, 0:ntok], in_=ps[:, pa, 0:ntok], func=AF.Silu),
                         reads=[b_ps[pa]], writes=[b_sa[si]])
                    T.op(dve, lambda e, pb=pb, si=si, c=c, gb=gb: e.scalar_tensor_tensor(
                        out=gT[gb][:, c, 0:ntok], in0=satmp[si][:, 0:ntok], scalar=0.5, in1=ps[:, pb, 0:ntok],
                        op0=ALU.mult, op1=ALU.mult),
                        reads=[b_sa[si], b_ps[pb]], writes=[b_gT[gb][c]])
                for s in range(ns):
                    for hf in range(2):
                        py = palloc()

                        def fdn(e, s=s, hf=hf, py=py, nch=nch, gb=gb, wd_s=wd_s):
                            r = None
                            for c in range(nch):
                                r = e.matmul(ps[0:L, py, :], lhsT=gT[gb][:, c, s * 128:s * 128 + L],
                                             rhs=wd_s[:, c, hf * 512:(hf + 1) * 512], start=(c == 0), stop=(c == nch - 1))
                            return r
                        T.op(pe, fdn, reads=[wd_b] + b_gT[gb][0:nch], writes=[b_ps[py]])
                        residual_add(s, L, hf, py, first=(gi == 0))

        def mixer_core(s, L, dec_ap, dec_buf, S, b_S, c1T_ap, gn_ap, yT, b_yT):
            T.op(dve, lambda e: e.tensor_tensor(out=Sbf[:], in0=S[:], in1=dec_ap.rearrange("p (h o) -> p h o", o=1).to_broadcast([128, 4, 256]), op=ALU.mult),
                 reads=[b_S, dec_buf], writes=[b_Sbf])
            pt = palloc()

            def ftr(e):
                r = None
                for hh in range(4):
                    r = e.transpose(psb[:, pt, hh * 128:hh * 128 + L], qt[0:L, s, hh * 128:(hh + 1) * 128], idb[0:L, 0:L])
                for hh in range(4):
                    r = e.transpose(psb[:, pt, 512 + hh * 128:512 + hh * 128 + L], kt[0:L, s, hh * 128:(hh + 1) * 128], idb[0:L, 0:L])
                return r
            T.op(pe, ftr, reads=[b_qt[s], b_kt[s], b_const], writes=[b_ps[pt]])
            qsrc = psb[:, pt, 0:512].rearrange("p (h t) -> p h t", h=4)[:, :, 0:L]
            ksrc = psb[:, pt, 512:1024].rearrange("p (h t) -> p h t", h=4)[:, :, 0:L]
            if c1T_ap is None:
                T.op(act, lambda e: e.copy(out=qTt[:, :, 0:L], in_=qsrc), reads=[b_ps[pt]], writes=[b_qTt])
            else:
                T.op(dve, lambda e: e.tensor_tensor(out=qTt[:, :, 0:L], in0=qsrc, in1=c1T_ap, op=ALU.mult),
                     reads=[b_ps[pt], b_const], writes=[b_qTt])
            T.op(dve, lambda e: e.tensor_copy(out=kTt[:, :, 0:L], in_=ksrc), reads=[b_ps[pt]], writes=[b_kTt])
            if CORE_STOP < 2:
                return
            pa = palloc()

            def fatt(e):
                r = None
                for hh in range(4):
                    r = e.matmul(ps[0:L, pa, hh * 128:hh * 128 + L], lhsT=kTt[:, hh, 0:L], rhs=qTt[:, hh, 0:L],
                                 start=True, stop=True)
                return r
            T.op(pe, fatt, reads=[b_qTt, b_kTt], writes=[b_ps[pa]])
            T.op(dve, lambda e: e.tensor_tensor(
                out=attb[0:L, :, 0:L], in0=ps[0:L, pa, :].rearrange("p (h t) -> p h t", h=4)[:, :, 0:L],
                in1=mask[0:L, 0:L].rearrange("p (o t) -> p o t", o=1).to_broadcast([L, 4, L]), op=ALU.mult),
                reads=[b_ps[pa], b_const], writes=[b_attb])
            if CORE_STOP < 3:
                return
            po0, po1 = palloc(), palloc()

            def fo(e):
                r = None
                for hh in range(4):
                    bank = po0 if hh < 2 else po1
                    oo = ps[0:L, bank, (hh % 2) * 256:(hh % 2 + 1) * 256]
                    e.matmul(oo, lhsT=attb[0:L, hh, 0:L], rhs=vv[0:L, s, hh * 256:(hh + 1) * 256], start=True, stop=False)
                    r = e.matmul(oo, lhsT=qTt[:, hh, 0:L], rhs=Sbf[:, hh, :], start=False, stop=True)
                return r
            T.op(pe, fo, reads=[b_attb, b_vv[s][0], b_vv[s][1], b_qTt, b_Sbf], writes=[b_ps[po0], b_ps[po1]])
            if CORE_STOP < 4:
                return
            pk0, pk1 = palloc(), palloc()

            def fkv(e):
                r = None
                for hh in range(4):
                    bank = pk0 if hh < 2 else pk1
                    r = e.matmul(ps[:, bank, (hh % 2) * 256:(hh % 2 + 1) * 256], lhsT=kt[0:L, s, hh * 128:(hh + 1) * 128],
                                 rhs=vv[0:L, s, hh * 256:(hh + 1) * 256], start=True, stop=True)
                return r
            T.op(pe, fkv, reads=[b_kt[s], b_vv[s][0], b_vv[s][1]], writes=[b_ps[pk0], b_ps[pk1]])
            for hh in range(4):
                bank = pk0 if hh < 2 else pk1
                T.op(dve, lambda e, hh=hh, bank=bank: e.scalar_tensor_tensor(
                    out=S[:, hh, :], in0=S[:, hh, :], scalar=dec_ap[:, hh:hh + 1],
                    in1=ps[:, bank, (hh % 2) * 256:(hh % 2 + 1) * 256], op0=ALU.mult, op1=ALU.add),
                    reads=[b_S, dec_buf, b_ps[bank], b_Sbf], writes=[b_S])
            if CORE_STOP < 5:
                return
            for hh in range(4):
                bank = po0 if hh < 2 else po1
                T.op(dve, lambda e, hh=hh, bank=bank: e.bn_stats(out=hst[0:L, hh, :], in_=ps[0:L, bank, (hh % 2) * 256:(hh % 2 + 1) * 256]),
                     reads=[b_ps[bank]], writes=[b_hst])
            for hh in range(4):
                T.op(dve, lambda e, hh=hh: e.bn_aggr(out=hmv[0:L, hh, :], in_=hst[0:L, hh:hh + 1, :]),
                     reads=[b_hst], writes=[b_hmv])
            T.op(dve, lambda e: e.tensor_scalar_add(out=hve[0:L, :], in0=hmv[0:L, :, 1], scalar1=GN_EPS),
                 reads=[b_hmv], writes=[b_hve])
            rstd_pow(hrs[0:L, :], hve[0:L, :], L, 4, b_hve, b_hrs)
            for hh in range(4):
                bank = po0 if hh < 2 else po1
                T.op(dve, lambda e, hh=hh, bank=bank: e.tensor_scalar(
                    out=on[0:L, hh, :], in0=ps[0:L, bank, (hh % 2) * 256:(hh % 2 + 1) * 256],
                    scalar1=hmv[0:L, hh, 0:1], scalar2=hrs[0:L, hh:hh + 1], op0=ALU.subtract, op1=ALU.mult),
                    reads=[b_ps[bank], b_hmv, b_hrs], writes=[b_on])
            if CORE_STOP < 6:
                return
            T.op(dve, lambda e: e.tensor_tensor(out=yin[0:L, :], in0=on[0:L, :, :].rearrange("p h e -> p (h e)"),
                                                 in1=sgate[0:L, s, :], op=ALU.mult),
                 reads=[b_on, b_sg[s][0], b_sg[s][1]], writes=[b_yin])
            py = palloc()

            def fty(e):
                r = None
                for c in range(8):
                    r = e.transpose(psb[:, py, c * 128:c * 128 + L], yin[0:L, c * 128:(c + 1) * 128], idb[0:L, 0:L])
                return r
            T.op(pe, fty, reads=[b_yin, b_const], writes=[b_ps[py]])
            T.op(dve, lambda e: e.tensor_tensor(
                out=yT[:, :, s * 128:s * 128 + L], in0=psb[:, py, :].rearrange("p (c t) -> p c t", c=8)[:, :, 0:L],
                in1=gn_ap.rearrange("p (c o) -> p c o", o=1).to_broadcast([128, 8, L]), op=ALU.mult),
                reads=[b_ps[py], b_const], writes=[b_yT[s]])

        def proj_tok(s, L, w_v, w_b):
            pz = palloc()

            def f(e):
                r = None
                for k in range(8):
                    r = e.matmul(ps[0:L, pz, :], lhsT=hT[:, k, s * 128:s * 128 + L], rhs=w_v[:, k, :],
                                 start=(k == 0), stop=(k == 7))
                return r
            T.op(pe, f, reads=[w_b, b_hT[s]], writes=[b_ps[pz]])
            return pz

        def rotary(s, L, pz, out_ap, out_bufs):
            T.op(act, lambda e: e.copy(out=zc[0:L, :], in_=ps[0:L, pz, :]), reads=[b_ps[pz]], writes=[b_zc])
            z4 = zc[0:L, :].rearrange("p (h w j) -> p h w j", h=4, w=2)
            a4 = za[0:L, :].rearrange("p (h w j) -> p h w j", h=4, w=2)
            t4 = ztmp[0:L, :].rearrange("p (h w j) -> p h w j", h=4, w=2)
            cos_b = cosT[0:L, s, :].rearrange("p (o j) -> p o j", o=1).to_broadcast([L, 8, 64])
            sin_b = sinT[0:L, s, :].rearrange("p (o j) -> p o j", o=1).to_broadcast([L, 4, 64])
            nsin_b = nsinT[0:L, s, :].rearrange("p (o j) -> p o j", o=1).to_broadcast([L, 4, 64])
            T.op(dve, lambda e: e.tensor_tensor(out=za[0:L, :].rearrange("p (g j) -> p g j", g=8),
                                                 in0=zc[0:L, :].rearrange("p (g j) -> p g j", g=8), in1=cos_b, op=ALU.mult),
                 reads=[b_zc, b_cs], writes=[b_za])
            T.op(dve, lambda e: e.tensor_tensor(out=t4[:, :, 0, :], in0=z4[:, :, 1, :], in1=nsin_b, op=ALU.mult),
                 reads=[b_zc, b_cs], writes=[b_ztmp])
            T.op(dve, lambda e: e.tensor_tensor(out=t4[:, :, 1, :], in0=z4[:, :, 0, :], in1=sin_b, op=ALU.mult),
                 reads=[b_zc, b_cs], writes=[b_ztmp])
            T.op(dve, lambda e: e.tensor_tensor(out=out_ap, in0=za[0:L, :], in1=ztmp[0:L, :], op=ALU.add),
                 reads=[b_za, b_ztmp], writes=out_bufs)

        def mixers(tile, l):
            ns, L, ntok = tile["ns"], tile["L"], tile["ntok"]
            li = LIDX[L]
            W = wsrc("w_in", l).rearrange("(k p) n -> p k n", p=128)
            wdeps = [b_w["w_in"][l]]
            T.dma(pool, s_lp, lambda e: [e.dma_start(out=ba_bc[:], in_=b_a[l].partition_broadcast(128)),
                                         e.dma_start(out=wa2[:], in_=w_a2[l])], writes=[b_lp], n=2)
            wag_v, wag_b = wload(W[:, :, 3072:3088], 8, 16, wdeps[0])
            pg = palloc()

            def fag(e):
                r = None
                for k in range(8):
                    r = e.matmul(ps[0:16, pg, 0:ntok], lhsT=wag_v[:, k, :], rhs=hT[:, k, 0:ntok], start=(k == 0), stop=(k == 7))
                return r
            T.op(pe, fag, reads=[wag_b] + b_hT[0:ns], writes=[b_ps[pg]])
            T.op(act, lambda e: e.copy(out=agT[:, 0:ntok], in_=ps[0:16, pg, 0:ntok]), reads=[b_ps[pg]], writes=[b_agT])
            if MIX_STOP < 2:
                return
            wq_v, wq_b = wload(W[:, :, 0:512], 8, 512, wdeps[0])
            wk_v, wk_b = wload(W[:, :, 512:1024], 8, 512, wdeps[0])
            for s in range(ns):
                px = palloc()
                T.op(pe, lambda e, s=s, px=px: e.matmul(ps[0:L, px, :], lhsT=agT[:, s * 128:s * 128 + L], rhs=wa2[:, :],
                                                         start=True, stop=True),
                     reads=[b_agT, b_lp], writes=[b_ps[px]])
                T.op(dve, lambda e, px=px: e.tensor_tensor(out=xb[0:L, :], in0=ps[0:L, px, :], in1=ba_bc[0:L, :], op=ALU.add),
                     reads=[b_ps[px], b_lp], writes=[b_xb])
                T.op(act, lambda e: e.activation(out=ee[0:L, :], in_=xb[0:L, :], func=AF.Exp, scale=-1.0),
                     reads=[b_xb], writes=[b_ee])
                T.op(act, lambda e: e.activation(out=ltok[0:L, :], in_=ee[0:L, :], func=AF.Ln, bias=1.0),
                     reads=[b_ee], writes=[b_ltok])
                pd, pbl = palloc(), palloc()
                T.op(pe, lambda e, pd=pd: e.matmul(ps[0:L, pd, :], lhsT=tri[0:L, 0:L], rhs=ltok[0:L, :], start=True, stop=True),
                     reads=[b_ltok, b_const], writes=[b_ps[pd]])

                def fbl(e, pbl=pbl):
                    r = None
                    for hh in range(4):
                        r = e.matmul(ps[:, pbl, hh:hh + 1], lhsT=ltok[0:L, hh * 128:(hh + 1) * 128], rhs=neg16[0:L, 0:1],
                                     start=True, stop=True)
                    return r
                T.op(pe, fbl, reads=[b_ltok, b_const], writes=[b_ps[pbl]])
                T.op(act, lambda e, pd=pd: e.activation(out=E1[0:L, :], in_=ps[0:L, pd, :], func=AF.Exp,
                                                         bias=float(math.log(128.0 ** -0.5)), scale=1.0),
                     reads=[b_ps[pd]], writes=[b_E1])
                T.op(act, lambda e, pd=pd: e.activation(out=E2[0:L, :], in_=ps[0:L, pd, :], func=AF.Exp, scale=-1.0),
                     reads=[b_ps[pd]], writes=[b_E2])
                T.op(act, lambda e, pbl=pbl, s=s: e.activation(out=decg[:, s, :], in_=ps[:, pbl, 0:4], func=AF.Exp),
                     reads=[b_ps[pbl]], writes=[b_decg[s]])
                pq = proj_tok(s, L, wq_v, wq_b)
                T.op(dve, lambda e, pq=pq, s=s: e.tensor_tensor(out=qt[0:L, s, :], in0=ps[0:L, pq, :], in1=E1[0:L, :], op=ALU.mult),
                     reads=[b_ps[pq], b_E1], writes=[b_qt[s]])
                pk = proj_tok(s, L, wk_v, wk_b)
                T.op(dve, lambda e, pk=pk, s=s: e.tensor_tensor(out=kt[0:L, s, :], in0=ps[0:L, pk, :], in1=E2[0:L, :], op=ALU.mult),
                     reads=[b_ps[pk], b_E2], writes=[b_kt[s]])

            def vg_pieces(c0v, c0g):
                for j in range(2):
                    w_v, w_b = wload(W[:, :, c0v + 512 * j:c0v + 512 * (j + 1)], 8, 512, wdeps[0])
                    for s in range(ns):
                        pz = proj_tok(s, L, w_v, w_b)
                        T.op(act, lambda e, pz=pz, s=s, j=j: e.copy(out=vv[0:L, s, 512 * j:512 * (j + 1)], in_=ps[0:L, pz, :]),
                             reads=[b_ps[pz]], writes=[b_vv[s][j]])
                for j in range(2):
                    w_v, w_b = wload(W[:, :, c0g + 512 * j:c0g + 512 * (j + 1)], 8, 512, wdeps[0])
                    for s in range(ns):
                        pz = proj_tok(s, L, w_v, w_b)
                        T.op(act, lambda e, pz=pz, s=s, j=j: e.activation(out=sgate[0:L, s, 512 * j:512 * (j + 1)], in_=ps[0:L, pz, :],
                                                                            func=AF.Silu),
                             reads=[b_ps[pz]], writes=[b_sg[s][j]])

            if MIX_STOP < 3:
                return
            vg_pieces(1024, 2048)
            if MIX_STOP < 4:
                return
            tile["state_load"](l, "g")
            for s in range(ns):
                mixer_core(s, L, decg[:, s, :], b_decg[s], Sg, b_Sg, None, gng[:, l, :], yinTa, b_yTa)
            tile["state_store"](l, "g")
            if MIX_STOP < 5:
                return
            wq_v, wq_b = wload(W[:, :, 3088:3600], 8, 512, wdeps[0])
            wk_v, wk_b = wload(W[:, :, 3600:4112], 8, 512, wdeps[0])
            for s in range(ns):
                pq = proj_tok(s, L, wq_v, wq_b)
                rotary(s, L, pq, qt[0:L, s, :], [b_qt[s]])
                pk = proj_tok(s, L, wk_v, wk_b)
                rotary(s, L, pk, rot[0:L, :], [b_rot])
                T.op(dve, lambda e, s=s: e.tensor_tensor(
                    out=kt[0:L, s, :].rearrange("p (h d) -> p h d", h=4), in0=rot[0:L, :].rearrange("p (h d) -> p h d", h=4),
                    in1=c2[0:L, li, :].rearrange("p (h o) -> p h o", o=1).to_broadcast([L, 4, 128]), op=ALU.mult),
                    reads=[b_rot, b_const], writes=[b_kt[s]])
            if MIX_STOP < 6:
                return
            vg_pieces(4112, 5136)
            tile["state_load"](l, "r")
            for s in range(ns):
                mixer_core(s, L, decr[:, li, :], b_const, Sr, b_Sr, c1T[:, :, LOFF[L]:LOFF[L] + L], gnr[:, l, :], yinTb, b_yTb)
            tile["state_store"](l, "r")
            if MIX_STOP < 7:
                return
            woa = wsrc("w_oa", l).rearrange("(k p) n -> p k n", p=128)
            wob = wsrc("w_ob", l).rearrange("(k p) n -> p k n", p=128)
            for cg in range(2):
                a_v, a_b = wload(woa[:, :, 512 * cg:512 * (cg + 1)], 8, 512, b_w["w_oa"][l])
                b_v, b_b = wload(wob[:, :, 512 * cg:512 * (cg + 1)], 8, 512, b_w["w_ob"][l])
                ma_v, ma_b = wload(W[:, :, 6160 + 512 * cg:6160 + 512 * (cg + 1)], 8, 512, wdeps[0])
                mb_v, mb_b = wload(W[:, :, 7184 + 512 * cg:7184 + 512 * (cg + 1)], 8, 512, wdeps[0])
                for c4 in range(4):
                    c = 4 * cg + c4
                    banks = []
                    for (w_v, w_b, src, srcb) in ((a_v, a_b, yinTa, b_yTa), (b_v, b_b, yinTb, b_yTb),
                                                  (ma_v, ma_b, hT, b_hT), (mb_v, mb_b, hT, b_hT)):
                        pz = palloc()

                        def f(e, w_v=w_v, src=src, pz=pz, c4=c4):
                            r = None
                            for k in range(8):
                                r = e.matmul(ps[:, pz, 0:ntok], lhsT=w_v[:, k, c4 * 128:(c4 + 1) * 128], rhs=src[:, k, 0:ntok],
                                             start=(k == 0), stop=(k == 7))
                            return r
                        T.op(pe, f, reads=[w_b] + srcb[0:ns], writes=[b_ps[pz]])
                        banks.append(pz)
                    pya, pyb, pma, pmb = banks
                    T.op(act, lambda e, pma=pma, c=c: e.activation(out=sgA[:, 0:ntok], in_=ps[:, pma, 0:ntok], func=AF.Sigmoid,
                                                                    bias=bm[:, l, c:c + 1], scale=1.0),
                         reads=[b_ps[pma], b_const], writes=[b_sgA])
                    T.op(act, lambda e, pmb=pmb, c=c: e.activation(out=sgB[:, 0:ntok], in_=ps[:, pmb, 0:ntok], func=AF.Sigmoid,
                                                                    bias=bm[:, l, 8 + c:9 + c], scale=1.0),
                         reads=[b_ps[pmb], b_const], writes=[b_sgB])
                    T.op(dve, lambda e, pya=pya: e.tensor_tensor(out=t1[:, 0:ntok], in0=sgA[:, 0:ntok], in1=ps[:, pya, 0:ntok], op=ALU.mult),
                         reads=[b_sgA, b_ps[pya]], writes=[b_t1])
                    T.op(dve, lambda e, pyb=pyb: e.tensor_tensor(out=t2[:, 0:ntok], in0=sgB[:, 0:ntok], in1=ps[:, pyb, 0:ntok], op=ALU.mult),
                         reads=[b_sgB, b_ps[pyb]], writes=[b_t2])
                    T.op(dve, lambda e, c=c: e.tensor_tensor(out=mT[:, c, 0:ntok], in0=t1[:, 0:ntok], in1=t2[:, 0:ntok], op=ALU.add),
                         reads=[b_t1, b_t2], writes=[b_mT[c]])
            if MIX_STOP < 8:
                return
            wo = wsrc("w_o", l).rearrange("(k p) n -> p k n", p=128)
            wo_p = [wload(wo[:, :, 512 * hf:512 * (hf + 1)], 8, 512, b_w["w_o"][l]) for hf in range(2)]
            for s in range(ns):
                for hf in range(2):
                    po = palloc()

                    def f(e, s=s, hf=hf, po=po):
                        r = None
                        for k in range(8):
                            r = e.matmul(ps[0:L, po, :], lhsT=mT[:, k, s * 128:s * 128 + L], rhs=wo_p[hf][0][:, k, :],
                                         start=(k == 0), stop=(k == 7))
                        return r
                    T.op(pe, f, reads=[wo_p[hf][1]] + b_mT, writes=[b_ps[po]])
                    residual_add(s, L, hf, po, first=True)

        def run_tile(tile):
            ns, L = tile["ns"], tile["L"]
            T.dma(pool, s_x, lambda e: e.dma_start(out=h[0:L, 0:ns, :], in_=tile["x"].rearrange("(s p) d -> p s d", p=L)),
                  writes=[b for s in range(ns) for b in b_h[s]])
            pos0 = tile["pos0"]

            def fcs(e):
                return [e.dma_start(out=dst[0:L, 0:ns, :], in_=src[pos0:pos0 + ns * L, :].rearrange("(s p) j -> p s j", p=L))
                        for dst, src in ((cosT, c_cos), (sinT, c_sin), (nsinT, c_nsin))]
            T.dma(pool, s_cs, fcs, writes=[b_cs], n=3)
            if DEBUG_STOP >= 1:
                for s in range(ns):
                    transpose_to_hT(s, L)
            for l in range(depth):
                if DEBUG_STOP < 2:
                    break
                load_ln(l, 0)
                ffn(tile, "w1u", "w1d", l)
                if DEBUG_STOP < 3:
                    break
                for s in range(ns):
                    layer_norm(s, L)
                if DEBUG_STOP < 4:
                    break
                load_ln(l, 1)
                mixers(tile, l)
                if DEBUG_STOP < 5:
                    break
                for s in range(ns):
                    layer_norm(s, L)
                load_ln(l, 2)
                ffn(tile, "w2u", "w2d", l)
                for s in range(ns):
                    layer_norm(s, L)
            if tile["y"] is not None:
                T.dma(pool, s_y, lambda e: e.dma_start(out=tile["y"].rearrange("(s p) d -> p s d", p=L), in_=h[0:L, 0:ns, :]),
                      reads=[b for s in range(ns) for b in b_h[s]], is_out=True)

        def make_state_fns(src_g, src_r, dst_g, dst_r, out_g, out_r):
            def load(l, which):
                S, b_S, sem = (Sg, b_Sg, s_stl_g) if which == "g" else (Sr, b_Sr, s_stl_r)
                src = (src_g if which == "g" else src_r)
                if src is None:
                    T.op(dve, lambda e: e.memset(S[:].rearrange("p h e -> p (h e)"), 0.0), writes=[b_S])
                    return
                ap, bb = src(l)
                T.dma(pool, sem, lambda e: e.dma_start(out=S[:], in_=ap.rearrange("h d e -> d h e")),
                      reads=([bb] if bb is not None else []), writes=[b_S])

            def store(l, which):
                if NOSTORE:
                    return
                S, b_S, sem = (Sg, b_Sg, s_sts_g) if which == "g" else (Sr, b_Sr, s_sts_r)
                ap, bb = (dst_g if which == "g" else dst_r)(l)
                is_out = out_g if which == "g" else out_r
                T.dma(pool, sem, lambda e: e.dma_start(out=ap.rearrange("h d e -> d h e"), in_=S[:]),
                      reads=[b_S], writes=([bb] if bb is not None else []), is_out=is_out)
            return load, store

        load_consts()
        convert_weights()
        if with_meta:
            ld, stf = make_state_fns(None, None, lambda l: (smeta_g[l], b_smg[l]), lambda l: (smeta_r[l], b_smr[l]), False, False)
            run_tile(dict(ns=1, L=16, ntok=16, x=meta, y=None, pos0=0, state_load=ld, state_store=stf))
        if with_sample:
            ld, stf = make_state_fns(lambda l: (sg_in[l], None), lambda l: (sr_in[l], None),
                                     lambda l: (gs[l], None), lambda l: (rs[l], None), True, True)
            run_tile(dict(ns=1, L=32, ntok=32, x=xs, y=ys, pos0=N_META + PAST_LEN, state_load=ld, state_store=stf))
        for q in range(n_seq):
            for k in range(tiles_per_seq):
                first, last = (k == 0), (k == tiles_per_seq - 1)
                if first:
                    sgf = (lambda l: (smeta_g[l], b_smg[l])) if with_meta else None
                    srf = (lambda l: (smeta_r[l], b_smr[l])) if with_meta else None
                else:
                    sgf = lambda l: (scr_g[l], b_scg[l])
                    srf = lambda l: (scr_r[l], b_scr[l])
                if last:
                    dgf = lambda l, q=q: (gp[l, q], None)
                    drf = lambda l, q=q: (rp[l, q], None)
                else:
                    dgf = lambda l: (scr_g[l], b_scg[l])
                    drf = lambda l: (scr_r[l], b_scr[l])
                ld, stf = make_state_fns(sgf, srf, dgf, drf, last, last)
                run_tile(dict(ns=nsub, L=128, ntok=NT, x=xp[q, k * NT:(k + 1) * NT, :], y=yp[q, k * NT:(k + 1) * NT, :],
                              pos0=N_META + k * NT, state_load=ld, state_store=stf))
        T.finish()
        T.replay()
    return nc


def _const_inputs():
    idf = np.eye(128, dtype=np.float32)
    s_idx = np.arange(128)[:, None]
    t_idx = np.arange(128)[None, :]
    mask = (t_idx >= s_idx).astype(np.float32)
    tri = (s_idx > t_idx).astype(np.float32) / np.float32(16.0)
    half = 64
    inv = (np.float32(10000.0) ** (-np.arange(half, dtype=np.float32) / np.float32(half))).astype(np.float32)
    pos = np.arange(N_META + 2048, dtype=np.float32)
    ang = (pos[:, None] * inv[None, :]).astype(np.float32)
    cos = np.cos(ang).astype(np.float32)
    sin = np.sin(ang).astype(np.float32)
    log_gamma = np.log1p(-(2.0 ** (-5.0 - np.arange(4, dtype=np.float64))))
    c1T = np.zeros((4, 176), np.float32)
    c2 = np.zeros((128, 3, 4), np.float32)
    decr = np.zeros((3, 4), np.float32)
    for L in LVARS:
        t = np.arange(L, dtype=np.float64)
        c1T[:, LOFF[L]:LOFF[L] + L] = np.exp((t[None, :] - (L - 1.0)) * log_gamma[:, None])
        c2[:L, LIDX[L], :] = np.exp((L - 1.0 - t)[:, None] * log_gamma[None, :]) * (128.0 ** -0.5)
        decr[LIDX[L], :] = np.exp(L * log_gamma)
    return dict(c_idf=idf, c_mask=mask, c_tri=tri, c_cos=cos, c_sin=sin, c_nsin=(-sin).astype(np.float32),
                c_c1T=c1T, c_c2=c2, c_decr=decr)


def run_cores(inputs, n_cores, n_seq, seq_len, depth, nsub, with_meta=True, with_sample=True, trace=False):
    f = lambda a: np.ascontiguousarray(np.asarray(a, dtype=np.float32))
    nc = build_program(n_seq, seq_len, depth, nsub, with_meta, with_sample)
    consts = _const_inputs()
    shared = dict(
        meta=f(inputs["meta"]), ln_g=f(inputs["ln_g"][:depth]), ln_b=f(inputs["ln_b"][:depth]),
        w1u=f(inputs["w_ffn1_up"][:depth]), w1d=f(inputs["w_ffn1_down"][:depth]), w_in=f(inputs["w_in"][:depth]),
        w_a2=f(inputs["w_alpha2"][:depth]), b_a=f(inputs["b_alpha"][:depth]),
        bm_t=f(np.asarray(inputs["b_merge"][:depth]).reshape(depth, 16, 128).transpose(2, 0, 1)),
        gng_t=f(np.asarray(inputs["gn_gla"][:depth]).reshape(depth, 8, 128).transpose(2, 0, 1)),
        gnr_t=f(np.asarray(inputs["gn_ret"][:depth]).reshape(depth, 8, 128).transpose(2, 0, 1)),
        w_oa=f(inputs["w_o_gla"][:depth]), w_ob=f(inputs["w_o_ret"][:depth]), w_o=f(inputs["w_out"][:depth]),
        w2u=f(inputs["w_ffn2_up"][:depth]), w2d=f(inputs["w_ffn2_down"][:depth]),
    )
    shared.update(consts)
    xpr = np.asarray(inputs["x_prompt"])
    xsm = np.asarray(inputs["x_sample"])
    sgl = np.asarray(inputs["state_gla"])
    srt = np.asarray(inputs["state_ret"])
    in_maps = []
    for c in range(n_cores):
        m = dict(shared)
        m["xp"] = f(xpr[c * n_seq:(c + 1) * n_seq, :seq_len])
        m["xs"] = f(xsm[c])
        m["sg_in"] = f(sgl[:depth, c])
        m["sr_in"] = f(srt[:depth, c])
        in_maps.append(m)
    res = run_bass_kernel_spmd(nc, in_maps, core_ids=list(range(n_cores)), trace=trace)
    R = res.results
    y_prompt = np.concatenate([r["yp"] for r in R], axis=0)
    y_sample = np.stack([r["ys"] for r in R], axis=0)
    gla_p = np.concatenate([r["gp"] for r in R], axis=1)
    ret_p = np.concatenate([r["rp"] for r in R], axis=1)
    gla_s = np.stack([r["gs"] for r in R], axis=1)
    ret_s = np.stack([r["rs"] for r in R], axis=1)
    return (y_prompt, y_sample, gla_p, ret_p, gla_s, ret_s), res


def kernel(**inputs):
    outs, _ = run_cores(inputs, n_cores=8, n_seq=4, seq_len=2048, depth=DEPTH, nsub=4)
    return tuple(np.ascontiguousarray(o.astype(np.float32, copy=False)) for o in outs)
```

```python
from contextlib import ExitStack
import math
import numpy as np
import concourse.bass as bass
import concourse.mybir as mybir
from concourse.bass_utils import run_bass_kernel_spmd

F32 = mybir.dt.float32
BF16 = mybir.dt.bfloat16
AF = mybir.ActivationFunctionType
ALU = mybir.AluOpType

D = 1024
DFF = 2816
NIN = 8208
DEPTH = 4
N_META = 16
PAST_LEN = 1024
LN_EPS = 1e-5
GN_EPS = 1e-5
DN_ALPHA = (2.0 * DEPTH) ** 0.25
LVARS = (128, 16, 32)
LOFF = {128: 0, 16: 128, 32: 144}
LIDX = {128: 0, 16: 1, 32: 2}
NSLOT = 8
DEBUG_STOP = 99
MIX_STOP = 99
CORE_STOP = 99
NOSTORE = 0
SAME_SYNC = True
PROFILE_LOG = False
LAST_TRACKER = None


class Sem:
    __slots__ = ("h", "val")

    def __init__(self, h):
        self.h = h
        self.val = 0


class Buf:
    __slots__ = ("w", "r", "name", "x")

    def __init__(self, name="", x=False):
        self.w = None
        self.r = {}
        self.name = name
        self.x = x


class Eng:
    def __init__(self, name, sem, same_sync):
        self.name = name
        self.sem = sem
        self.waited = {}
        self.prog = []
        self.same_sync = same_sync


class Tracker:
    def __init__(self, nc, stack):
        self.nc = nc
        self.stack = stack
        self.pe = Eng("pe", self.new_sem("s_pe"), False)
        self.act = Eng("act", self.new_sem("s_act"), SAME_SYNC)
        self.dve = Eng("dve", self.new_sem("s_dve"), SAME_SYNC)
        self.pool = Eng("pool", self.new_sem("s_pool"), SAME_SYNC)
        self.sp = Eng("sp", self.new_sem("s_sp"), False)
        self.out_sems = []
        self.tag = ""
        self.pe_log = []

    def new_sem(self, name):
        return Sem(self.stack.enter_context(self.nc.semaphore(name)))

    def _deps(self, eng, reads, writes):
        deps = {}
        for b in reads:
            if b.w is not None:
                s, v = b.w
                if deps.get(s, 0) < v:
                    deps[s] = v
            if b.x:
                for s, v in b.r.items():
                    if deps.get(s, 0) < v:
                        deps[s] = v
        for b in writes:
            if b.w is not None:
                s, v = b.w
                if deps.get(s, 0) < v:
                    deps[s] = v
            for s, v in b.r.items():
                if deps.get(s, 0) < v:
                    deps[s] = v
        for s, v in deps.items():
            if s is eng.sem and not eng.same_sync:
                continue
            if eng.waited.get(s, 0) < v:
                eng.prog.append((0, s, v))
                eng.waited[s] = v

    def op(self, eng, fn, reads=(), writes=()):
        self._deps(eng, reads, writes)
        s = eng.sem
        s.val += 1
        tok = (s, s.val)
        eng.prog.append((1, fn, s, 1, self.tag))
        for b in writes:
            b.w = tok
            b.r = {}
        for b in reads:
            b.r[s] = s.val
        return tok

    def dma(self, qeng, sem, fn, reads=(), writes=(), n=1, is_out=False):
        self._deps(qeng, reads, writes)
        sem.val += 16 * n
        tok = (sem, sem.val)
        qeng.prog.append((1, fn, sem, 16))
        for b in writes:
            b.w = tok
            b.r = {}
        for b in reads:
            b.r[sem] = sem.val
        if is_out and sem not in self.out_sems:
            self.out_sems.append(sem)
        return tok

    def finish(self):
        for s in self.out_sems:
            self.sp.prog.append((0, s, s.val))

    def replay(self):
        def run(prog, e):
            for it in prog:
                if it[0] == 0:
                    e.wait_ge(it[1].h, it[2])
                else:
                    if PROFILE_LOG and len(it) > 4 and prog is self.pe.prog:
                        n0 = self.nc.n_instructions() if callable(self.nc.n_instructions) else self.nc.n_instructions
                    r = it[1](e)
                    if PROFILE_LOG and len(it) > 4 and prog is self.pe.prog:
                        n1 = self.nc.n_instructions() if callable(self.nc.n_instructions) else self.nc.n_instructions
                        self.pe_log.append((it[4], n1 - n0))
                    if isinstance(r, (list, tuple)):
                        for x in r:
                            x.then_inc(it[2].h, it[3])
                    else:
                        r.then_inc(it[2].h, it[3])

        with self.nc.Block() as block:
            @block.tensor
            def _(e):
                run(self.pe.prog, e)

            @block.scalar
            def _(e):
                run(self.act.prog, e)

            @block.vector
            def _(e):
                run(self.dve.prog, e)

            @block.gpsimd
            def _(e):
                run(self.pool.prog, e)

            @block.sync
            def _(e):
                run(self.sp.prog, e)


def build_program(n_seq, seq_len, depth, nsub, with_meta=True, with_sample=True):
    NT = 128 * nsub
    assert seq_len % NT == 0
    tiles_per_seq = seq_len // NT
    nc = bass.Bass("TRN2", target_bir_lowering=False)

    def din(name, shape):
        return nc.dram_tensor(name, list(shape), F32, kind="ExternalInput").ap()

    def dout(name, shape):
        return nc.dram_tensor(name, list(shape), F32, kind="ExternalOutput").ap()

    def dscr(name, shape):
        return nc.dram_tensor(name, list(shape), F32).ap()

    xp = din("xp", [n_seq, seq_len, D])
    xs = din("xs", [32, D])
    sg_in = din("sg_in", [depth, 4, 128, 256])
    sr_in = din("sr_in", [depth, 4, 128, 256])
    meta = din("meta", [N_META, D])
    ln_g = din("ln_g", [depth, 3, D])
    ln_b = din("ln_b", [depth, 3, D])
    w1u = din("w1u", [depth, D, 2 * DFF])
    w1d = din("w1d", [depth, DFF, D])
    w_in = din("w_in", [depth, D, NIN])
    w_a2 = din("w_a2", [depth, 16, 512])
    b_a = din("b_a", [depth, 512])
    bm_t = din("bm_t", [128, depth, 16])
    gng_t = din("gng_t", [128, depth, 8])
    gnr_t = din("gnr_t", [128, depth, 8])
    w_oa = din("w_oa", [depth, D, D])
    w_ob = din("w_ob", [depth, D, D])
    w_o = din("w_o", [depth, D, D])
    w2u = din("w2u", [depth, D, 2 * DFF])
    w2d = din("w2d", [depth, DFF, D])
    c_idf = din("c_idf", [128, 128])
    c_mask = din("c_mask", [128, 128])
    c_tri = din("c_tri", [128, 128])
    c_cos = din("c_cos", [N_META + 2048, 64])
    c_sin = din("c_sin", [N_META + 2048, 64])
    c_nsin = din("c_nsin", [N_META + 2048, 64])
    c_c1T = din("c_c1T", [8, 176])
    lng_t = din("lng_t", [128, depth, 3, 8])
    lnb_t = din("lnb_t", [128, depth, 3, 8])
    c_c2 = din("c_c2", [128, 3, 4])
    c_decr = din("c_decr", [3, 4])

    yp = dout("yp", [n_seq, seq_len, D])
    ys = dout("ys", [32, D])
    gp = dout("gp", [depth, n_seq, 4, 128, 256])
    rp = dout("rp", [depth, n_seq, 4, 128, 256])
    gs = dout("gs", [depth, 4, 128, 256])
    rs = dout("rs", [depth, 4, 128, 256])

    WSH = {"w1u": (D, 2 * DFF), "w1d": (DFF, D), "w_in": (D, NIN), "w_oa": (D, D), "w_ob": (D, D), "w_o": (D, D),
           "w2u": (D, 2 * DFF), "w2d": (DFF, D)}
    wfp = {"w1u": w1u, "w1d": w1d, "w_in": w_in, "w_oa": w_oa, "w_ob": w_ob, "w_o": w_o, "w2u": w2u, "w2d": w2d}
    wsc = {k: nc.dram_tensor(k + "_bf", [depth, r, c], BF16).ap() for k, (r, c) in WSH.items()}

    def wsrc(name, l):
        return wsc[name][l]

    smeta_g = dscr("smeta_g", [depth, 4, 128, 256])
    smeta_r = dscr("smeta_r", [depth, 4, 128, 256])
    scr_g = dscr("scr_g", [depth, 4, 128, 256])
    scr_r = dscr("scr_r", [depth, 4, 128, 256])

    with ExitStack() as st:
        T = Tracker(nc, st)
        pe, act, dve, pool, sp = T.pe, T.act, T.dve, T.pool, T.sp

        def sb(name, shape, dt=F32):
            return st.enter_context(nc.sbuf_tensor(name, list(shape), dt))

        idf = sb("idf", [128, 128])
        idb = sb("idb", [128, 128], BF16)
        mask = sb("mask", [128, 128])
        tri = sb("tri", [128, 128])
        neg16 = sb("neg16", [128, 1])
        mhalf = sb("mhalf", [128, 8])
        c1T = sb("c1T", [128, 8, 176])
        c2 = sb("c2", [128, 3, 4])
        decr = sb("decr", [128, 3, 4])
        ba_bc = sb("ba_bc", [128, 512])
        wa2 = sb("wa2", [16, 512], BF16)
        G = [sb(f"G{i}", [128, 512]) for i in range(5)]
        bm = sb("bm", [128, depth, 16])
        gng = sb("gng", [128, depth, 8])
        gnr = sb("gnr", [128, depth, 8])
        lng = sb("lng", [128, D])
        lnb = sb("lnb", [128, D])
        cosT = sb("cosT", [128, nsub, 64])
        sinT = sb("sinT", [128, nsub, 64])
        nsinT = sb("nsinT", [128, nsub, 64])
        h = sb("h", [128, nsub, D])
        hT = sb("hT", [128, 8, NT], BF16)
        ring = [sb(f"ring{i}", [128, 4096], BF16) for i in range(NSLOT)]
        gT = [sb(f"gT{i}", [128, 4, NT], BF16) for i in range(2)]
        satmp = [G[0], G[1]]
        agT = sb("agT", [16, NT], BF16)
        xb, ee, ltok, E1, E2 = G
        decg = sb("decg", [128, nsub, 4])
        qt = sb("qt", [128, nsub, 512], BF16)
        kt = sb("kt", [128, nsub, 512], BF16)
        vv = sb("vv", [128, nsub, 1024], BF16)
        sgate = sb("sgate", [128, nsub, 1024], BF16)
        zc, za, ztmp, rot = G[0:4]
        Sbf = sb("Sbf", [128, 4, 256], BF16)
        attb = sb("attb", [128, 4, 128], BF16)
        on = sb("on", [128, 4, 256])
        yin = sb("yin", [128, 1024], BF16)
        hst = sb("hst", [128, 4, 6])
        hmv = sb("hmv", [128, 4, 2])
        hve = sb("hve", [128, 4])
        hrs = sb("hrs", [128, 4])
        yinTa = sb("yinTa", [128, 8, NT], BF16)
        yinTb = sb("yinTb", [128, 8, NT], BF16)
        mT = sb("mT", [128, 8, NT], BF16)
        sgA, sgB, t1, t2 = G[0:4]
        Sg = sb("Sg", [128, 4, 256])
        Sr = sb("Sr", [128, 4, 256])
        lst = [sb(f"lst{i}", [128, 2, 6]) for i in range(nsub)]
        lmv = [sb(f"lmv{i}", [128, 2]) for i in range(nsub)]
        lve = [sb(f"lve{i}", [128, 1]) for i in range(nsub)]
        lrs = [sb(f"lrs{i}", [128, 1]) for i in range(nsub)]
        lnm = [sb(f"lnm{i}", [128, 1]) for i in range(nsub)]
        hb = [sb(f"hb{i}", [128, D], BF16) for i in range(nsub)]
        lngT = sb("lngT", [128, depth, 3, 8])
        lnbT = sb("lnbT", [128, depth, 3, 8])
        hnm = sb("hnm", [128, 4])
        qkT = sb("qkT", [128, 8, 128], BF16)

        ps = st.enter_context(nc.psum_tensor("ps", [128, 8, 512], F32))
        psb = ps.bitcast(BF16)

        b_const = Buf("const")
        b_w = {k: [Buf(f"w_{k}{l}") for l in range(depth)] for k in WSH}
        b_lp = Buf("lp")
        b_ln = Buf("lnp")
        b_cs = Buf("cossin")
        b_h = [[Buf(f"h{s}{j}") for j in range(2)] for s in range(nsub)]
        b_hT = [[Buf(f"hT{s}_{k}") for k in range(8)] for s in range(nsub)]

        def flat(xs):
            out = []
            for x in xs:
                if isinstance(x, list):
                    out.extend(x)
                else:
                    out.append(x)
            return out
        b_ring = [Buf(f"ring{i}") for i in range(NSLOT)]
        b_gT = [[Buf(f"gT{i}{c}") for c in range(4)] for i in range(2)]
        b_G = [Buf(f"G{i}") for i in range(5)]
        b_sa = [b_G[0], b_G[1]]
        b_ps = [Buf(f"ps{i}", x=True) for i in range(8)]
        b_agT = Buf("agT")
        b_xb, b_ee, b_ltok, b_E1, b_E2 = b_G
        b_decg = [Buf(f"decg{s}") for s in range(nsub)]
        b_qt = [Buf(f"qt{s}") for s in range(nsub)]
        b_kt = [Buf(f"kt{s}") for s in range(nsub)]
        b_vv = [[Buf(f"vv{s}{j}") for j in range(2)] for s in range(nsub)]
        b_sg = [[Buf(f"sg{s}{j}") for j in range(2)] for s in range(nsub)]
        b_zc, b_za, b_ztmp, b_rot = b_G[0:4]
        b_Sbf, b_qTt, b_kTt, b_attb, b_on, b_yin = Buf("Sbf"), Buf("qTt"), Buf("kTt"), Buf("attb"), Buf("on"), Buf("yin")
        b_hst, b_hmv, b_hve, b_hrs = Buf("hst"), Buf("hmv"), Buf("hve"), Buf("hrs")
        b_yTa = [Buf(f"yTa{s}") for s in range(nsub)]
        b_yTb = [Buf(f"yTb{s}") for s in range(nsub)]
        b_mT = [Buf(f"mT{c}") for c in range(8)]
        b_sgA, b_sgB, b_t1, b_t2 = b_G[0:4]
        b_Sg, b_Sr = Buf("Sg"), Buf("Sr")
        b_lst = [Buf(f"lst{i}") for i in range(nsub)]
        b_lmv = [Buf(f"lmv{i}") for i in range(nsub)]
        b_lve = [Buf(f"lve{i}") for i in range(nsub)]
        b_lrs = [Buf(f"lrs{i}") for i in range(nsub)]
        b_lnm = [Buf(f"lnm{i}") for i in range(nsub)]
        b_hb = [Buf(f"hb{i}") for i in range(nsub)]
        b_hnm, b_qkT = Buf("hnm"), Buf("qkT")
        b_smg = [Buf(f"smg{l}") for l in range(depth)]
        b_smr = [Buf(f"smr{l}") for l in range(depth)]
        b_scg = [Buf(f"scg{l}") for l in range(depth)]
        b_scr = [Buf(f"scr{l}") for l in range(depth)]

        s_ring = [T.new_sem(f"d_ring{i}") for i in range(NSLOT)]
        s_const = T.new_sem("d_const")
        s_cv = {k: [T.new_sem(f"d_cv_{k}{l}") for l in range(depth)] for k in WSH}
        s_lp = T.new_sem("d_lp")
        s_x = T.new_sem("d_x")
        s_y = T.new_sem("d_y")
        s_ln = T.new_sem("d_ln")
        s_cs = T.new_sem("d_cs")
        s_stl_g, s_stl_r = T.new_sem("d_stlg"), T.new_sem("d_stlr")
        s_sts_g, s_sts_r = T.new_sem("d_stsg"), T.new_sem("d_stsr")

        pstate = {"i": 0}

        def palloc():
            i = pstate["i"]
            pstate["i"] = (i + 1) % 8
            return i

        rstate = {"i": 0}

        def wload(src, a, b, dep):
            i = rstate["i"]
            rstate["i"] = (i + 1) % NSLOT
            view = ring[i][:, 0:a * b].rearrange("p (a b) -> p a b", a=a)
            T.dma(sp, s_ring[i], lambda e, view=view, src=src: e.dma_start(out=view, in_=src),
                  reads=[dep], writes=[b_ring[i]])
            return view, b_ring[i]

        def load_consts():
            def f(e):
                r = [
                    e.dma_start(out=idf[:], in_=c_idf[:, :]),
                    e.dma_start(out=mask[:], in_=c_mask[:, :]),
                    e.dma_start(out=tri[:], in_=c_tri[:, :]),
                    e.dma_start(out=c1T[:], in_=c_c1T.partition_broadcast(128)),
                    e.dma_start(out=c2[:], in_=c_c2[:, :, :]),
                    e.dma_start(out=decr[:], in_=c_decr.partition_broadcast(128)),
                    e.dma_start(out=bm[:], in_=bm_t[:, :, :]),
                    e.dma_start(out=gng[:], in_=gng_t[:, :, :]),
                    e.dma_start(out=gnr[:], in_=gnr_t[:, :, :]),
                    e.dma_start(out=lngT[:], in_=lng_t[:, :, :, :]),
                    e.dma_start(out=lnbT[:], in_=lnb_t[:, :, :, :]),
                ]
                return r
            T.dma(pool, s_const, f, writes=[b_const], n=11)
            T.op(dve, lambda e: e.tensor_copy(out=idb[:], in_=idf[:]), reads=[b_const], writes=[b_const])
            T.op(dve, lambda e: e.memset(neg16[:], -1.0 / 16.0), writes=[b_const])
            T.op(dve, lambda e: e.memset(mhalf[:], -0.5), writes=[b_const])

        def convert_weights():
            for l in range(depth):
                for name in ("w1u", "w1d", "w_in", "w_oa", "w_ob", "w_o", "w2u", "w2d"):
                    src, dst = wfp[name][l], wsc[name][l]
                    nchunk = WSH[name][0] // 128

                    def f(e, src=src, dst=dst, nchunk=nchunk):
                        return [e.dma_start(out=dst[c * 128:(c + 1) * 128, :], in_=src[c * 128:(c + 1) * 128, :])
                                for c in range(nchunk)]
                    T.dma(pool, s_cv[name][l], f, writes=[b_w[name][l]], n=nchunk)

        def rstd_pow(out_ap, in_ap, nrow, ncol, rb, wb):
            T.op(pool, lambda e: e.tensor_tensor(out=out_ap, in0=in_ap, in1=mhalf[0:nrow, 0:ncol], op=ALU.pow),
                 reads=[rb, b_const], writes=[wb])

        def transpose_to_hT(s, L, aff=None):
            p0, p1 = palloc(), palloc()
            if aff is None:
                def f(e):
                    r = None
                    for k in range(8):
                        bank = p0 if k < 4 else p1
                        r = e.transpose(ps[:, bank, (k % 4) * 128:(k % 4) * 128 + L],
                                        h[0:L, s, k * 128:(k + 1) * 128], idf[0:L, 0:L])
                    return r
                T.op(pe, f, reads=[b_h[s][0], b_h[s][1], b_const], writes=[b_ps[p0], b_ps[p1]])
            else:
                i = s

                def f(e):
                    r = None
                    for k in range(8):
                        bank = p0 if k < 4 else p1
                        r = e.transpose(psb[:, bank, (k % 4) * 128:(k % 4) * 128 + L],
                                        hb[i][0:L, k * 128:(k + 1) * 128], idb[0:L, 0:L])
                    return r
                T.op(pe, f, reads=[b_hb[i], b_const], writes=[b_ps[p0], b_ps[p1]])
            if aff is None:
                T.op(act, lambda e: e.copy(out=hT[:, 0:4, s * 128:s * 128 + L],
                                            in_=ps[:, p0, :].rearrange("p (k t) -> p k t", k=4)[:, :, 0:L]),
                     reads=[b_ps[p0]], writes=b_hT[s][0:4])
                T.op(dve, lambda e: e.tensor_copy(out=hT[:, 4:8, s * 128:s * 128 + L],
                                                   in_=ps[:, p1, :].rearrange("p (k t) -> p k t", k=4)[:, :, 0:L]),
                     reads=[b_ps[p1]], writes=b_hT[s][4:8])
                return
            l, idx = aff
            for k in range(8):
                bank = p0 if k < 4 else p1
                src = psb[:, bank, (k % 4) * 128:(k % 4) * 128 + L]
                dst = hT[:, k, s * 128:s * 128 + L]
                if k < 4:
                    T.op(act, lambda e, src=src, dst=dst, k=k: e.activation(
                        out=dst, in_=src, func=AF.Identity, scale=lngT[:, l, idx, k:k + 1], bias=lnbT[:, l, idx, k:k + 1]),
                        reads=[b_ps[bank], b_const], writes=[b_hT[s][k]])
                else:
                    T.op(dve, lambda e, src=src, dst=dst, k=k: e.tensor_scalar(
                        out=dst, in0=src, scalar1=lngT[:, l, idx, k:k + 1], scalar2=lnbT[:, l, idx, k:k + 1],
                        op0=ALU.mult, op1=ALU.add),
                        reads=[b_ps[bank], b_const], writes=[b_hT[s][k]])

        def load_lp(l):
            T.dma(pool, s_lp, lambda e: [e.dma_start(out=ba_bc[:], in_=b_a[l].partition_broadcast(128)),
                                         e.dma_start(out=wa2[:], in_=w_a2[l])], writes=[b_lp], n=2)

        def load_ln(l, idx):
            def f(e):
                return [e.dma_start(out=lng[:], in_=ln_g[l, idx].partition_broadcast(128)),
                        e.dma_start(out=lnb[:], in_=ln_b[l, idx].partition_broadcast(128))]
            T.dma(pool, s_ln, f, writes=[b_ln], n=2)

        def layer_norm_all(tile, l, idx):
            ns, L = tile["ns"], tile["L"]
            T.tag = "ln"
            for s in range(ns):
                i = s
                hs = h[0:L, s, :]
                T.op(dve, lambda e, i=i, s=s: e.bn_stats(out=lst[i][0:L, 0, :], in_=h[0:L, s, 0:512]), reads=[b_h[s][0]], writes=[b_lst[i]])
                T.op(dve, lambda e, i=i, s=s: e.bn_stats(out=lst[i][0:L, 1, :], in_=h[0:L, s, 512:1024]), reads=[b_h[s][1]], writes=[b_lst[i]])
                T.op(dve, lambda e, i=i: e.bn_aggr(out=lmv[i][0:L, :], in_=lst[i][0:L, :, :]), reads=[b_lst[i]], writes=[b_lmv[i]])
                T.op(dve, lambda e, i=i: e.tensor_scalar_add(out=lve[i][0:L, :], in0=lmv[i][0:L, 1:2], scalar1=LN_EPS),
                     reads=[b_lmv[i]], writes=[b_lve[i]])
                rstd_pow(lrs[i][0:L, :], lve[i][0:L, :], L, 1, b_lve[i], b_lrs[i])
                T.op(dve, lambda e, i=i: e.scalar_tensor_tensor(out=lnm[i][0:L, :], in0=lmv[i][0:L, 0:1], scalar=-1.0, in1=lrs[i][0:L, :],
                                                                 op0=ALU.mult, op1=ALU.mult),
                     reads=[b_lmv[i], b_lrs[i]], writes=[b_lnm[i]])
                T.op(act, lambda e, i=i, hs=hs: e.activation(out=hb[i][0:L, :], in_=hs, func=AF.Identity, scale=lrs[i][0:L, :], bias=lnm[i][0:L, :]),
                     reads=[b_h[s][0], b_h[s][1], b_lrs[i], b_lnm[i]], writes=[b_hb[i]])
                T.op(act, lambda e, i=i, hs=hs: e.activation(out=hs, in_=hs, func=AF.Identity, scale=lrs[i][0:L, :], bias=lnm[i][0:L, :]),
                     reads=[b_h[s][0], b_h[s][1], b_lrs[i], b_lnm[i]], writes=[b_h[s][0], b_h[s][1]])
            for s in range(ns):
                transpose_to_hT(s, L, aff=(l, idx))
            for s in range(ns):
                hs = h[0:L, s, :]
                T.op(pool, lambda e, hs=hs: e.tensor_tensor(out=hs, in0=hs, in1=lng[0:L, :], op=ALU.mult),
                     reads=[b_h[s][0], b_h[s][1], b_ln], writes=[b_h[s][0], b_h[s][1]])
                T.op(pool, lambda e, hs=hs: e.tensor_tensor(out=hs, in0=hs, in1=lnb[0:L, :], op=ALU.add),
                     reads=[b_h[s][0], b_h[s][1], b_ln], writes=[b_h[s][0], b_h[s][1]])

        def residual_add(s, L, hf, pbank, first):
            hh = h[0:L, s, hf * 512:(hf + 1) * 512]
            if first:
                T.op(dve, lambda e: e.scalar_tensor_tensor(out=hh, in0=hh, scalar=DN_ALPHA, in1=ps[0:L, pbank, :],
                                                            op0=ALU.mult, op1=ALU.add),
                     reads=[b_h[s][hf], b_ps[pbank]], writes=[b_h[s][hf]])
            else:
                T.op(dve, lambda e: e.tensor_tensor(out=hh, in0=hh, in1=ps[0:L, pbank, :], op=ALU.add),
                     reads=[b_h[s][hf], b_ps[pbank]], writes=[b_h[s][hf]])

        def ffn(tile, nu, nd, l):
            T.tag = "ffn"
            ns, L = tile["ns"], tile["L"]
            ntok = tile["ntok"]
            wu_v = wsrc(nu, l).rearrange("(k p) n -> p k n", p=128)
            wd_v = wsrc(nd, l).rearrange("(c p) n -> p c n", p=128)
            du, dd = b_w[nu][l], b_w[nd][l]
            groups = [(0, 512), (512, 512), (1024, 512), (1536, 512), (2048, 512), (2560, 256)]
            for gi, (f0, fw) in enumerate(groups):
                nch = fw // 128
                wa_v, wa_b = wload(wu_v[:, :, f0:f0 + fw], 8, fw, du)
                wb_v, wb_b = wload(wu_v[:, :, DFF + f0:DFF + f0 + fw], 8, fw, du)
                wd_s, wd_b = wload(wd_v[:, f0 // 128:f0 // 128 + nch, :], nch, 1024, dd)
                gb = gi % 2
                for c in range(nch):
                    pa, pb = palloc(), palloc()

                    def fup(e, w_v=wa_v, bank=pa, c=c):
                        r = None
                        for k in range(8):
                            r = e.matmul(ps[:, bank, 0:ntok], lhsT=w_v[:, k, c * 128:(c + 1) * 128],
                                         rhs=hT[:, k, 0:ntok], start=(k == 0), stop=(k == 7))
                        return r
                    T.op(pe, fup, reads=[wa_b] + flat(b_hT[0:ns]), writes=[b_ps[pa]])

                    def fupb(e, w_v=wb_v, bank=pb, c=c):
                        r = None
                        for k in range(8):
                            r = e.matmul(ps[:, bank, 0:ntok], lhsT=w_v[:, k, c * 128:(c + 1) * 128],
                                         rhs=hT[:, k, 0:ntok], start=(k == 0), stop=(k == 7))
                        return r
                    T.op(pe, fupb, reads=[wb_b] + flat(b_hT[0:ns]), writes=[b_ps[pb]])
                    si = c % 2
                    T.op(act, lambda e, pa=pa, si=si: e.activation(out=satmp[si][:, 0:ntok], in_=ps[:, pa, 0:ntok], func=AF.Silu),
                         reads=[b_ps[pa]], writes=[b_sa[si]])
                    T.op(dve, lambda e, pb=pb, si=si, c=c, gb=gb: e.scalar_tensor_tensor(
                        out=gT[gb][:, c, 0:ntok], in0=satmp[si][:, 0:ntok], scalar=0.5, in1=ps[:, pb, 0:ntok],
                        op0=ALU.mult, op1=ALU.mult),
                        reads=[b_sa[si], b_ps[pb]], writes=[b_gT[gb][c]])
                for s in range(ns):
                    for hf in range(2):
                        py = palloc()

                        def fdn(e, s=s, hf=hf, py=py, nch=nch, gb=gb, wd_s=wd_s):
                            r = None
                            for c in range(nch):
                                r = e.matmul(ps[0:L, py, :], lhsT=gT[gb][:, c, s * 128:s * 128 + L],
                                             rhs=wd_s[:, c, hf * 512:(hf + 1) * 512], start=(c == 0), stop=(c == nch - 1))
                            return r
                        T.op(pe, fdn, reads=[wd_b] + b_gT[gb][0:nch], writes=[b_ps[py]])
                        residual_add(s, L, hf, py, first=(gi == 0))

        def core_A(s, L, dec_ap, dec_buf, S, b_S, is_ret):
            pt, pa, pk0, pk1 = 0, 1, 2, 3
            po0, po1 = (4, 5) if s % 2 == 0 else (6, 7)
            T.op(pool, lambda e: e.tensor_tensor(out=Sbf[:], in0=S[:], in1=dec_ap.rearrange("p (h o) -> p h o", o=1).to_broadcast([128, 4, 256]),
                                                  op=ALU.mult),
                 reads=[b_S, dec_buf], writes=[b_Sbf])

            def ftr(e):
                r = None
                for hh in range(4):
                    r = e.transpose(psb[:, pt, hh * 128:hh * 128 + L], qt[0:L, s, hh * 128:(hh + 1) * 128], idb[0:L, 0:L])
                for hh in range(4):
                    r = e.transpose(psb[:, pt, 512 + hh * 128:512 + hh * 128 + L], kt[0:L, s, hh * 128:(hh + 1) * 128], idb[0:L, 0:L])
                return r
            T.op(pe, ftr, reads=[b_qt[s], b_kt[s], b_const], writes=[b_ps[pt]])
            src = psb[:, pt, :].rearrange("p (g t) -> p g t", g=8)[:, :, 0:L]
            if not is_ret:
                T.op(act, lambda e: e.copy(out=qkT[:, :, 0:L], in_=src), reads=[b_ps[pt]], writes=[b_qkT])
            else:
                T.op(dve, lambda e: e.tensor_tensor(out=qkT[:, :, 0:L], in0=src, in1=c1T[:, :, LOFF[L]:LOFF[L] + L], op=ALU.mult),
                     reads=[b_ps[pt], b_const], writes=[b_qkT])

            def fatt(e):
                r = None
                for hh in range(4):
                    r = e.matmul(ps[0:L, pa, hh * 128:hh * 128 + L], lhsT=qkT[:, 4 + hh, 0:L], rhs=qkT[:, hh, 0:L],
                                 start=True, stop=True)
                return r
            T.op(pe, fatt, reads=[b_qkT], writes=[b_ps[pa]])
            T.op(dve, lambda e: e.tensor_tensor(
                out=attb[0:L, :, 0:L], in0=ps[0:L, pa, :].rearrange("p (h t) -> p h t", h=4)[:, :, 0:L],
                in1=mask[0:L, 0:L].rearrange("p (o t) -> p o t", o=1).to_broadcast([L, 4, L]), op=ALU.mult),
                reads=[b_ps[pa], b_const], writes=[b_attb])

            def fo(e):
                r = None
                for hh in range(4):
                    bank = po0 if hh < 2 else po1
                    oo = ps[0:L, bank, (hh % 2) * 256:(hh % 2 + 1) * 256]
                    e.matmul(oo, lhsT=attb[0:L, hh, 0:L], rhs=vv[0:L, s, hh * 256:(hh + 1) * 256], start=True, stop=False)
                    r = e.matmul(oo, lhsT=qkT[:, hh, 0:L], rhs=Sbf[:, hh, :], start=False, stop=True)
                return r
            T.op(pe, fo, reads=[b_attb, b_vv[s][0], b_vv[s][1], b_qkT, b_Sbf], writes=[b_ps[po0], b_ps[po1]])

            def fkv(e):
                r = None
                for hh in range(4):
                    bank = pk0 if hh < 2 else pk1
                    r = e.matmul(ps[:, bank, (hh % 2) * 256:(hh % 2 + 1) * 256], lhsT=kt[0:L, s, hh * 128:(hh + 1) * 128],
                                 rhs=vv[0:L, s, hh * 256:(hh + 1) * 256], start=True, stop=True)
                return r
            T.op(pe, fkv, reads=[b_kt[s], b_vv[s][0], b_vv[s][1]], writes=[b_ps[pk0], b_ps[pk1]])
            for hh in range(4):
                bank = pk0 if hh < 2 else pk1
                T.op(dve, lambda e, hh=hh, bank=bank: e.scalar_tensor_tensor(
                    out=S[:, hh, :], in0=S[:, hh, :], scalar=dec_ap[:, hh:hh + 1],
                    in1=ps[:, bank, (hh % 2) * 256:(hh % 2 + 1) * 256], op0=ALU.mult, op1=ALU.add),
                    reads=[b_S, dec_buf, b_ps[bank]], writes=[b_S])

        def core_B(s, L, gn_ap, yT, b_yT):
            po0, po1 = (4, 5) if s % 2 == 0 else (6, 7)
            py = 2
            for hh in range(4):
                bank = po0 if hh < 2 else po1
                T.op(dve, lambda e, hh=hh, bank=bank: e.bn_stats(out=hst[0:L, hh, :], in_=ps[0:L, bank, (hh % 2) * 256:(hh % 2 + 1) * 256]),
                     reads=[b_ps[bank]], writes=[b_hst])
            for hh in range(4):
                T.op(dve, lambda e, hh=hh: e.bn_aggr(out=hmv[0:L, hh, :], in_=hst[0:L, hh:hh + 1, :]),
                     reads=[b_hst], writes=[b_hmv])
            T.op(dve, lambda e: e.tensor_scalar_add(out=hve[0:L, :], in0=hmv[0:L, :, 1], scalar1=GN_EPS),
                 reads=[b_hmv], writes=[b_hve])
            rstd_pow(hrs[0:L, :], hve[0:L, :], L, 4, b_hve, b_hrs)
            T.op(act, lambda e: e.activation(out=on[0:L, 0, :], in_=sgate[0:L, s, 0:256], func=AF.Identity, scale=hrs[0:L, 0:1]),
                 reads=[b_sg[s][0], b_hrs], writes=[b_on])
            for hh in range(1, 4):
                T.op(act, lambda e, hh=hh: e.activation(out=on[0:L, hh, :], in_=sgate[0:L, s, hh * 256:(hh + 1) * 256], func=AF.Identity,
                                                         scale=hrs[0:L, hh:hh + 1]),
                     reads=[b_sg[s][hh // 2], b_hrs, b_on], writes=[b_on])
            for hh in range(4):
                bank = po0 if hh < 2 else po1
                T.op(dve, lambda e, hh=hh, bank=bank: e.scalar_tensor_tensor(
                    out=yin[0:L, hh * 256:(hh + 1) * 256], in0=ps[0:L, bank, (hh % 2) * 256:(hh % 2 + 1) * 256],
                    scalar=hmv[0:L, hh, 0:1], in1=on[0:L, hh, :], op0=ALU.subtract, op1=ALU.mult),
                    reads=[b_ps[bank], b_hmv, b_on, b_yin] if hh else [b_ps[bank], b_hmv, b_on], writes=[b_yin])

            def fty(e):
                r = None
                for c in range(8):
                    r = e.transpose(psb[:, py, c * 128:c * 128 + L], yin[0:L, c * 128:(c + 1) * 128], idb[0:L, 0:L])
                return r
            T.op(pe, fty, reads=[b_yin, b_const], writes=[b_ps[py]])
            T.op(dve, lambda e: e.tensor_tensor(
                out=yT[:, :, s * 128:s * 128 + L], in0=psb[:, py, :].rearrange("p (c t) -> p c t", c=8)[:, :, 0:L],
                in1=gn_ap.rearrange("p (c o) -> p c o", o=1).to_broadcast([128, 8, L]), op=ALU.mult),
                reads=[b_ps[py], b_const], writes=[b_yT[s]])

        def mixer_cores(ns, L, dec_fn, S, b_S, is_ret, gn_ap, yT, b_yT):
            for s in range(ns):
                dec_ap, dec_buf = dec_fn(s)
                core_A(s, L, dec_ap, dec_buf, S, b_S, is_ret)
                if s >= 1:
                    core_B(s - 1, L, gn_ap, yT, b_yT)
            core_B(ns - 1, L, gn_ap, yT, b_yT)

        def proj_tok(s, L, w_v, w_b):
            pz = palloc()

            def f(e):
                r = None
                for k in range(8):
                    r = e.matmul(ps[0:L, pz, :], lhsT=hT[:, k, s * 128:s * 128 + L], rhs=w_v[:, k, :],
                                 start=(k == 0), stop=(k == 7))
                return r
            T.op(pe, f, reads=[w_b] + b_hT[s], writes=[b_ps[pz]])
            return pz

        def rotary(s, L, pz, out_ap, out_bufs):
            T.op(act, lambda e: e.copy(out=zc[0:L, :], in_=ps[0:L, pz, :]), reads=[b_ps[pz]], writes=[b_zc])
            z4 = zc[0:L, :].rearrange("p (h w j) -> p h w j", h=4, w=2)
            a4 = za[0:L, :].rearrange("p (h w j) -> p h w j", h=4, w=2)
            t4 = ztmp[0:L, :].rearrange("p (h w j) -> p h w j", h=4, w=2)
            cos_b = cosT[0:L, s, :].rearrange("p (o j) -> p o j", o=1).to_broadcast([L, 8, 64])
            sin_b = sinT[0:L, s, :].rearrange("p (o j) -> p o j", o=1).to_broadcast([L, 4, 64])
            nsin_b = nsinT[0:L, s, :].rearrange("p (o j) -> p o j", o=1).to_broadcast([L, 4, 64])
            T.op(dve, lambda e: e.tensor_tensor(out=za[0:L, :].rearrange("p (g j) -> p g j", g=8),
                                                 in0=zc[0:L, :].rearrange("p (g j) -> p g j", g=8), in1=cos_b, op=ALU.mult),
                 reads=[b_zc, b_cs], writes=[b_za])
            T.op(pool, lambda e: e.tensor_tensor(out=t4[:, :, 0, :], in0=z4[:, :, 1, :], in1=nsin_b, op=ALU.mult),
                 reads=[b_zc, b_cs], writes=[b_ztmp])
            T.op(pool, lambda e: e.tensor_tensor(out=t4[:, :, 1, :], in0=z4[:, :, 0, :], in1=sin_b, op=ALU.mult),
                 reads=[b_zc, b_cs], writes=[b_ztmp])
            T.op(dve, lambda e: e.tensor_tensor(out=out_ap, in0=za[0:L, :], in1=ztmp[0:L, :], op=ALU.add),
                 reads=[b_za, b_ztmp], writes=out_bufs)

        def mixers(tile, l):
            ns, L, ntok = tile["ns"], tile["L"], tile["ntok"]
            T.tag = "mix.gla_qk"
            li = LIDX[L]
            W = wsrc("w_in", l).rearrange("(k p) n -> p k n", p=128)
            wdeps = [b_w["w_in"][l]]
            wag_v, wag_b = wload(W[:, :, 3072:3088], 8, 16, wdeps[0])
            pg = palloc()

            def fag(e):
                r = None
                for k in range(8):
                    r = e.matmul(ps[0:16, pg, 0:ntok], lhsT=wag_v[:, k, :], rhs=hT[:, k, 0:ntok], start=(k == 0), stop=(k == 7))
                return r
            T.op(pe, fag, reads=[wag_b] + flat(b_hT[0:ns]), writes=[b_ps[pg]])
            T.op(act, lambda e: e.copy(out=agT[:, 0:ntok], in_=ps[0:16, pg, 0:ntok]), reads=[b_ps[pg]], writes=[b_agT])
            if MIX_STOP < 2:
                return
            wq_v, wq_b = wload(W[:, :, 0:512], 8, 512, wdeps[0])
            wk_v, wk_b = wload(W[:, :, 512:1024], 8, 512, wdeps[0])
            for s in range(ns):
                px = palloc()
                T.op(pe, lambda e, s=s, px=px: e.matmul(ps[0:L, px, :], lhsT=agT[:, s * 128:s * 128 + L], rhs=wa2[:, :],
                                                         start=True, stop=True),
                     reads=[b_agT, b_lp], writes=[b_ps[px]])
                T.op(dve, lambda e, px=px: e.tensor_tensor(out=xb[0:L, :], in0=ps[0:L, px, :], in1=ba_bc[0:L, :], op=ALU.add),
                     reads=[b_ps[px], b_lp], writes=[b_xb])
                T.op(act, lambda e: e.activation(out=ee[0:L, :], in_=xb[0:L, :], func=AF.Exp, scale=-1.0),
                     reads=[b_xb], writes=[b_ee])
                T.op(act, lambda e: e.activation(out=ltok[0:L, :], in_=ee[0:L, :], func=AF.Ln, bias=1.0),
                     reads=[b_ee], writes=[b_ltok])
                pd, pbl = palloc(), palloc()
                T.op(pe, lambda e, pd=pd: e.matmul(ps[0:L, pd, :], lhsT=tri[0:L, 0:L], rhs=ltok[0:L, :], start=True, stop=True),
                     reads=[b_ltok, b_const], writes=[b_ps[pd]])

                def fbl(e, pbl=pbl):
                    r = None
                    for hh in range(4):
                        r = e.matmul(ps[:, pbl, hh:hh + 1], lhsT=ltok[0:L, hh * 128:(hh + 1) * 128], rhs=neg16[0:L, 0:1],
                                     start=True, stop=True)
                    return r
                T.op(pe, fbl, reads=[b_ltok, b_const], writes=[b_ps[pbl]])
                T.op(act, lambda e, pd=pd: e.activation(out=E1[0:L, :], in_=ps[0:L, pd, :], func=AF.Exp,
                                                         bias=float(math.log(128.0 ** -0.5)), scale=1.0),
                     reads=[b_ps[pd]], writes=[b_E1])
                T.op(act, lambda e, pd=pd: e.activation(out=E2[0:L, :], in_=ps[0:L, pd, :], func=AF.Exp, scale=-1.0),
                     reads=[b_ps[pd]], writes=[b_E2])
                T.op(act, lambda e, pbl=pbl, s=s: e.activation(out=decg[:, s, :], in_=ps[:, pbl, 0:4], func=AF.Exp),
                     reads=[b_ps[pbl]], writes=[b_decg[s]])
                pq = proj_tok(s, L, wq_v, wq_b)
                T.op(dve, lambda e, pq=pq, s=s: e.tensor_tensor(out=qt[0:L, s, :], in0=ps[0:L, pq, :], in1=E1[0:L, :], op=ALU.mult),
                     reads=[b_ps[pq], b_E1], writes=[b_qt[s]])
                pk = proj_tok(s, L, wk_v, wk_b)
                T.op(dve, lambda e, pk=pk, s=s: e.tensor_tensor(out=kt[0:L, s, :], in0=ps[0:L, pk, :], in1=E2[0:L, :], op=ALU.mult),
                     reads=[b_ps[pk], b_E2], writes=[b_kt[s]])

            def vg_pieces(c0v, c0g):
                for j in range(2):
                    w_v, w_b = wload(W[:, :, c0v + 512 * j:c0v + 512 * (j + 1)], 8, 512, wdeps[0])
                    for s in range(ns):
                        pz = proj_tok(s, L, w_v, w_b)
                        T.op(act, lambda e, pz=pz, s=s, j=j: e.copy(out=vv[0:L, s, 512 * j:512 * (j + 1)], in_=ps[0:L, pz, :]),
                             reads=[b_ps[pz]], writes=[b_vv[s][j]])
                for j in range(2):
                    w_v, w_b = wload(W[:, :, c0g + 512 * j:c0g + 512 * (j + 1)], 8, 512, wdeps[0])
                    for s in range(ns):
                        pz = proj_tok(s, L, w_v, w_b)
                        T.op(act, lambda e, pz=pz, s=s, j=j: e.activation(out=sgate[0:L, s, 512 * j:512 * (j + 1)], in_=ps[0:L, pz, :],
                                                                            func=AF.Silu),
                             reads=[b_ps[pz]], writes=[b_sg[s][j]])

            if MIX_STOP < 3:
                return
            T.tag = "mix.gla_vg"
            vg_pieces(1024, 2048)
            T.tag = "mix.gla_core"
            if MIX_STOP < 4:
                return
            tile["state_load"](l, "g")
            mixer_cores(ns, L, lambda s: (decg[:, s, :], b_decg[s]), Sg, b_Sg, False, gng[:, l, :], yinTa, b_yTa)
            tile["state_store"](l, "g")
            if MIX_STOP < 5:
                return
            T.tag = "mix.ret_qk"
            wq_v, wq_b = wload(W[:, :, 3088:3600], 8, 512, wdeps[0])
            wk_v, wk_b = wload(W[:, :, 3600:4112], 8, 512, wdeps[0])
            for s in range(ns):
                pq = proj_tok(s, L, wq_v, wq_b)
                rotary(s, L, pq, qt[0:L, s, :], [b_qt[s]])
                pk = proj_tok(s, L, wk_v, wk_b)
                rotary(s, L, pk, rot[0:L, :], [b_rot])
                T.op(dve, lambda e, s=s: e.tensor_tensor(
                    out=kt[0:L, s, :].rearrange("p (h d) -> p h d", h=4), in0=rot[0:L, :].rearrange("p (h d) -> p h d", h=4),
                    in1=c2[0:L, li, :].rearrange("p (h o) -> p h o", o=1).to_broadcast([L, 4, 128]), op=ALU.mult),
                    reads=[b_rot, b_const], writes=[b_kt[s]])
            if MIX_STOP < 6:
                return
            T.tag = "mix.ret_vg"
            vg_pieces(4112, 5136)
            T.tag = "mix.ret_core"
            tile["state_load"](l, "r")
            mixer_cores(ns, L, lambda s: (decr[:, li, :], b_const), Sr, b_Sr, True, gnr[:, l, :], yinTb, b_yTb)
            tile["state_store"](l, "r")
            if MIX_STOP < 7:
                return
            T.tag = "mix.merge"
            woa = wsrc("w_oa", l).rearrange("(k p) n -> p k n", p=128)
            wob = wsrc("w_ob", l).rearrange("(k p) n -> p k n", p=128)
            for cg in range(2):
                a_v, a_b = wload(woa[:, :, 512 * cg:512 * (cg + 1)], 8, 512, b_w["w_oa"][l])
                b_v, b_b = wload(wob[:, :, 512 * cg:512 * (cg + 1)], 8, 512, b_w["w_ob"][l])
                ma_v, ma_b = wload(W[:, :, 6160 + 512 * cg:6160 + 512 * (cg + 1)], 8, 512, wdeps[0])
                mb_v, mb_b = wload(W[:, :, 7184 + 512 * cg:7184 + 512 * (cg + 1)], 8, 512, wdeps[0])
                for c4 in range(4):
                    c = 4 * cg + c4
                    banks = []
                    for (w_v, w_b, src, srcb) in ((a_v, a_b, yinTa, b_yTa), (b_v, b_b, yinTb, b_yTb),
                                                  (ma_v, ma_b, hT, b_hT), (mb_v, mb_b, hT, b_hT)):
                        pz = palloc()

                        def f(e, w_v=w_v, src=src, pz=pz, c4=c4):
                            r = None
                            for k in range(8):
                                r = e.matmul(ps[:, pz, 0:ntok], lhsT=w_v[:, k, c4 * 128:(c4 + 1) * 128], rhs=src[:, k, 0:ntok],
                                             start=(k == 0), stop=(k == 7))
                            return r
                        T.op(pe, f, reads=[w_b] + flat(srcb[0:ns]), writes=[b_ps[pz]])
                        banks.append(pz)
                    pya, pyb, pma, pmb = banks
                    T.op(act, lambda e, pma=pma, c=c: e.activation(out=sgA[:, 0:ntok], in_=ps[:, pma, 0:ntok], func=AF.Sigmoid,
                                                                    bias=bm[:, l, c:c + 1], scale=1.0),
                         reads=[b_ps[pma], b_const], writes=[b_sgA])
                    T.op(act, lambda e, pmb=pmb, c=c: e.activation(out=sgB[:, 0:ntok], in_=ps[:, pmb, 0:ntok], func=AF.Sigmoid,
                                                                    bias=bm[:, l, 8 + c:9 + c], scale=1.0),
                         reads=[b_ps[pmb], b_const], writes=[b_sgB])
                    T.op(dve, lambda e, pya=pya: e.tensor_tensor(out=t1[:, 0:ntok], in0=sgA[:, 0:ntok], in1=ps[:, pya, 0:ntok], op=ALU.mult),
                         reads=[b_sgA, b_ps[pya]], writes=[b_t1])
                    T.op(dve, lambda e, pyb=pyb: e.tensor_tensor(out=t2[:, 0:ntok], in0=sgB[:, 0:ntok], in1=ps[:, pyb, 0:ntok], op=ALU.mult),
                         reads=[b_sgB, b_ps[pyb]], writes=[b_t2])
                    T.op(dve, lambda e, c=c: e.tensor_tensor(out=mT[:, c, 0:ntok], in0=t1[:, 0:ntok], in1=t2[:, 0:ntok], op=ALU.add),
                         reads=[b_t1, b_t2], writes=[b_mT[c]])
            if MIX_STOP < 8:
                return
            T.tag = "mix.outproj"
            wo = wsrc("w_o", l).rearrange("(k p) n -> p k n", p=128)
            wo_p = [wload(wo[:, :, 512 * hf:512 * (hf + 1)], 8, 512, b_w["w_o"][l]) for hf in range(2)]
            for s in range(ns):
                for hf in range(2):
                    po = palloc()

                    def f(e, s=s, hf=hf, po=po):
                        r = None
                        for k in range(8):
                            r = e.matmul(ps[0:L, po, :], lhsT=mT[:, k, s * 128:s * 128 + L], rhs=wo_p[hf][0][:, k, :],
                                         start=(k == 0), stop=(k == 7))
                        return r
                    T.op(pe, f, reads=[wo_p[hf][1]] + b_mT, writes=[b_ps[po]])
                    residual_add(s, L, hf, po, first=True)

        def run_tile(tile):
            ns, L = tile["ns"], tile["L"]
            T.dma(pool, s_x, lambda e: e.dma_start(out=h[0:L, 0:ns, :], in_=tile["x"].rearrange("(s p) d -> p s d", p=L)),
                  writes=[b for s in range(ns) for b in b_h[s]])
            pos0 = tile["pos0"]

            def fcs(e):
                return [e.dma_start(out=dst[0:L, 0:ns, :], in_=src[pos0:pos0 + ns * L, :].rearrange("(s p) j -> p s j", p=L))
                        for dst, src in ((cosT, c_cos), (sinT, c_sin), (nsinT, c_nsin))]
            T.dma(pool, s_cs, fcs, writes=[b_cs], n=3)
            if DEBUG_STOP >= 1:
                for s in range(ns):
                    transpose_to_hT(s, L)
            for l in range(depth):
                if DEBUG_STOP < 2:
                    break
                load_lp(l)
                load_ln(l, 0)
                ffn(tile, "w1u", "w1d", l)
                if DEBUG_STOP < 3:
                    break
                layer_norm_all(tile, l, 0)
                if DEBUG_STOP < 4:
                    break
                load_ln(l, 1)
                mixers(tile, l)
                if DEBUG_STOP < 5:
                    break
                layer_norm_all(tile, l, 1)
                load_ln(l, 2)
                ffn(tile, "w2u", "w2d", l)
                layer_norm_all(tile, l, 2)
            if tile["y"] is not None:
                T.dma(pool, s_y, lambda e: e.dma_start(out=tile["y"].rearrange("(s p) d -> p s d", p=L), in_=h[0:L, 0:ns, :]),
                      reads=[b for s in range(ns) for b in b_h[s]], is_out=True)

        def make_state_fns(src_g, src_r, dst_g, dst_r, out_g, out_r):
            def load(l, which):
                S, b_S, sem = (Sg, b_Sg, s_stl_g) if which == "g" else (Sr, b_Sr, s_stl_r)
                src = (src_g if which == "g" else src_r)
                if src is None:
                    T.op(dve, lambda e: e.memset(S[:].rearrange("p h e -> p (h e)"), 0.0), writes=[b_S])
                    return
                ap, bb = src(l)
                T.dma(pool, sem, lambda e: e.dma_start(out=S[:], in_=ap.rearrange("h d e -> d h e")),
                      reads=([bb] if bb is not None else []), writes=[b_S])

            def store(l, which):
                if NOSTORE:
                    return
                S, b_S, sem = (Sg, b_Sg, s_sts_g) if which == "g" else (Sr, b_Sr, s_sts_r)
                ap, bb = (dst_g if which == "g" else dst_r)(l)
                is_out = out_g if which == "g" else out_r
                T.dma(pool, sem, lambda e: e.dma_start(out=ap.rearrange("h d e -> d h e"), in_=S[:]),
                      reads=[b_S], writes=([bb] if bb is not None else []), is_out=is_out)
            return load, store

        load_consts()
        convert_weights()
        if with_meta:
            ld, stf = make_state_fns(None, None, lambda l: (smeta_g[l], b_smg[l]), lambda l: (smeta_r[l], b_smr[l]), False, False)
            run_tile(dict(ns=1, L=16, ntok=16, x=meta, y=None, pos0=0, state_load=ld, state_store=stf))
        if with_sample:
            ld, stf = make_state_fns(lambda l: (sg_in[l], None), lambda l: (sr_in[l], None),
                                     lambda l: (gs[l], None), lambda l: (rs[l], None), True, True)
            run_tile(dict(ns=1, L=32, ntok=32, x=xs, y=ys, pos0=N_META + PAST_LEN, state_load=ld, state_store=stf))
        for q in range(n_seq):
            for k in range(tiles_per_seq):
                first, last = (k == 0), (k == tiles_per_seq - 1)
                if first:
                    sgf = (lambda l: (smeta_g[l], b_smg[l])) if with_meta else None
                    srf = (lambda l: (smeta_r[l], b_smr[l])) if with_meta else None
                else:
                    sgf = lambda l: (scr_g[l], b_scg[l])
                    srf = lambda l: (scr_r[l], b_scr[l])
                if last:
                    dgf = lambda l, q=q: (gp[l, q], None)
                    drf = lambda l, q=q: (rp[l, q], None)
                else:
                    dgf = lambda l: (scr_g[l], b_scg[l])
                    drf = lambda l: (scr_r[l], b_scr[l])
                ld, stf = make_state_fns(sgf, srf, dgf, drf, last, last)
                run_tile(dict(ns=nsub, L=128, ntok=NT, x=xp[q, k * NT:(k + 1) * NT, :], y=yp[q, k * NT:(k + 1) * NT, :],
                              pos0=N_META + k * NT, state_load=ld, state_store=stf))
        T.finish()
        T.replay()
        global LAST_TRACKER
        LAST_TRACKER = T
    return nc


def _const_inputs():
    idf = np.eye(128, dtype=np.float32)
    s_idx = np.arange(128)[:, None]
    t_idx = np.arange(128)[None, :]
    mask = (t_idx >= s_idx).astype(np.float32)
    tri = (s_idx > t_idx).astype(np.float32) / np.float32(16.0)
    half = 64
    inv = (np.float32(10000.0) ** (-np.arange(half, dtype=np.float32) / np.float32(half))).astype(np.float32)
    pos = np.arange(N_META + 2048, dtype=np.float32)
    ang = (pos[:, None] * inv[None, :]).astype(np.float32)
    cos = np.cos(ang).astype(np.float32)
    sin = np.sin(ang).astype(np.float32)
    log_gamma = np.log1p(-(2.0 ** (-5.0 - np.arange(4, dtype=np.float64))))
    c1T = np.ones((8, 176), np.float32)
    c2 = np.zeros((128, 3, 4), np.float32)
    decr = np.zeros((3, 4), np.float32)
    for L in LVARS:
        t = np.arange(L, dtype=np.float64)
        c1T[0:4, LOFF[L]:LOFF[L] + L] = np.exp((t[None, :] - (L - 1.0)) * log_gamma[:, None])
        c2[:L, LIDX[L], :] = np.exp((L - 1.0 - t)[:, None] * log_gamma[None, :]) * (128.0 ** -0.5)
        decr[LIDX[L], :] = np.exp(L * log_gamma)
    return dict(c_idf=idf, c_mask=mask, c_tri=tri, c_cos=cos, c_sin=sin, c_nsin=(-sin).astype(np.float32),
                c_c1T=c1T, c_c2=c2, c_decr=decr)


def run_cores(inputs, n_cores, n_seq, seq_len, depth, nsub, with_meta=True, with_sample=True, trace=False):
    f = lambda a: np.ascontiguousarray(np.asarray(a, dtype=np.float32))
    nc = build_program(n_seq, seq_len, depth, nsub, with_meta, with_sample)
    consts = _const_inputs()
    shared = dict(
        meta=f(inputs["meta"]), ln_g=f(inputs["ln_g"][:depth]), ln_b=f(inputs["ln_b"][:depth]),
        w1u=f(inputs["w_ffn1_up"][:depth]), w1d=f(inputs["w_ffn1_down"][:depth]), w_in=f(inputs["w_in"][:depth]),
        w_a2=f(inputs["w_alpha2"][:depth]), b_a=f(inputs["b_alpha"][:depth]),
        bm_t=f(np.asarray(inputs["b_merge"][:depth]).reshape(depth, 16, 128).transpose(2, 0, 1)),
        gng_t=f(np.asarray(inputs["gn_gla"][:depth]).reshape(depth, 8, 128).transpose(2, 0, 1)),
        gnr_t=f(np.asarray(inputs["gn_ret"][:depth]).reshape(depth, 8, 128).transpose(2, 0, 1)),
        lng_t=f(np.asarray(inputs["ln_g"][:depth]).reshape(depth, 3, 8, 128).transpose(3, 0, 1, 2)),
        lnb_t=f(np.asarray(inputs["ln_b"][:depth]).reshape(depth, 3, 8, 128).transpose(3, 0, 1, 2)),
        w_oa=f(inputs["w_o_gla"][:depth]), w_ob=f(inputs["w_o_ret"][:depth]), w_o=f(inputs["w_out"][:depth]),
        w2u=f(inputs["w_ffn2_up"][:depth]), w2d=f(inputs["w_ffn2_down"][:depth]),
    )
    shared.update(consts)
    xpr = np.asarray(inputs["x_prompt"])
    xsm = np.asarray(inputs["x_sample"])
    sgl = np.asarray(inputs["state_gla"])
    srt = np.asarray(inputs["state_ret"])
    in_maps = []
    for c in range(n_cores):
        m = dict(shared)
        m["xp"] = f(xpr[c * n_seq:(c + 1) * n_seq, :seq_len])
        m["xs"] = f(xsm[c])
        m["sg_in"] = f(sgl[:depth, c])
        m["sr_in"] = f(srt[:depth, c])
        in_maps.append(m)
    res = run_bass_kernel_spmd(nc, in_maps, core_ids=list(range(n_cores)), trace=trace)
    R = res.results
    y_prompt = np.concatenate([r["yp"] for r in R], axis=0)
    y_sample = np.stack([r["ys"] for r in R], axis=0)
    gla_p = np.concatenate([r["gp"] for r in R], axis=1)
    ret_p = np.concatenate([r["rp"] for r in R], axis=1)
    gla_s = np.stack([r["gs"] for r in R], axis=1)
    ret_s = np.stack([r["rs"] for r in R], axis=1)
    return (y_prompt, y_sample, gla_p, ret_p, gla_s, ret_s), res


def kernel(**inputs):
    outs, _ = run_cores(inputs, n_cores=8, n_seq=4, seq_len=2048, depth=DEPTH, nsub=4)
    return tuple(np.ascontiguousarray(o.astype(np.float32, copy=False)) for o in outs)
```

```python
from contextlib import ExitStack
import math
import numpy as np
import concourse.bass as bass
import concourse.mybir as mybir
from concourse.bass_utils import run_bass_kernel_spmd

F32 = mybir.dt.float32
BF16 = mybir.dt.bfloat16
AF = mybir.ActivationFunctionType
ALU = mybir.AluOpType

D = 1024
DFF = 2816
NIN = 8208
DEPTH = 4
N_META = 16
PAST_LEN = 1024
LN_EPS = 1e-5
GN_EPS = 1e-5
DN_ALPHA = (2.0 * DEPTH) ** 0.25
LVARS = (128, 16, 32)
LOFF = {128: 0, 16: 128, 32: 144}
LIDX = {128: 0, 16: 1, 32: 2}
NSLOT = 8
DEBUG_STOP = 99
MIX_STOP = 99
CORE_STOP = 99
NOSTORE = 0
SAME_SYNC = True
PROFILE_LOG = False
LAST_TRACKER = None


class Sem:
    __slots__ = ("h", "val")

    def __init__(self, h):
        self.h = h
        self.val = 0


class Buf:
    __slots__ = ("w", "r", "name", "x")

    def __init__(self, name="", x=False):
        self.w = None
        self.r = {}
        self.name = name
        self.x = x


class Eng:
    def __init__(self, name, sem, same_sync):
        self.name = name
        self.sem = sem
        self.waited = {}
        self.prog = []
        self.same_sync = same_sync


class Tracker:
    def __init__(self, nc, stack):
        self.nc = nc
        self.stack = stack
        self.pe = Eng("pe", self.new_sem("s_pe"), False)
        self.act = Eng("act", self.new_sem("s_act"), SAME_SYNC)
        self.dve = Eng("dve", self.new_sem("s_dve"), SAME_SYNC)
        self.pool = Eng("pool", self.new_sem("s_pool"), SAME_SYNC)
        self.sp = Eng("sp", self.new_sem("s_sp"), False)
        self.out_sems = []
        self.tag = ""
        self.pe_log = []

    def new_sem(self, name):
        return Sem(self.stack.enter_context(self.nc.semaphore(name)))

    def _deps(self, eng, reads, writes):
        deps = {}
        for b in reads:
            if b.w is not None:
                s, v = b.w
                if deps.get(s, 0) < v:
                    deps[s] = v
            if b.x:
                for s, v in b.r.items():
                    if deps.get(s, 0) < v:
                        deps[s] = v
        for b in writes:
            if b.w is not None:
                s, v = b.w
                if deps.get(s, 0) < v:
                    deps[s] = v
            for s, v in b.r.items():
                if deps.get(s, 0) < v:
                    deps[s] = v
        for s, v in deps.items():
            if s is eng.sem and not eng.same_sync:
                continue
            if eng.waited.get(s, 0) < v:
                eng.prog.append((0, s, v))
                eng.waited[s] = v

    def op(self, eng, fn, reads=(), writes=()):
        self._deps(eng, reads, writes)
        s = eng.sem
        s.val += 1
        tok = (s, s.val)
        eng.prog.append((1, fn, s, 1, self.tag))
        for b in writes:
            b.w = tok
            b.r = {}
        for b in reads:
            b.r[s] = s.val
        return tok

    def dma(self, qeng, sem, fn, reads=(), writes=(), n=1, is_out=False):
        self._deps(qeng, reads, writes)
        sem.val += 16 * n
        tok = (sem, sem.val)
        qeng.prog.append((1, fn, sem, 16))
        for b in writes:
            b.w = tok
            b.r = {}
        for b in reads:
            b.r[sem] = sem.val
        if is_out and sem not in self.out_sems:
            self.out_sems.append(sem)
        return tok

    def finish(self):
        for s in self.out_sems:
            self.sp.prog.append((0, s, s.val))

    def replay(self):
        def run(prog, e):
            for it in prog:
                if it[0] == 0:
                    e.wait_ge(it[1].h, it[2])
                else:
                    if PROFILE_LOG and len(it) > 4 and prog is self.pe.prog:
                        n0 = self.nc.n_instructions() if callable(self.nc.n_instructions) else self.nc.n_instructions
                    r = it[1](e)
                    if PROFILE_LOG and len(it) > 4 and prog is self.pe.prog:
                        n1 = self.nc.n_instructions() if callable(self.nc.n_instructions) else self.nc.n_instructions
                        self.pe_log.append((it[4], n1 - n0))
                    if isinstance(r, (list, tuple)):
                        for x in r:
                            x.then_inc(it[2].h, it[3])
                    else:
                        r.then_inc(it[2].h, it[3])

        with self.nc.Block() as block:
            @block.tensor
            def _(e):
                run(self.pe.prog, e)

            @block.scalar
            def _(e):
                run(self.act.prog, e)

            @block.vector
            def _(e):
                run(self.dve.prog, e)

            @block.gpsimd
            def _(e):
                run(self.pool.prog, e)

            @block.sync
            def _(e):
                run(self.sp.prog, e)


def build_program(n_seq, seq_len, depth, nsub, with_meta=True, with_sample=True):
    NT = 128 * nsub
    assert seq_len % NT == 0
    tiles_per_seq = seq_len // NT
    nc = bass.Bass("TRN2", target_bir_lowering=False)

    def din(name, shape):
        return nc.dram_tensor(name, list(shape), F32, kind="ExternalInput").ap()

    def dout(name, shape):
        return nc.dram_tensor(name, list(shape), F32, kind="ExternalOutput").ap()

    def dscr(name, shape):
        return nc.dram_tensor(name, list(shape), F32).ap()

    xp = din("xp", [n_seq, seq_len, D])
    xs = din("xs", [32, D])
    sg_in = din("sg_in", [depth, 4, 128, 256])
    sr_in = din("sr_in", [depth, 4, 128, 256])
    meta = din("meta", [N_META, D])
    ln_g = din("ln_g", [depth, 3, D])
    ln_b = din("ln_b", [depth, 3, D])
    w1u = din("w1u", [depth, D, 2 * DFF])
    w1d = din("w1d", [depth, DFF, D])
    w_in = din("w_in", [depth, D, NIN])
    w_a2 = din("w_a2", [depth, 16, 512])
    b_a = din("b_a", [depth, 512])
    bm_t = din("bm_t", [128, depth, 16])
    gng_t = din("gng_t", [128, depth, 8])
    gnr_t = din("gnr_t", [128, depth, 8])
    w_oa = din("w_oa", [depth, D, D])
    w_ob = din("w_ob", [depth, D, D])
    w_o = din("w_o", [depth, D, D])
    w2u = din("w2u", [depth, D, 2 * DFF])
    w2d = din("w2d", [depth, DFF, D])
    c_idf = din("c_idf", [128, 128])
    c_mask = din("c_mask", [128, 128])
    c_tri = din("c_tri", [128, 128])
    c_cos = din("c_cos", [N_META + 2048, 64])
    c_sin = din("c_sin", [N_META + 2048, 64])
    c_nsin = din("c_nsin", [N_META + 2048, 64])
    c_c1T = din("c_c1T", [8, 176])
    lng_t = din("lng_t", [128, depth, 3, 8])
    lnb_t = din("lnb_t", [128, depth, 3, 8])
    c_c2 = din("c_c2", [128, 3, 4])
    c_decr = din("c_decr", [3, 4])

    yp = dout("yp", [n_seq, seq_len, D])
    ys = dout("ys", [32, D])
    gp = dout("gp", [depth, n_seq, 4, 128, 256])
    rp = dout("rp", [depth, n_seq, 4, 128, 256])
    gs = dout("gs", [depth, 4, 128, 256])
    rs = dout("rs", [depth, 4, 128, 256])

    WSH = {"w1u": (D, 2 * DFF), "w1d": (DFF, D), "w_in": (D, NIN), "w_oa": (D, D), "w_ob": (D, D), "w_o": (D, D),
           "w2u": (D, 2 * DFF), "w2d": (DFF, D)}
    wfp = {"w1u": w1u, "w1d": w1d, "w_in": w_in, "w_oa": w_oa, "w_ob": w_ob, "w_o": w_o, "w2u": w2u, "w2d": w2d}
    wsc = {k: nc.dram_tensor(k + "_bf", [depth, r, c], BF16).ap() for k, (r, c) in WSH.items()}

    def wsrc(name, l):
        return wsc[name][l]

    smeta_g = dscr("smeta_g", [depth, 4, 128, 256])
    smeta_r = dscr("smeta_r", [depth, 4, 128, 256])
    scr_g = dscr("scr_g", [depth, 4, 128, 256])
    scr_r = dscr("scr_r", [depth, 4, 128, 256])

    with ExitStack() as st:
        T = Tracker(nc, st)
        pe, act, dve, pool, sp = T.pe, T.act, T.dve, T.pool, T.sp

        def sb(name, shape, dt=F32):
            return st.enter_context(nc.sbuf_tensor(name, list(shape), dt))

        idf = sb("idf", [128, 128])
        idb = sb("idb", [128, 128], BF16)
        mask = sb("mask", [128, 128])
        tri = sb("tri", [128, 128])
        neg16 = sb("neg16", [128, 1])
        mhalf = sb("mhalf", [128, 8])
        c1T = sb("c1T", [128, 8, 176])
        c2 = sb("c2", [128, 3, 4])
        decr = sb("decr", [128, 3, 4])
        ba_bc = sb("ba_bc", [128, 512])
        wa2 = sb("wa2", [16, 512], BF16)
        G = [sb(f"G{i}", [128, 512]) for i in range(5)]
        bm = sb("bm", [128, depth, 16])
        gng = sb("gng", [128, depth, 8])
        gnr = sb("gnr", [128, depth, 8])
        lng = sb("lng", [128, D])
        lnb = sb("lnb", [128, D])
        cosT = sb("cosT", [128, nsub, 64])
        sinT = sb("sinT", [128, nsub, 64])
        nsinT = sb("nsinT", [128, nsub, 64])
        h = sb("h", [128, nsub, D])
        hT = sb("hT", [128, 8, NT], BF16)
        ring = [sb(f"ring{i}", [128, 4096], BF16) for i in range(NSLOT)]
        satmp = [G[0], G[1]]
        agT = sb("agT", [16, NT], BF16)
        xb, ee, ltok, E1, E2 = G
        decg = sb("decg", [128, nsub, 4])
        qt = sb("qt", [128, nsub, 512], BF16)
        kt = sb("kt", [128, nsub, 512], BF16)
        assert nsub == 4, "buffer aliasing below assumes 512-token tiles"
        vv = sb("vv", [128, nsub, 1024], BF16)
        sgate = sb("sgate", [128, nsub, 1024], BF16)
        vv2 = sb("vv2", [128, nsub, 1024], BF16)
        sgate2 = sb("sgate2", [128, nsub, 1024], BF16)
        gT = [sgate[:, 2 * i:2 * i + 2, :].rearrange("p s (j t) -> p (s j) t", j=2) for i in range(2)]
        mT = vv[:, :, :].rearrange("p s (j t) -> p (s j) t", j=2)
        zc, za, ztmp, rot = G[0:4]
        Sbf = sb("Sbf", [128, 4, 256], BF16)
        attb = sb("attb", [128, 4, 128], BF16)
        on = sb("on", [128, 4, 256])
        yin = sb("yin", [128, 1024], BF16)
        hst = sb("hst", [128, 4, 6])
        hmv = sb("hmv", [128, 4, 2])
        hve = sb("hve", [128, 4])
        hrs = sb("hrs", [128, 4])
        yinTa = sb("yinTa", [128, 8, NT], BF16)
        yinTb = sb("yinTb", [128, 8, NT], BF16)
        sgA, sgB, t1, t2 = G[0:4]
        Sg = sb("Sg", [128, 4, 256])
        Sr = sb("Sr", [128, 4, 256])
        lst = [sb(f"lst{i}", [128, 2, 6]) for i in range(nsub)]
        lmv = [sb(f"lmv{i}", [128, 2]) for i in range(nsub)]
        lve = [sb(f"lve{i}", [128, 1]) for i in range(nsub)]
        lrs = [sb(f"lrs{i}", [128, 1]) for i in range(nsub)]
        lnm = [sb(f"lnm{i}", [128, 1]) for i in range(nsub)]
        hb = [sb(f"hb{i}", [128, D], BF16) for i in range(nsub)]
        lngT = sb("lngT", [128, depth, 3, 8])
        lnbT = sb("lnbT", [128, depth, 3, 8])
        hnm = sb("hnm", [128, 4])
        qkT = sb("qkT", [128, 8, 128], BF16)

        ps = st.enter_context(nc.psum_tensor("ps", [128, 8, 512], F32))
        psb = ps.bitcast(BF16)

        b_const = Buf("const")
        b_w = {k: [Buf(f"w_{k}{l}") for l in range(depth)] for k in WSH}
        b_lp = Buf("lp")
        b_ln = Buf("lnp")
        b_cs = Buf("cossin")
        b_h = [[Buf(f"h{s}{j}") for j in range(2)] for s in range(nsub)]
        b_hT = [[Buf(f"hT{s}_{k}") for k in range(8)] for s in range(nsub)]

        def flat(xs):
            out = []
            for x in xs:
                if isinstance(x, list):
                    out.extend(x)
                else:
                    out.append(x)
            return out
        b_ring = [Buf(f"ring{i}") for i in range(NSLOT)]
        b_G = [Buf(f"G{i}") for i in range(5)]
        b_sa = [b_G[0], b_G[1]]
        b_ps = [Buf(f"ps{i}", x=True) for i in range(8)]
        b_agT = Buf("agT")
        b_xb, b_ee, b_ltok, b_E1, b_E2 = b_G
        b_decg = [Buf(f"decg{s}") for s in range(nsub)]
        b_qt = [Buf(f"qt{s}") for s in range(nsub)]
        b_kt = [Buf(f"kt{s}") for s in range(nsub)]
        b_vv = [[Buf(f"vv{s}{j}") for j in range(2)] for s in range(nsub)]
        b_sg = [[Buf(f"sg{s}{j}") for j in range(2)] for s in range(nsub)]
        b_vv2 = [[Buf(f"vw{s}{j}") for j in range(2)] for s in range(nsub)]
        b_sg2 = [[Buf(f"sh{s}{j}") for j in range(2)] for s in range(nsub)]
        b_gT = [[b_sg[2 * i + c // 2][c % 2] for c in range(4)] for i in range(2)]
        b_zc, b_za, b_ztmp, b_rot = b_G[0:4]
        b_Sbf, b_qTt, b_kTt, b_attb, b_on, b_yin = Buf("Sbf"), Buf("qTt"), Buf("kTt"), Buf("attb"), Buf("on"), Buf("yin")
        b_hst, b_hmv, b_hve, b_hrs = Buf("hst"), Buf("hmv"), Buf("hve"), Buf("hrs")
        b_yTa = [Buf(f"yTa{s}") for s in range(nsub)]
        b_yTb = [Buf(f"yTb{s}") for s in range(nsub)]
        b_mT = [b_vv[c // 2][c % 2] for c in range(8)]
        b_sgA, b_sgB, b_t1, b_t2 = b_G[0:4]
        b_Sg, b_Sr = Buf("Sg"), Buf("Sr")
        b_lst = [Buf(f"lst{i}") for i in range(nsub)]
        b_lmv = [Buf(f"lmv{i}") for i in range(nsub)]
        b_lve = [Buf(f"lve{i}") for i in range(nsub)]
        b_lrs = [Buf(f"lrs{i}") for i in range(nsub)]
        b_lnm = [Buf(f"lnm{i}") for i in range(nsub)]
        b_hb = [Buf(f"hb{i}") for i in range(nsub)]
        b_hnm, b_qkT = Buf("hnm"), Buf("qkT")
        b_smg = [Buf(f"smg{l}") for l in range(depth)]
        b_smr = [Buf(f"smr{l}") for l in range(depth)]
        b_scg = [Buf(f"scg{l}") for l in range(depth)]
        b_scr = [Buf(f"scr{l}") for l in range(depth)]

        s_ring = [T.new_sem(f"d_ring{i}") for i in range(NSLOT)]
        s_const = T.new_sem("d_const")
        s_cv = {k: [T.new_sem(f"d_cv_{k}{l}") for l in range(depth)] for k in WSH}
        s_lp = T.new_sem("d_lp")
        s_x = T.new_sem("d_x")
        s_y = T.new_sem("d_y")
        s_ln = T.new_sem("d_ln")
        s_cs = T.new_sem("d_cs")
        s_stl_g, s_stl_r = T.new_sem("d_stlg"), T.new_sem("d_stlr")
        s_sts_g, s_sts_r = T.new_sem("d_stsg"), T.new_sem("d_stsr")

        pstate = {"i": 0}

        def palloc():
            i = pstate["i"]
            pstate["i"] = (i + 1) % 8
            return i

        rstate = {"i": 0}

        def wload(src, a, b, dep):
            i = rstate["i"]
            rstate["i"] = (i + 1) % NSLOT
            view = ring[i][:, 0:a * b].rearrange("p (a b) -> p a b", a=a)
            T.dma(sp, s_ring[i], lambda e, view=view, src=src: e.dma_start(out=view, in_=src),
                  reads=[dep], writes=[b_ring[i]])
            return view, b_ring[i]

        def load_consts():
            def f(e):
                r = [
                    e.dma_start(out=idf[:], in_=c_idf[:, :]),
                    e.dma_start(out=mask[:], in_=c_mask[:, :]),
                    e.dma_start(out=tri[:], in_=c_tri[:, :]),
                    e.dma_start(out=c1T[:], in_=c_c1T.partition_broadcast(128)),
                    e.dma_start(out=c2[:], in_=c_c2[:, :, :]),
                    e.dma_start(out=decr[:], in_=c_decr.partition_broadcast(128)),
                    e.dma_start(out=bm[:], in_=bm_t[:, :, :]),
                    e.dma_start(out=gng[:], in_=gng_t[:, :, :]),
                    e.dma_start(out=gnr[:], in_=gnr_t[:, :, :]),
                    e.dma_start(out=lngT[:], in_=lng_t[:, :, :, :]),
                    e.dma_start(out=lnbT[:], in_=lnb_t[:, :, :, :]),
                ]
                return r
            T.dma(pool, s_const, f, writes=[b_const], n=11)
            T.op(dve, lambda e: e.tensor_copy(out=idb[:], in_=idf[:]), reads=[b_const], writes=[b_const])
            T.op(dve, lambda e: e.memset(neg16[:], -1.0 / 16.0), writes=[b_const])
            T.op(dve, lambda e: e.memset(mhalf[:], -0.5), writes=[b_const])

        def convert_weights():
            for l in range(depth):
                for name in ("w1u", "w1d", "w_in", "w_oa", "w_ob", "w_o", "w2u", "w2d"):
                    src, dst = wfp[name][l], wsc[name][l]
                    nchunk = WSH[name][0] // 128

                    def f(e, src=src, dst=dst, nchunk=nchunk):
                        return [e.dma_start(out=dst[c * 128:(c + 1) * 128, :], in_=src[c * 128:(c + 1) * 128, :])
                                for c in range(nchunk)]
                    T.dma(pool, s_cv[name][l], f, writes=[b_w[name][l]], n=nchunk)

        def rstd_pow(out_ap, in_ap, nrow, ncol, rb, wb):
            T.op(pool, lambda e: e.tensor_tensor(out=out_ap, in0=in_ap, in1=mhalf[0:nrow, 0:ncol], op=ALU.pow),
                 reads=[rb, b_const], writes=[wb])

        def transpose_to_hT(s, L, aff=None):
            p0, p1 = palloc(), palloc()
            if aff is None:
                def f(e):
                    r = None
                    for k in range(8):
                        bank = p0 if k < 4 else p1
                        r = e.transpose(ps[:, bank, (k % 4) * 128:(k % 4) * 128 + L],
                                        h[0:L, s, k * 128:(k + 1) * 128], idf[0:L, 0:L])
                    return r
                T.op(pe, f, reads=[b_h[s][0], b_h[s][1], b_const], writes=[b_ps[p0], b_ps[p1]])
            else:
                i = s

                def f(e):
                    r = None
                    for k in range(8):
                        bank = p0 if k < 4 else p1
                        r = e.transpose(psb[:, bank, (k % 4) * 128:(k % 4) * 128 + L],
                                        hb[i][0:L, k * 128:(k + 1) * 128], idb[0:L, 0:L])
                    return r
                T.op(pe, f, reads=[b_hb[i], b_const], writes=[b_ps[p0], b_ps[p1]])
            if aff is None:
                T.op(act, lambda e: e.copy(out=hT[:, 0:4, s * 128:s * 128 + L],
                                            in_=ps[:, p0, :].rearrange("p (k t) -> p k t", k=4)[:, :, 0:L]),
                     reads=[b_ps[p0]], writes=b_hT[s][0:4])
                T.op(dve, lambda e: e.tensor_copy(out=hT[:, 4:8, s * 128:s * 128 + L],
                                                   in_=ps[:, p1, :].rearrange("p (k t) -> p k t", k=4)[:, :, 0:L]),
                     reads=[b_ps[p1]], writes=b_hT[s][4:8])
                return
            l, idx = aff
            for k in range(8):
                bank = p0 if k < 4 else p1
                src = psb[:, bank, (k % 4) * 128:(k % 4) * 128 + L]
                dst = hT[:, k, s * 128:s * 128 + L]
                if k < 4:
                    T.op(act, lambda e, src=src, dst=dst, k=k: e.activation(
                        out=dst, in_=src, func=AF.Identity, scale=lngT[:, l, idx, k:k + 1], bias=lnbT[:, l, idx, k:k + 1]),
                        reads=[b_ps[bank], b_const], writes=[b_hT[s][k]])
                else:
                    T.op(dve, lambda e, src=src, dst=dst, k=k: e.tensor_scalar(
                        out=dst, in0=src, scalar1=lngT[:, l, idx, k:k + 1], scalar2=lnbT[:, l, idx, k:k + 1],
                        op0=ALU.mult, op1=ALU.add),
                        reads=[b_ps[bank], b_const], writes=[b_hT[s][k]])

        def load_lp(l):
            T.dma(pool, s_lp, lambda e: [e.dma_start(out=ba_bc[:], in_=b_a[l].partition_broadcast(128)),
                                         e.dma_start(out=wa2[:], in_=w_a2[l])], writes=[b_lp], n=2)

        def load_ln(l, idx):
            def f(e):
                return [e.dma_start(out=lng[:], in_=ln_g[l, idx].partition_broadcast(128)),
                        e.dma_start(out=lnb[:], in_=ln_b[l, idx].partition_broadcast(128))]
            T.dma(pool, s_ln, f, writes=[b_ln], n=2)

        def layer_norm_all(tile, l, idx):
            ns, L = tile["ns"], tile["L"]
            T.tag = "ln"
            for s in range(ns):
                transpose_to_hT(s, L, aff=(l, idx))
            for s in range(ns):
                hs = h[0:L, s, :]
                T.op(pool, lambda e, hs=hs: e.tensor_tensor(out=hs, in0=hs, in1=lng[0:L, :], op=ALU.mult),
                     reads=[b_h[s][0], b_h[s][1], b_ln], writes=[b_h[s][0], b_h[s][1]])
                T.op(pool, lambda e, hs=hs: e.tensor_tensor(out=hs, in0=hs, in1=lnb[0:L, :], op=ALU.add),
                     reads=[b_h[s][0], b_h[s][1], b_ln], writes=[b_h[s][0], b_h[s][1]])

        def ln_front(s, L):
            if True:
                i = s
                hs = h[0:L, s, :]
                T.op(dve, lambda e, i=i, s=s: e.bn_stats(out=lst[i][0:L, 0, :], in_=h[0:L, s, 0:512]), reads=[b_h[s][0]], writes=[b_lst[i]])
                T.op(dve, lambda e, i=i, s=s: e.bn_stats(out=lst[i][0:L, 1, :], in_=h[0:L, s, 512:1024]), reads=[b_h[s][1]], writes=[b_lst[i]])
                T.op(dve, lambda e, i=i: e.bn_aggr(out=lmv[i][0:L, :], in_=lst[i][0:L, :, :]), reads=[b_lst[i]], writes=[b_lmv[i]])
                T.op(dve, lambda e, i=i: e.tensor_scalar_add(out=lve[i][0:L, :], in0=lmv[i][0:L, 1:2], scalar1=LN_EPS),
                     reads=[b_lmv[i]], writes=[b_lve[i]])
                rstd_pow(lrs[i][0:L, :], lve[i][0:L, :], L, 1, b_lve[i], b_lrs[i])
                T.op(dve, lambda e, i=i: e.scalar_tensor_tensor(out=lnm[i][0:L, :], in0=lmv[i][0:L, 0:1], scalar=-1.0, in1=lrs[i][0:L, :],
                                                                 op0=ALU.mult, op1=ALU.mult),
                     reads=[b_lmv[i], b_lrs[i]], writes=[b_lnm[i]])
                T.op(act, lambda e, i=i, hs=hs: e.activation(out=hb[i][0:L, :], in_=hs, func=AF.Identity, scale=lrs[i][0:L, :], bias=lnm[i][0:L, :]),
                     reads=[b_h[s][0], b_h[s][1], b_lrs[i], b_lnm[i]], writes=[b_hb[i]])
                T.op(act, lambda e, i=i, hs=hs: e.activation(out=hs, in_=hs, func=AF.Identity, scale=lrs[i][0:L, :], bias=lnm[i][0:L, :]),
                     reads=[b_h[s][0], b_h[s][1], b_lrs[i], b_lnm[i]], writes=[b_h[s][0], b_h[s][1]])

        def residual_add(s, L, hf, pbank, first):
            hh = h[0:L, s, hf * 512:(hf + 1) * 512]
            if first:
                T.op(dve, lambda e: e.scalar_tensor_tensor(out=hh, in0=hh, scalar=DN_ALPHA, in1=ps[0:L, pbank, :],
                                                            op0=ALU.mult, op1=ALU.add),
                     reads=[b_h[s][hf], b_ps[pbank]], writes=[b_h[s][hf]])
            else:
                T.op(dve, lambda e: e.tensor_tensor(out=hh, in0=hh, in1=ps[0:L, pbank, :], op=ALU.add),
                     reads=[b_h[s][hf], b_ps[pbank]], writes=[b_h[s][hf]])

        def ffn(tile, nu, nd, l):
            T.tag = "ffn"
            ns, L = tile["ns"], tile["L"]
            ntok = tile["ntok"]
            wu_v = wsrc(nu, l).rearrange("(k p) n -> p k n", p=128)
            wd_v = wsrc(nd, l).rearrange("(c p) n -> p c n", p=128)
            du, dd = b_w[nu][l], b_w[nd][l]
            groups = [(0, 512), (512, 512), (1024, 512), (1536, 512), (2048, 512), (2560, 256)]
            for gi, (f0, fw) in enumerate(groups):
                nch = fw // 128
                wa_v, wa_b = wload(wu_v[:, :, f0:f0 + fw], 8, fw, du)
                wb_v, wb_b = wload(wu_v[:, :, DFF + f0:DFF + f0 + fw], 8, fw, du)
                wd_s, wd_b = wload(wd_v[:, f0 // 128:f0 // 128 + nch, :], nch, 1024, dd)
                gb = gi % 2
                for c in range(nch):
                    pa, pb = palloc(), palloc()

                    def fup(e, w_v=wa_v, bank=pa, c=c):
                        r = None
                        for k in range(8):
                            r = e.matmul(ps[:, bank, 0:ntok], lhsT=w_v[:, k, c * 128:(c + 1) * 128],
                                         rhs=hT[:, k, 0:ntok], start=(k == 0), stop=(k == 7))
                        return r
                    T.op(pe, fup, reads=[wa_b] + flat(b_hT[0:ns]), writes=[b_ps[pa]])

                    def fupb(e, w_v=wb_v, bank=pb, c=c):
                        r = None
                        for k in range(8):
                            r = e.matmul(ps[:, bank, 0:ntok], lhsT=w_v[:, k, c * 128:(c + 1) * 128],
                                         rhs=hT[:, k, 0:ntok], start=(k == 0), stop=(k == 7))
                        return r
                    T.op(pe, fupb, reads=[wb_b] + flat(b_hT[0:ns]), writes=[b_ps[pb]])
                    si = c % 2
                    T.op(act, lambda e, pa=pa, si=si: e.activation(out=satmp[si][:, 0:ntok], in_=ps[:, pa, 0:ntok], func=AF.Silu),
                         reads=[b_ps[pa]], writes=[b_sa[si]])
                    T.op(dve, lambda e, pb=pb, si=si, c=c, gb=gb: e.scalar_tensor_tensor(
                        out=gT[gb][:, c, 0:ntok], in0=satmp[si][:, 0:ntok], scalar=0.5, in1=ps[:, pb, 0:ntok],
                        op0=ALU.mult, op1=ALU.mult),
                        reads=[b_sa[si], b_ps[pb]], writes=[b_gT[gb][c]])
                for s in range(ns):
                    for hf in range(2):
                        py = palloc()

                        def fdn(e, s=s, hf=hf, py=py, nch=nch, gb=gb, wd_s=wd_s):
                            r = None
                            for c in range(nch):
                                r = e.matmul(ps[0:L, py, :], lhsT=gT[gb][:, c, s * 128:s * 128 + L],
                                             rhs=wd_s[:, c, hf * 512:(hf + 1) * 512], start=(c == 0), stop=(c == nch - 1))
                            return r
                        T.op(pe, fdn, reads=[wd_b] + b_gT[gb][0:nch], writes=[b_ps[py]])
                        residual_add(s, L, hf, py, first=(gi == 0))
                    if gi == len(groups) - 1:
                        ln_front(s, L)
                        T.tag = "ffn"

        def core_A(s, L, dec_ap, dec_buf, S, b_S, is_ret, vt, b_vt):
            pt, pa, pk0, pk1 = 0, 1, 2, 3
            po0, po1 = (4, 5) if s % 2 == 0 else (6, 7)
            T.op(pool, lambda e: e.tensor_tensor(out=Sbf[:], in0=S[:], in1=dec_ap.rearrange("p (h o) -> p h o", o=1).to_broadcast([128, 4, 256]),
                                                  op=ALU.mult),
                 reads=[b_S, dec_buf], writes=[b_Sbf])

            def ftr(e):
                r = None
                for hh in range(4):
                    r = e.transpose(psb[:, pt, hh * 128:hh * 128 + L], qt[0:L, s, hh * 128:(hh + 1) * 128], idb[0:L, 0:L])
                for hh in range(4):
                    r = e.transpose(psb[:, pt, 512 + hh * 128:512 + hh * 128 + L], kt[0:L, s, hh * 128:(hh + 1) * 128], idb[0:L, 0:L])
                return r
            T.op(pe, ftr, reads=[b_qt[s], b_kt[s], b_const], writes=[b_ps[pt]])
            src = psb[:, pt, :].rearrange("p (g t) -> p g t", g=8)[:, :, 0:L]
            if not is_ret:
                T.op(act, lambda e: e.copy(out=qkT[:, :, 0:L], in_=src), reads=[b_ps[pt]], writes=[b_qkT])
            else:
                T.op(dve, lambda e: e.tensor_tensor(out=qkT[:, :, 0:L], in0=src, in1=c1T[:, :, LOFF[L]:LOFF[L] + L], op=ALU.mult),
                     reads=[b_ps[pt], b_const], writes=[b_qkT])

            def fatt(e):
                r = None
                for hh in range(4):
                    r = e.matmul(ps[0:L, pa, hh * 128:hh * 128 + L], lhsT=qkT[:, 4 + hh, 0:L], rhs=qkT[:, hh, 0:L],
                                 start=True, stop=True)
                return r
            T.op(pe, fatt, reads=[b_qkT], writes=[b_ps[pa]])
            T.op(dve, lambda e: e.tensor_tensor(
                out=attb[0:L, :, 0:L], in0=ps[0:L, pa, :].rearrange("p (h t) -> p h t", h=4)[:, :, 0:L],
                in1=mask[0:L, 0:L].rearrange("p (o t) -> p o t", o=1).to_broadcast([L, 4, L]), op=ALU.mult),
                reads=[b_ps[pa], b_const], writes=[b_attb])

            def fo(e):
                r = None
                for hh in range(4):
                    bank = po0 if hh < 2 else po1
                    oo = ps[0:L, bank, (hh % 2) * 256:(hh % 2 + 1) * 256]
                    e.matmul(oo, lhsT=attb[0:L, hh, 0:L], rhs=vt[0:L, s, hh * 256:(hh + 1) * 256], start=True, stop=False)
                    r = e.matmul(oo, lhsT=qkT[:, hh, 0:L], rhs=Sbf[:, hh, :], start=False, stop=True)
                return r
            T.op(pe, fo, reads=[b_attb, b_vt[s][0], b_vt[s][1], b_qkT, b_Sbf], writes=[b_ps[po0], b_ps[po1]])

            def fkv(e):
                r = None
                for hh in range(4):
                    bank = pk0 if hh < 2 else pk1
                    r = e.matmul(ps[:, bank, (hh % 2) * 256:(hh % 2 + 1) * 256], lhsT=kt[0:L, s, hh * 128:(hh + 1) * 128],
                                 rhs=vt[0:L, s, hh * 256:(hh + 1) * 256], start=True, stop=True)
                return r
            T.op(pe, fkv, reads=[b_kt[s], b_vt[s][0], b_vt[s][1]], writes=[b_ps[pk0], b_ps[pk1]])

        def core_A2(s, dec_ap, dec_buf, S, b_S):
            pk0, pk1 = 2, 3
            for hh in range(4):
                bank = pk0 if hh < 2 else pk1
                T.op(dve, lambda e, hh=hh, bank=bank: e.scalar_tensor_tensor(
                    out=S[:, hh, :], in0=S[:, hh, :], scalar=dec_ap[:, hh:hh + 1],
                    in1=ps[:, bank, (hh % 2) * 256:(hh % 2 + 1) * 256], op0=ALU.mult, op1=ALU.add),
                    reads=[b_S, dec_buf, b_ps[bank]], writes=[b_S])

        def core_B(s, L, gt, b_gt):
            po0, po1 = (4, 5) if s % 2 == 0 else (6, 7)
            for hh in range(4):
                bank = po0 if hh < 2 else po1
                T.op(dve, lambda e, hh=hh, bank=bank: e.bn_stats(out=hst[0:L, hh, :], in_=ps[0:L, bank, (hh % 2) * 256:(hh % 2 + 1) * 256]),
                     reads=[b_ps[bank]], writes=[b_hst])
            for hh in range(4):
                T.op(dve, lambda e, hh=hh: e.bn_aggr(out=hmv[0:L, hh, :], in_=hst[0:L, hh:hh + 1, :]),
                     reads=[b_hst], writes=[b_hmv])
            T.op(dve, lambda e: e.tensor_scalar_add(out=hve[0:L, :], in0=hmv[0:L, :, 1], scalar1=GN_EPS),
                 reads=[b_hmv], writes=[b_hve])
            rstd_pow(hrs[0:L, :], hve[0:L, :], L, 4, b_hve, b_hrs)
            T.op(act, lambda e: e.activation(out=on[0:L, 0, :], in_=gt[0:L, s, 0:256], func=AF.Identity, scale=hrs[0:L, 0:1]),
                 reads=[b_gt[s][0], b_hrs], writes=[b_on])
            for hh in range(1, 4):
                T.op(act, lambda e, hh=hh: e.activation(out=on[0:L, hh, :], in_=gt[0:L, s, hh * 256:(hh + 1) * 256], func=AF.Identity,
                                                         scale=hrs[0:L, hh:hh + 1]),
                     reads=[b_gt[s][hh // 2], b_hrs, b_on], writes=[b_on])
            for hh in range(4):
                bank = po0 if hh < 2 else po1
                T.op(dve, lambda e, hh=hh, bank=bank: e.scalar_tensor_tensor(
                    out=yin[0:L, hh * 256:(hh + 1) * 256], in0=ps[0:L, bank, (hh % 2) * 256:(hh % 2 + 1) * 256],
                    scalar=hmv[0:L, hh, 0:1], in1=on[0:L, hh, :], op0=ALU.subtract, op1=ALU.mult),
                    reads=[b_ps[bank], b_hmv, b_on, b_yin] if hh else [b_ps[bank], b_hmv, b_on], writes=[b_yin])

        def core_B2(s, L, gn_ap, yT, b_yT):
            py = 2

            def fty(e):
                r = None
                for c in range(8):
                    r = e.transpose(psb[:, py, c * 128:c * 128 + L], yin[0:L, c * 128:(c + 1) * 128], idb[0:L, 0:L])
                return r
            T.op(pe, fty, reads=[b_yin, b_const], writes=[b_ps[py]])
            T.op(dve, lambda e: e.tensor_tensor(
                out=yT[:, :, s * 128:s * 128 + L], in0=psb[:, py, :].rearrange("p (c t) -> p c t", c=8)[:, :, 0:L],
                in1=gn_ap.rearrange("p (c o) -> p c o", o=1).to_broadcast([128, 8, L]), op=ALU.mult),
                reads=[b_ps[py], b_const], writes=[b_yT[s]])

        def mixer_cores(ns, L, dec_fn, S, b_S, is_ret, gn_ap, yT, b_yT, vt, b_vt, gt, b_gt, fillers=()):
            fillers = list(fillers)
            for s in range(ns):
                dec_ap, dec_buf = dec_fn(s)
                core_A(s, L, dec_ap, dec_buf, S, b_S, is_ret, vt, b_vt)
                if s >= 1:
                    core_B(s - 1, L, gt, b_gt)
                core_A2(s, dec_ap, dec_buf, S, b_S)
                if fillers:
                    fillers.pop(0)()
                if s >= 1:
                    core_B2(s - 1, L, gn_ap, yT, b_yT)
            core_B(ns - 1, L, gt, b_gt)
            while fillers:
                fillers.pop(0)()
            core_B2(ns - 1, L, gn_ap, yT, b_yT)

        def proj_tok(s, L, w_v, w_b, bank=None):
            pz = palloc() if bank is None else bank

            def f(e):
                r = None
                for k in range(8):
                    r = e.matmul(ps[0:L, pz, :], lhsT=hT[:, k, s * 128:s * 128 + L], rhs=w_v[:, k, :],
                                 start=(k == 0), stop=(k == 7))
                return r
            T.op(pe, f, reads=[w_b] + b_hT[s], writes=[b_ps[pz]])
            return pz

        def rotary(s, L, pz, out_ap, out_bufs):
            T.op(act, lambda e: e.copy(out=zc[0:L, :], in_=ps[0:L, pz, :]), reads=[b_ps[pz]], writes=[b_zc])
            z4 = zc[0:L, :].rearrange("p (h w j) -> p h w j", h=4, w=2)
            a4 = za[0:L, :].rearrange("p (h w j) -> p h w j", h=4, w=2)
            t4 = ztmp[0:L, :].rearrange("p (h w j) -> p h w j", h=4, w=2)
            cos_b = cosT[0:L, s, :].rearrange("p (o j) -> p o j", o=1).to_broadcast([L, 8, 64])
            sin_b = sinT[0:L, s, :].rearrange("p (o j) -> p o j", o=1).to_broadcast([L, 4, 64])
            nsin_b = nsinT[0:L, s, :].rearrange("p (o j) -> p o j", o=1).to_broadcast([L, 4, 64])
            T.op(dve, lambda e: e.tensor_tensor(out=za[0:L, :].rearrange("p (g j) -> p g j", g=8),
                                                 in0=zc[0:L, :].rearrange("p (g j) -> p g j", g=8), in1=cos_b, op=ALU.mult),
                 reads=[b_zc, b_cs], writes=[b_za])
            T.op(pool, lambda e: e.tensor_tensor(out=t4[:, :, 0, :], in0=z4[:, :, 1, :], in1=nsin_b, op=ALU.mult),
                 reads=[b_zc, b_cs], writes=[b_ztmp])
            T.op(pool, lambda e: e.tensor_tensor(out=t4[:, :, 1, :], in0=z4[:, :, 0, :], in1=sin_b, op=ALU.mult),
                 reads=[b_zc, b_cs], writes=[b_ztmp])
            T.op(dve, lambda e: e.tensor_tensor(out=out_ap, in0=za[0:L, :], in1=ztmp[0:L, :], op=ALU.add),
                 reads=[b_za, b_ztmp], writes=out_bufs)

        def mixers(tile, l):
            ns, L, ntok = tile["ns"], tile["L"], tile["ntok"]
            T.tag = "mix.gla_qk"
            li = LIDX[L]
            W = wsrc("w_in", l).rearrange("(k p) n -> p k n", p=128)
            wdeps = [b_w["w_in"][l]]
            wag_v, wag_b = wload(W[:, :, 3072:3088], 8, 16, wdeps[0])
            pg = palloc()

            def fag(e):
                r = None
                for k in range(8):
                    r = e.matmul(ps[0:16, pg, 0:ntok], lhsT=wag_v[:, k, :], rhs=hT[:, k, 0:ntok], start=(k == 0), stop=(k == 7))
                return r
            T.op(pe, fag, reads=[wag_b] + flat(b_hT[0:ns]), writes=[b_ps[pg]])
            T.op(act, lambda e: e.copy(out=agT[:, 0:ntok], in_=ps[0:16, pg, 0:ntok]), reads=[b_ps[pg]], writes=[b_agT])
            if MIX_STOP < 2:
                return
            wq_v, wq_b = wload(W[:, :, 0:512], 8, 512, wdeps[0])
            wk_v, wk_b = wload(W[:, :, 512:1024], 8, 512, wdeps[0])
            for s in range(ns):
                px = palloc()
                T.op(pe, lambda e, s=s, px=px: e.matmul(ps[0:L, px, :], lhsT=agT[:, s * 128:s * 128 + L], rhs=wa2[:, :],
                                                         start=True, stop=True),
                     reads=[b_agT, b_lp], writes=[b_ps[px]])
                T.op(dve, lambda e, px=px: e.tensor_tensor(out=xb[0:L, :], in0=ps[0:L, px, :], in1=ba_bc[0:L, :], op=ALU.add),
                     reads=[b_ps[px], b_lp], writes=[b_xb])
                T.op(act, lambda e: e.activation(out=ee[0:L, :], in_=xb[0:L, :], func=AF.Exp, scale=-1.0),
                     reads=[b_xb], writes=[b_ee])
                T.op(act, lambda e: e.activation(out=ltok[0:L, :], in_=ee[0:L, :], func=AF.Ln, bias=1.0),
                     reads=[b_ee], writes=[b_ltok])
                pd, pbl = palloc(), palloc()
                T.op(pe, lambda e, pd=pd: e.matmul(ps[0:L, pd, :], lhsT=tri[0:L, 0:L], rhs=ltok[0:L, :], start=True, stop=True),
                     reads=[b_ltok, b_const], writes=[b_ps[pd]])

                def fbl(e, pbl=pbl):
                    r = None
                    for hh in range(4):
                        r = e.matmul(ps[:, pbl, hh:hh + 1], lhsT=ltok[0:L, hh * 128:(hh + 1) * 128], rhs=neg16[0:L, 0:1],
                                     start=True, stop=True)
                    return r
                T.op(pe, fbl, reads=[b_ltok, b_const], writes=[b_ps[pbl]])
                T.op(act, lambda e, pd=pd: e.activation(out=E1[0:L, :], in_=ps[0:L, pd, :], func=AF.Exp,
                                                         bias=float(math.log(128.0 ** -0.5)), scale=1.0),
                     reads=[b_ps[pd]], writes=[b_E1])
                T.op(act, lambda e, pd=pd: e.activation(out=E2[0:L, :], in_=ps[0:L, pd, :], func=AF.Exp, scale=-1.0),
                     reads=[b_ps[pd]], writes=[b_E2])
                T.op(act, lambda e, pbl=pbl, s=s: e.activation(out=decg[:, s, :], in_=ps[:, pbl, 0:4], func=AF.Exp),
                     reads=[b_ps[pbl]], writes=[b_decg[s]])
                pq = proj_tok(s, L, wq_v, wq_b)
                T.op(dve, lambda e, pq=pq, s=s: e.tensor_tensor(out=qt[0:L, s, :], in0=ps[0:L, pq, :], in1=E1[0:L, :], op=ALU.mult),
                     reads=[b_ps[pq], b_E1], writes=[b_qt[s]])
                pk = proj_tok(s, L, wk_v, wk_b)
                T.op(dve, lambda e, pk=pk, s=s: e.tensor_tensor(out=kt[0:L, s, :], in0=ps[0:L, pk, :], in1=E2[0:L, :], op=ALU.mult),
                     reads=[b_ps[pk], b_E2], writes=[b_kt[s]])

            def v_piece(c0, j, vt, b_vt, banks=None):
                w_v, w_b = wload(W[:, :, c0 + 512 * j:c0 + 512 * (j + 1)], 8, 512, wdeps[0])
                for s in range(ns):
                    pz = proj_tok(s, L, w_v, w_b, None if banks is None else banks[s % 2])
                    T.op(act, lambda e, pz=pz, s=s: e.copy(out=vt[0:L, s, 512 * j:512 * (j + 1)], in_=ps[0:L, pz, :]),
                         reads=[b_ps[pz]], writes=[b_vt[s][j]])

            def g_piece(c0, j, gt, b_gt, banks=None):
                w_v, w_b = wload(W[:, :, c0 + 512 * j:c0 + 512 * (j + 1)], 8, 512, wdeps[0])
                for s in range(ns):
                    pz = proj_tok(s, L, w_v, w_b, None if banks is None else banks[s % 2])
                    T.op(act, lambda e, pz=pz, s=s: e.activation(out=gt[0:L, s, 512 * j:512 * (j + 1)], in_=ps[0:L, pz, :],
                                                                  func=AF.Silu),
                         reads=[b_ps[pz]], writes=[b_gt[s][j]])

            def vg_pieces(c0v, c0g, vt, b_vt, gt, b_gt):
                for j in range(2):
                    v_piece(c0v, j, vt, b_vt)
                for j in range(2):
                    g_piece(c0g, j, gt, b_gt)

            if MIX_STOP < 3:
                return
            T.tag = "mix.gla_vg"
            vg_pieces(1024, 2048, vv, b_vv, sgate, b_sg)
            T.tag = "mix.gla_core"
            if MIX_STOP < 4:
                return
            tile["state_load"](l, "g")
            fill = [lambda j=j: v_piece(4112, j, vv2, b_vv2, banks=(0, 1)) for j in range(2)] + \
                   [lambda j=j: g_piece(5136, j, sgate2, b_sg2, banks=(0, 1)) for j in range(2)]
            mixer_cores(ns, L, lambda s: (decg[:, s, :], b_decg[s]), Sg, b_Sg, False, gng[:, l, :], yinTa, b_yTa,
                        vv, b_vv, sgate, b_sg, fillers=fill)
            tile["state_store"](l, "g")
            if MIX_STOP < 5:
                return
            T.tag = "mix.ret_qk"
            wq_v, wq_b = wload(W[:, :, 3088:3600], 8, 512, wdeps[0])
            wk_v, wk_b = wload(W[:, :, 3600:4112], 8, 512, wdeps[0])
            for s in range(ns):
                pq = proj_tok(s, L, wq_v, wq_b)
                rotary(s, L, pq, qt[0:L, s, :], [b_qt[s]])
                pk = proj_tok(s, L, wk_v, wk_b)
                rotary(s, L, pk, rot[0:L, :], [b_rot])
                T.op(dve, lambda e, s=s: e.tensor_tensor(
                    out=kt[0:L, s, :].rearrange("p (h d) -> p h d", h=4), in0=rot[0:L, :].rearrange("p (h d) -> p h d", h=4),
                    in1=c2[0:L, li, :].rearrange("p (h o) -> p h o", o=1).to_broadcast([L, 4, 128]), op=ALU.mult),
                    reads=[b_rot, b_const], writes=[b_kt[s]])
            if MIX_STOP < 6:
                return
            T.tag = "mix.ret_core"
            tile["state_load"](l, "r")
            mixer_cores(ns, L, lambda s: (decr[:, li, :], b_const), Sr, b_Sr, True, gnr[:, l, :], yinTb, b_yTb,
                        vv2, b_vv2, sgate2, b_sg2)
            tile["state_store"](l, "r")
            if MIX_STOP < 7:
                return
            T.tag = "mix.merge"
            woa = wsrc("w_oa", l).rearrange("(k p) n -> p k n", p=128)
            wob = wsrc("w_ob", l).rearrange("(k p) n -> p k n", p=128)
            for cg in range(2):
                a_v, a_b = wload(woa[:, :, 512 * cg:512 * (cg + 1)], 8, 512, b_w["w_oa"][l])
                b_v, b_b = wload(wob[:, :, 512 * cg:512 * (cg + 1)], 8, 512, b_w["w_ob"][l])
                ma_v, ma_b = wload(W[:, :, 6160 + 512 * cg:6160 + 512 * (cg + 1)], 8, 512, wdeps[0])
                mb_v, mb_b = wload(W[:, :, 7184 + 512 * cg:7184 + 512 * (cg + 1)], 8, 512, wdeps[0])
                for c4 in range(4):
                    c = 4 * cg + c4
                    banks = []
                    for (w_v, w_b, src, srcb) in ((a_v, a_b, yinTa, b_yTa), (b_v, b_b, yinTb, b_yTb),
                                                  (ma_v, ma_b, hT, b_hT), (mb_v, mb_b, hT, b_hT)):
                        pz = palloc()

                        def f(e, w_v=w_v, src=src, pz=pz, c4=c4):
                            r = None
                            for k in range(8):
                                r = e.matmul(ps[:, pz, 0:ntok], lhsT=w_v[:, k, c4 * 128:(c4 + 1) * 128], rhs=src[:, k, 0:ntok],
                                             start=(k == 0), stop=(k == 7))
                            return r
                        T.op(pe, f, reads=[w_b] + flat(srcb[0:ns]), writes=[b_ps[pz]])
                        banks.append(pz)
                    pya, pyb, pma, pmb = banks
                    T.op(act, lambda e, pma=pma, c=c: e.activation(out=sgA[:, 0:ntok], in_=ps[:, pma, 0:ntok], func=AF.Sigmoid,
                                                                    bias=bm[:, l, c:c + 1], scale=1.0),
                         reads=[b_ps[pma], b_const], writes=[b_sgA])
                    T.op(act, lambda e, pmb=pmb, c=c: e.activation(out=sgB[:, 0:ntok], in_=ps[:, pmb, 0:ntok], func=AF.Sigmoid,
                                                                    bias=bm[:, l, 8 + c:9 + c], scale=1.0),
                         reads=[b_ps[pmb], b_const], writes=[b_sgB])
                    T.op(dve, lambda e, pya=pya: e.tensor_tensor(out=t1[:, 0:ntok], in0=sgA[:, 0:ntok], in1=ps[:, pya, 0:ntok], op=ALU.mult),
                         reads=[b_sgA, b_ps[pya]], writes=[b_t1])
                    T.op(dve, lambda e, pyb=pyb: e.tensor_tensor(out=t2[:, 0:ntok], in0=sgB[:, 0:ntok], in1=ps[:, pyb, 0:ntok], op=ALU.mult),
                         reads=[b_sgB, b_ps[pyb]], writes=[b_t2])
                    T.op(dve, lambda e, c=c: e.tensor_tensor(out=mT[:, c, 0:ntok], in0=t1[:, 0:ntok], in1=t2[:, 0:ntok], op=ALU.add),
                         reads=[b_t1, b_t2], writes=[b_mT[c]])
            if MIX_STOP < 8:
                return
            T.tag = "mix.outproj"
            wo = wsrc("w_o", l).rearrange("(k p) n -> p k n", p=128)
            wo_p = [wload(wo[:, :, 512 * hf:512 * (hf + 1)], 8, 512, b_w["w_o"][l]) for hf in range(2)]
            for s in range(ns):
                for hf in range(2):
                    po = palloc()

                    def f(e, s=s, hf=hf, po=po):
                        r = None
                        for k in range(8):
                            r = e.matmul(ps[0:L, po, :], lhsT=mT[:, k, s * 128:s * 128 + L], rhs=wo_p[hf][0][:, k, :],
                                         start=(k == 0), stop=(k == 7))
                        return r
                    T.op(pe, f, reads=[wo_p[hf][1]] + b_mT, writes=[b_ps[po]])
                    residual_add(s, L, hf, po, first=True)
                ln_front(s, L)

        def run_tile(tile):
            ns, L = tile["ns"], tile["L"]
            T.dma(pool, s_x, lambda e: e.dma_start(out=h[0:L, 0:ns, :], in_=tile["x"].rearrange("(s p) d -> p s d", p=L)),
                  writes=[b for s in range(ns) for b in b_h[s]])
            pos0 = tile["pos0"]

            def fcs(e):
                return [e.dma_start(out=dst[0:L, 0:ns, :], in_=src[pos0:pos0 + ns * L, :].rearrange("(s p) j -> p s j", p=L))
                        for dst, src in ((cosT, c_cos), (sinT, c_sin), (nsinT, c_nsin))]
            T.dma(pool, s_cs, fcs, writes=[b_cs], n=3)
            if DEBUG_STOP >= 1:
                for s in range(ns):
                    transpose_to_hT(s, L)
            for l in range(depth):
                if DEBUG_STOP < 2:
                    break
                load_lp(l)
                load_ln(l, 0)
                ffn(tile, "w1u", "w1d", l)
                if DEBUG_STOP < 3:
                    break
                layer_norm_all(tile, l, 0)
                if DEBUG_STOP < 4:
                    break
                load_ln(l, 1)
                mixers(tile, l)
                if DEBUG_STOP < 5:
                    break
                layer_norm_all(tile, l, 1)
                load_ln(l, 2)
                ffn(tile, "w2u", "w2d", l)
                layer_norm_all(tile, l, 2)
            if tile["y"] is not None:
                T.dma(pool, s_y, lambda e: e.dma_start(out=tile["y"].rearrange("(s p) d -> p s d", p=L), in_=h[0:L, 0:ns, :]),
                      reads=[b for s in range(ns) for b in b_h[s]], is_out=True)

        def make_state_fns(src_g, src_r, dst_g, dst_r, out_g, out_r):
            def load(l, which):
                S, b_S, sem = (Sg, b_Sg, s_stl_g) if which == "g" else (Sr, b_Sr, s_stl_r)
                src = (src_g if which == "g" else src_r)
                if src is None:
                    T.op(dve, lambda e: e.memset(S[:].rearrange("p h e -> p (h e)"), 0.0), writes=[b_S])
                    return
                ap, bb = src(l)
                T.dma(pool, sem, lambda e: e.dma_start(out=S[:], in_=ap.rearrange("h d e -> d h e")),
                      reads=([bb] if bb is not None else []), writes=[b_S])

            def store(l, which):
                if NOSTORE:
                    return
                S, b_S, sem = (Sg, b_Sg, s_sts_g) if which == "g" else (Sr, b_Sr, s_sts_r)
                ap, bb = (dst_g if which == "g" else dst_r)(l)
                is_out = out_g if which == "g" else out_r
                T.dma(pool, sem, lambda e: e.dma_start(out=ap.rearrange("h d e -> d h e"), in_=S[:]),
                      reads=[b_S], writes=([bb] if bb is not None else []), is_out=is_out)
            return load, store

        load_consts()
        convert_weights()
        if with_meta:
            ld, stf = make_state_fns(None, None, lambda l: (smeta_g[l], b_smg[l]), lambda l: (smeta_r[l], b_smr[l]), False, False)
            run_tile(dict(ns=1, L=16, ntok=16, x=meta, y=None, pos0=0, state_load=ld, state_store=stf))
        if with_sample:
            ld, stf = make_state_fns(lambda l: (sg_in[l], None), lambda l: (sr_in[l], None),
                                     lambda l: (gs[l], None), lambda l: (rs[l], None), True, True)
            run_tile(dict(ns=1, L=32, ntok=32, x=xs, y=ys, pos0=N_META + PAST_LEN, state_load=ld, state_store=stf))
        for q in range(n_seq):
            for k in range(tiles_per_seq):
                first, last = (k == 0), (k == tiles_per_seq - 1)
                if first:
                    sgf = (lambda l: (smeta_g[l], b_smg[l])) if with_meta else None
                    srf = (lambda l: (smeta_r[l], b_smr[l])) if with_meta else None
                else:
                    sgf = lambda l: (scr_g[l], b_scg[l])
                    srf = lambda l: (scr_r[l], b_scr[l])
                if last:
                    dgf = lambda l, q=q: (gp[l, q], None)
                    drf = lambda l, q=q: (rp[l, q], None)
                else:
                    dgf = lambda l: (scr_g[l], b_scg[l])
                    drf = lambda l: (scr_r[l], b_scr[l])
                ld, stf = make_state_fns(sgf, srf, dgf, drf, last, last)
                run_tile(dict(ns=nsub, L=128, ntok=NT, x=xp[q, k * NT:(k + 1) * NT, :], y=yp[q, k * NT:(k + 1) * NT, :],
                              pos0=N_META + k * NT, state_load=ld, state_store=stf))
        T.finish()
        T.replay()
        global LAST_TRACKER
        LAST_TRACKER = T
    return nc


def _const_inputs():
    idf = np.eye(128, dtype=np.float32)
    s_idx = np.arange(128)[:, None]
    t_idx = np.arange(128)[None, :]
    mask = (t_idx >= s_idx).astype(np.float32)
    tri = (s_idx > t_idx).astype(np.float32) / np.float32(16.0)
    half = 64
    inv = (np.float32(10000.0) ** (-np.arange(half, dtype=np.float32) / np.float32(half))).astype(np.float32)
    pos = np.arange(N_META + 2048, dtype=np.float32)
    ang = (pos[:, None] * inv[None, :]).astype(np.float32)
    cos = np.cos(ang).astype(np.float32)
    sin = np.sin(ang).astype(np.float32)
    log_gamma = np.log1p(-(2.0 ** (-5.0 - np.arange(4, dtype=np.float64))))
    c1T = np.ones((8, 176), np.float32)
    c2 = np.zeros((128, 3, 4), np.float32)
    decr = np.zeros((3, 4), np.float32)
    for L in LVARS:
        t = np.arange(L, dtype=np.float64)
        c1T[0:4, LOFF[L]:LOFF[L] + L] = np.exp((t[None, :] - (L - 1.0)) * log_gamma[:, None])
        c2[:L, LIDX[L], :] = np.exp((L - 1.0 - t)[:, None] * log_gamma[None, :]) * (128.0 ** -0.5)
        decr[LIDX[L], :] = np.exp(L * log_gamma)
    return dict(c_idf=idf, c_mask=mask, c_tri=tri, c_cos=cos, c_sin=sin, c_nsin=(-sin).astype(np.float32),
                c_c1T=c1T, c_c2=c2, c_decr=decr)


def run_cores(inputs, n_cores, n_seq, seq_len, depth, nsub, with_meta=True, with_sample=True, trace=False):
    f = lambda a: np.ascontiguousarray(np.asarray(a, dtype=np.float32))
    nc = build_program(n_seq, seq_len, depth, nsub, with_meta, with_sample)
    consts = _const_inputs()
    shared = dict(
        meta=f(inputs["meta"]), ln_g=f(inputs["ln_g"][:depth]), ln_b=f(inputs["ln_b"][:depth]),
        w1u=f(inputs["w_ffn1_up"][:depth]), w1d=f(inputs["w_ffn1_down"][:depth]), w_in=f(inputs["w_in"][:depth]),
        w_a2=f(inputs["w_alpha2"][:depth]), b_a=f(inputs["b_alpha"][:depth]),
        bm_t=f(np.asarray(inputs["b_merge"][:depth]).reshape(depth, 16, 128).transpose(2, 0, 1)),
        gng_t=f(np.asarray(inputs["gn_gla"][:depth]).reshape(depth, 8, 128).transpose(2, 0, 1)),
        gnr_t=f(np.asarray(inputs["gn_ret"][:depth]).reshape(depth, 8, 128).transpose(2, 0, 1)),
        lng_t=f(np.asarray(inputs["ln_g"][:depth]).reshape(depth, 3, 8, 128).transpose(3, 0, 1, 2)),
        lnb_t=f(np.asarray(inputs["ln_b"][:depth]).reshape(depth, 3, 8, 128).transpose(3, 0, 1, 2)),
        w_oa=f(inputs["w_o_gla"][:depth]), w_ob=f(inputs["w_o_ret"][:depth]), w_o=f(inputs["w_out"][:depth]),
        w2u=f(inputs["w_ffn2_up"][:depth]), w2d=f(inputs["w_ffn2_down"][:depth]),
    )
    shared.update(consts)
    xpr = np.asarray(inputs["x_prompt"])
    xsm = np.asarray(inputs["x_sample"])
    sgl = np.asarray(inputs["state_gla"])
    srt = np.asarray(inputs["state_ret"])
    in_maps = []
    for c in range(n_cores):
        m = dict(shared)
        m["xp"] = f(xpr[c * n_seq:(c + 1) * n_seq, :seq_len])
        m["xs"] = f(xsm[c])
        m["sg_in"] = f(sgl[:depth, c])
        m["sr_in"] = f(srt[:depth, c])
        in_maps.append(m)
    res = run_bass_kernel_spmd(nc, in_maps, core_ids=list(range(n_cores)), trace=trace)
    R = res.results
    y_prompt = np.concatenate([r["yp"] for r in R], axis=0)
    y_sample = np.stack([r["ys"] for r in R], axis=0)
    gla_p = np.concatenate([r["gp"] for r in R], axis=1)
    ret_p = np.concatenate([r["rp"] for r in R], axis=1)
    gla_s = np.stack([r["gs"] for r in R], axis=1)
    ret_s = np.stack([r["rs"] for r in R], axis=1)
    return (y_prompt, y_sample, gla_p, ret_p, gla_s, ret_s), res


def kernel(**inputs):
    outs, _ = run_cores(inputs, n_cores=8, n_seq=4, seq_len=2048, depth=DEPTH, nsub=4)
    return tuple(np.ascontiguousarray(o.astype(np.float32, copy=False)) for o in outs)
```

```python
from contextlib import ExitStack
import math
import numpy as np
import concourse.bass as bass
import concourse.mybir as mybir
from concourse.bass_utils import run_bass_kernel_spmd

F32 = mybir.dt.float32
BF16 = mybir.dt.bfloat16
AF = mybir.ActivationFunctionType
ALU = mybir.AluOpType

D = 1024
DFF = 2816
NIN = 8208
DEPTH = 4
N_META = 16
PAST_LEN = 1024
LN_EPS = 1e-5
GN_EPS = 1e-5
DN_ALPHA = (2.0 * DEPTH) ** 0.25
LVARS = (128, 16, 32)
LOFF = {128: 0, 16: 128, 32: 144}
LIDX = {128: 0, 16: 1, 32: 2}
NSLOT = 8
DEBUG_STOP = 99
MIX_STOP = 99
CORE_STOP = 99
NOSTORE = 0
SAME_SYNC = True
PROFILE_LOG = False
LAST_TRACKER = None


class Sem:
    __slots__ = ("h", "val")

    def __init__(self, h):
        self.h = h
        self.val = 0


class Buf:
    __slots__ = ("w", "r", "name", "x")

    def __init__(self, name="", x=False):
        self.w = None
        self.r = {}
        self.name = name
        self.x = x


class Eng:
    def __init__(self, name, sem, same_sync):
        self.name = name
        self.sem = sem
        self.waited = {}
        self.prog = []
        self.same_sync = same_sync


class Tracker:
    def __init__(self, nc, stack):
        self.nc = nc
        self.stack = stack
        self.pe = Eng("pe", self.new_sem("s_pe"), False)
        self.act = Eng("act", self.new_sem("s_act"), SAME_SYNC)
        self.dve = Eng("dve", self.new_sem("s_dve"), SAME_SYNC)
        self.pool = Eng("pool", self.new_sem("s_pool"), SAME_SYNC)
        self.sp = Eng("sp", self.new_sem("s_sp"), False)
        self.out_sems = []
        self.tag = ""
        self.pe_log = []

    def new_sem(self, name):
        return Sem(self.stack.enter_context(self.nc.semaphore(name)))

    def _deps(self, eng, reads, writes):
        deps = {}
        for b in reads:
            if b.w is not None:
                s, v = b.w
                if deps.get(s, 0) < v:
                    deps[s] = v
            if b.x:
                for s, v in b.r.items():
                    if deps.get(s, 0) < v:
                        deps[s] = v
        for b in writes:
            if b.w is not None:
                s, v = b.w
                if deps.get(s, 0) < v:
                    deps[s] = v
            for s, v in b.r.items():
                if deps.get(s, 0) < v:
                    deps[s] = v
        for s, v in deps.items():
            if s is eng.sem and not eng.same_sync:
                continue
            if eng.waited.get(s, 0) < v:
                eng.prog.append((0, s, v))
                eng.waited[s] = v

    def op(self, eng, fn, reads=(), writes=()):
        self._deps(eng, reads, writes)
        s = eng.sem
        s.val += 1
        tok = (s, s.val)
        eng.prog.append((1, fn, s, 1, self.tag))
        for b in writes:
            b.w = tok
            b.r = {}
        for b in reads:
            b.r[s] = s.val
        return tok

    def dma(self, qeng, sem, fn, reads=(), writes=(), n=1, is_out=False):
        self._deps(qeng, reads, writes)
        sem.val += 16 * n
        tok = (sem, sem.val)
        qeng.prog.append((1, fn, sem, 16))
        for b in writes:
            b.w = tok
            b.r = {}
        for b in reads:
            b.r[sem] = sem.val
        if is_out and sem not in self.out_sems:
            self.out_sems.append(sem)
        return tok

    def finish(self):
        for s in self.out_sems:
            self.sp.prog.append((0, s, s.val))

    def replay(self):
        def run(prog, e):
            for it in prog:
                if it[0] == 0:
                    e.wait_ge(it[1].h, it[2])
                else:
                    if PROFILE_LOG and len(it) > 4 and prog is self.pe.prog:
                        n0 = self.nc.n_instructions() if callable(self.nc.n_instructions) else self.nc.n_instructions
                    r = it[1](e)
                    if PROFILE_LOG and len(it) > 4 and prog is self.pe.prog:
                        n1 = self.nc.n_instructions() if callable(self.nc.n_instructions) else self.nc.n_instructions
                        self.pe_log.append((it[4], n1 - n0))
                    if isinstance(r, (list, tuple)):
                        for x in r:
                            x.then_inc(it[2].h, it[3])
                    else:
                        r.then_inc(it[2].h, it[3])

        with self.nc.Block() as block:
            @block.tensor
            def _(e):
                run(self.pe.prog, e)

            @block.scalar
            def _(e):
                run(self.act.prog, e)

            @block.vector
            def _(e):
                run(self.dve.prog, e)

            @block.gpsimd
            def _(e):
                run(self.pool.prog, e)

            @block.sync
            def _(e):
                run(self.sp.prog, e)


def build_program(n_seq, seq_len, depth, nsub, with_meta=True, with_sample=True):
    NT = 128 * nsub
    assert seq_len % NT == 0
    tiles_per_seq = seq_len // NT
    nc = bass.Bass("TRN2", target_bir_lowering=False)

    def din(name, shape):
        return nc.dram_tensor(name, list(shape), F32, kind="ExternalInput").ap()

    def dout(name, shape):
        return nc.dram_tensor(name, list(shape), F32, kind="ExternalOutput").ap()

    def dscr(name, shape):
        return nc.dram_tensor(name, list(shape), F32).ap()

    xp = din("xp", [n_seq, seq_len, D])
    xs = din("xs", [32, D])
    sg_in = din("sg_in", [depth, 4, 128, 256])
    sr_in = din("sr_in", [depth, 4, 128, 256])
    meta = din("meta", [N_META, D])
    ln_g = din("ln_g", [depth, 3, D])
    ln_b = din("ln_b", [depth, 3, D])
    w1u = din("w1u", [depth, D, 2 * DFF])
    w1d = din("w1d", [depth, DFF, D])
    w_in = din("w_in", [depth, D, NIN])
    w_a2 = din("w_a2", [depth, 16, 512])
    b_a = din("b_a", [depth, 512])
    bm_t = din("bm_t", [128, depth, 16])
    gng_t = din("gng_t", [128, depth, 8])
    gnr_t = din("gnr_t", [128, depth, 8])
    w_oa = din("w_oa", [depth, D, D])
    w_ob = din("w_ob", [depth, D, D])
    w_o = din("w_o", [depth, D, D])
    w2u = din("w2u", [depth, D, 2 * DFF])
    w2d = din("w2d", [depth, DFF, D])
    c_idf = din("c_idf", [128, 128])
    c_mask = din("c_mask", [128, 128])
    c_tri = din("c_tri", [128, 128])
    c_cos = din("c_cos", [N_META + 2048, 64])
    c_sin = din("c_sin", [N_META + 2048, 64])
    c_nsin = din("c_nsin", [N_META + 2048, 64])
    c_c1T = din("c_c1T", [8, 176])
    lng_t = din("lng_t", [128, depth, 3, 8])
    lnb_t = din("lnb_t", [128, depth, 3, 8])
    c_c2 = din("c_c2", [128, 3, 4])
    c_decr = din("c_decr", [3, 4])

    yp = dout("yp", [n_seq, seq_len, D])
    ys = dout("ys", [32, D])
    gp = dout("gp", [depth, n_seq, 4, 128, 256])
    rp = dout("rp", [depth, n_seq, 4, 128, 256])
    gs = dout("gs", [depth, 4, 128, 256])
    rs = dout("rs", [depth, 4, 128, 256])

    WSH = {"w1u": (D, 2 * DFF), "w1d": (DFF, D), "w_in": (D, NIN), "w_oa": (D, D), "w_ob": (D, D), "w_o": (D, D),
           "w2u": (D, 2 * DFF), "w2d": (DFF, D)}
    wfp = {"w1u": w1u, "w1d": w1d, "w_in": w_in, "w_oa": w_oa, "w_ob": w_ob, "w_o": w_o, "w2u": w2u, "w2d": w2d}
    wsc = {k: nc.dram_tensor(k + "_bf", [depth, r, c], BF16).ap() for k, (r, c) in WSH.items()}

    def wsrc(name, l):
        return wsc[name][l]

    smeta_g = dscr("smeta_g", [depth, 4, 128, 256])
    smeta_r = dscr("smeta_r", [depth, 4, 128, 256])
    scr_g = dscr("scr_g", [depth, 4, 128, 256])
    scr_r = dscr("scr_r", [depth, 4, 128, 256])

    with ExitStack() as st:
        T = Tracker(nc, st)
        pe, act, dve, pool, sp = T.pe, T.act, T.dve, T.pool, T.sp

        def sb(name, shape, dt=F32):
            return st.enter_context(nc.sbuf_tensor(name, list(shape), dt))

        idf = sb("idf", [128, 128])
        idb = sb("idb", [128, 128], BF16)
        mask = sb("mask", [128, 128])
        tri = sb("tri", [128, 128])
        neg16 = sb("neg16", [128, 1])
        mhalf = sb("mhalf", [128, 8])
        c1T = sb("c1T", [128, 8, 176])
        c2 = sb("c2", [128, 3, 4])
        decr = sb("decr", [128, 3, 4])
        ba_bc = sb("ba_bc", [128, 512])
        wa2 = sb("wa2", [16, 512], BF16)
        G = [sb(f"G{i}", [128, 512]) for i in range(5)]
        bm = sb("bm", [128, depth, 16])
        gng = sb("gng", [128, depth, 8])
        gnr = sb("gnr", [128, depth, 8])
        lng = sb("lng", [128, D])
        lnb = sb("lnb", [128, D])
        cosT = sb("cosT", [128, nsub, 64])
        sinT = sb("sinT", [128, nsub, 64])
        nsinT = sb("nsinT", [128, nsub, 64])
        h = sb("h", [128, nsub, D])
        hT = sb("hT", [128, 8, NT], BF16)
        ring = [sb(f"ring{i}", [128, 4096], BF16) for i in range(NSLOT)]
        satmp = [G[0], G[1]]
        agT = sb("agT", [16, NT], BF16)
        xb, ee, ltok, E1, E2 = G
        decg = sb("decg", [128, nsub, 4])
        qt = sb("qt", [128, nsub, 512], BF16)
        kt = sb("kt", [128, nsub, 512], BF16)
        assert nsub == 4, "buffer aliasing below assumes 512-token tiles"
        vv = sb("vv", [128, nsub, 1024], BF16)
        sgate = sb("sgate", [128, nsub, 1024], BF16)
        vv2 = sb("vv2", [128, nsub, 1024], BF16)
        sgate2 = sb("sgate2", [128, nsub, 1024], BF16)
        gT = [sgate[:, 2 * i:2 * i + 2, :].rearrange("p s (j t) -> p (s j) t", j=2) for i in range(2)]
        mT = vv[:, :, :].rearrange("p s (j t) -> p (s j) t", j=2)
        zc, za, ztmp, rot = G[0:4]
        Sbf = sb("Sbf", [128, 4, 256], BF16)
        attb = sb("attb", [128, 4, 128], BF16)
        on = sb("on", [128, 4, 256])
        yin = sb("yin", [128, 1024], BF16)
        hst = sb("hst", [128, 4, 6])
        hmv = sb("hmv", [128, 4, 2])
        hve = sb("hve", [128, 4])
        hrs = sb("hrs", [128, 4])
        yinTa = sb("yinTa", [128, 8, NT], BF16)
        yinTb = sb("yinTb", [128, 8, NT], BF16)
        sgA, sgB, t1, t2 = G[0:4]
        Sg = sb("Sg", [128, 4, 256])
        Sr = sb("Sr", [128, 4, 256])
        lst = [sb(f"lst{i}", [128, 2, 6]) for i in range(nsub)]
        lmv = [sb(f"lmv{i}", [128, 2]) for i in range(nsub)]
        lve = [sb(f"lve{i}", [128, 1]) for i in range(nsub)]
        lrs = [sb(f"lrs{i}", [128, 1]) for i in range(nsub)]
        lnm = [sb(f"lnm{i}", [128, 1]) for i in range(nsub)]
        hb = [sb(f"hb{i}", [128, D], BF16) for i in range(nsub)]
        lngT = sb("lngT", [128, depth, 3, 8])
        lnbT = sb("lnbT", [128, depth, 3, 8])
        hnm = sb("hnm", [128, 4])
        qkT = sb("qkT", [128, 8, 128], BF16)

        ps = st.enter_context(nc.psum_tensor("ps", [128, 8, 512], F32))
        psb = ps.bitcast(BF16)

        b_const = Buf("const")
        b_w = {k: [Buf(f"w_{k}{l}") for l in range(depth)] for k in WSH}
        b_lp = Buf("lp")
        b_ln = Buf("lnp")
        b_cs = Buf("cossin")
        b_h = [[Buf(f"h{s}{j}") for j in range(2)] for s in range(nsub)]
        b_hT = [[Buf(f"hT{s}_{k}") for k in range(8)] for s in range(nsub)]

        def flat(xs):
            out = []
            for x in xs:
                if isinstance(x, list):
                    out.extend(x)
                else:
                    out.append(x)
            return out
        b_ring = [Buf(f"ring{i}") for i in range(NSLOT)]
        b_G = [Buf(f"G{i}") for i in range(5)]
        b_sa = [b_G[0], b_G[1]]
        b_ps = [Buf(f"ps{i}", x=True) for i in range(8)]
        b_agT = Buf("agT")
        b_xb, b_ee, b_ltok, b_E1, b_E2 = b_G
        b_decg = [Buf(f"decg{s}") for s in range(nsub)]
        b_qt = [Buf(f"qt{s}") for s in range(nsub)]
        b_kt = [Buf(f"kt{s}") for s in range(nsub)]
        b_vv = [[Buf(f"vv{s}{j}") for j in range(2)] for s in range(nsub)]
        b_sg = [[Buf(f"sg{s}{j}") for j in range(2)] for s in range(nsub)]
        b_vv2 = [[Buf(f"vw{s}{j}") for j in range(2)] for s in range(nsub)]
        b_sg2 = [[Buf(f"sh{s}{j}") for j in range(2)] for s in range(nsub)]
        b_gT = [[b_sg[2 * i + c // 2][c % 2] for c in range(4)] for i in range(2)]
        b_zc, b_za, b_ztmp, b_rot = b_G[0:4]
        b_Sbf, b_qTt, b_kTt, b_attb, b_on, b_yin = Buf("Sbf"), Buf("qTt"), Buf("kTt"), Buf("attb"), Buf("on"), Buf("yin")
        b_hst, b_hmv, b_hve, b_hrs = Buf("hst"), Buf("hmv"), Buf("hve"), Buf("hrs")
        b_yTa = [Buf(f"yTa{s}") for s in range(nsub)]
        b_yTb = [Buf(f"yTb{s}") for s in range(nsub)]
        b_mT = [b_vv[c // 2][c % 2] for c in range(8)]
        b_sgA, b_sgB, b_t1, b_t2 = b_G[0:4]
        b_Sg, b_Sr = Buf("Sg"), Buf("Sr")
        b_lst = [Buf(f"lst{i}") for i in range(nsub)]
        b_lmv = [Buf(f"lmv{i}") for i in range(nsub)]
        b_lve = [Buf(f"lve{i}") for i in range(nsub)]
        b_lrs = [Buf(f"lrs{i}") for i in range(nsub)]
        b_lnm = [Buf(f"lnm{i}") for i in range(nsub)]
        b_hb = [Buf(f"hb{i}") for i in range(nsub)]
        b_hnm, b_qkT = Buf("hnm"), Buf("qkT")
        b_smg = [Buf(f"smg{l}") for l in range(depth)]
        b_smr = [Buf(f"smr{l}") for l in range(depth)]
        b_scg = [Buf(f"scg{l}") for l in range(depth)]
        b_scr = [Buf(f"scr{l}") for l in range(depth)]

        s_ring = [T.new_sem(f"d_ring{i}") for i in range(NSLOT)]
        s_const = T.new_sem("d_const")
        s_cv = {k: [T.new_sem(f"d_cv_{k}{l}") for l in range(depth)] for k in WSH}
        s_lp = T.new_sem("d_lp")
        s_x = T.new_sem("d_x")
        s_y = T.new_sem("d_y")
        s_ln = T.new_sem("d_ln")
        s_cs = T.new_sem("d_cs")
        s_stl_g, s_stl_r = T.new_sem("d_stlg"), T.new_sem("d_stlr")
        s_sts_g, s_sts_r = T.new_sem("d_stsg"), T.new_sem("d_stsr")

        pstate = {"i": 0}

        def palloc():
            i = pstate["i"]
            pstate["i"] = (i + 1) % 8
            return i

        rstate = {"i": 0}

        def wload(src, a, b, dep):
            i = rstate["i"]
            rstate["i"] = (i + 1) % NSLOT
            view = ring[i][:, 0:a * b].rearrange("p (a b) -> p a b", a=a)
            T.dma(sp, s_ring[i], lambda e, view=view, src=src: e.dma_start(out=view, in_=src),
                  reads=[dep], writes=[b_ring[i]])
            return view, b_ring[i]

        def load_consts():
            def f(e):
                r = [
                    e.dma_start(out=idf[:], in_=c_idf[:, :]),
                    e.dma_start(out=mask[:], in_=c_mask[:, :]),
                    e.dma_start(out=tri[:], in_=c_tri[:, :]),
                    e.dma_start(out=c1T[:], in_=c_c1T.partition_broadcast(128)),
                    e.dma_start(out=c2[:], in_=c_c2[:, :, :]),
                    e.dma_start(out=decr[:], in_=c_decr.partition_broadcast(128)),
                    e.dma_start(out=bm[:], in_=bm_t[:, :, :]),
                    e.dma_start(out=gng[:], in_=gng_t[:, :, :]),
                    e.dma_start(out=gnr[:], in_=gnr_t[:, :, :]),
                    e.dma_start(out=lngT[:], in_=lng_t[:, :, :, :]),
                    e.dma_start(out=lnbT[:], in_=lnb_t[:, :, :, :]),
                ]
                return r
            T.dma(pool, s_const, f, writes=[b_const], n=11)
            T.op(dve, lambda e: e.tensor_copy(out=idb[:], in_=idf[:]), reads=[b_const], writes=[b_const])
            T.op(dve, lambda e: e.memset(neg16[:], -1.0 / 16.0), writes=[b_const])
            T.op(dve, lambda e: e.memset(mhalf[:], -0.5), writes=[b_const])

        converted = set()

        def convert_layer(l):
            if l >= depth or l in converted:
                return
            converted.add(l)
            if True:
                for name in ("w1u", "w1d", "w_in", "w_oa", "w_ob", "w_o", "w2u", "w2d"):
                    src, dst = wfp[name][l], wsc[name][l]
                    nchunk = WSH[name][0] // 128

                    def f(e, src=src, dst=dst, nchunk=nchunk):
                        return [e.dma_start(out=dst[c * 128:(c + 1) * 128, :], in_=src[c * 128:(c + 1) * 128, :])
                                for c in range(nchunk)]
                    T.dma(pool, s_cv[name][l], f, writes=[b_w[name][l]], n=nchunk)

        def rstd_pow(out_ap, in_ap, nrow, ncol, rb, wb):
            T.op(pool, lambda e: e.tensor_tensor(out=out_ap, in0=in_ap, in1=mhalf[0:nrow, 0:ncol], op=ALU.pow),
                 reads=[rb, b_const], writes=[wb])

        def transpose_to_hT(s, L, aff=None):
            p0, p1 = palloc(), palloc()
            if aff is None:
                def f(e):
                    r = None
                    for k in range(8):
                        bank = p0 if k < 4 else p1
                        r = e.transpose(ps[:, bank, (k % 4) * 128:(k % 4) * 128 + L],
                                        h[0:L, s, k * 128:(k + 1) * 128], idf[0:L, 0:L])
                    return r
                T.op(pe, f, reads=[b_h[s][0], b_h[s][1], b_const], writes=[b_ps[p0], b_ps[p1]])
            else:
                i = s

                def f(e):
                    r = None
                    for k in range(8):
                        bank = p0 if k < 4 else p1
                        r = e.transpose(psb[:, bank, (k % 4) * 128:(k % 4) * 128 + L],
                                        hb[i][0:L, k * 128:(k + 1) * 128], idb[0:L, 0:L])
                    return r
                T.op(pe, f, reads=[b_hb[i], b_const], writes=[b_ps[p0], b_ps[p1]])
            if aff is None:
                T.op(act, lambda e: e.copy(out=hT[:, 0:4, s * 128:s * 128 + L],
                                            in_=ps[:, p0, :].rearrange("p (k t) -> p k t", k=4)[:, :, 0:L]),
                     reads=[b_ps[p0]], writes=b_hT[s][0:4])
                T.op(dve, lambda e: e.tensor_copy(out=hT[:, 4:8, s * 128:s * 128 + L],
                                                   in_=ps[:, p1, :].rearrange("p (k t) -> p k t", k=4)[:, :, 0:L]),
                     reads=[b_ps[p1]], writes=b_hT[s][4:8])
                return
            l, idx = aff
            for k in range(8):
                bank = p0 if k < 4 else p1
                src = psb[:, bank, (k % 4) * 128:(k % 4) * 128 + L]
                dst = hT[:, k, s * 128:s * 128 + L]
                if k < 4:
                    T.op(act, lambda e, src=src, dst=dst, k=k: e.activation(
                        out=dst, in_=src, func=AF.Identity, scale=lngT[:, l, idx, k:k + 1], bias=lnbT[:, l, idx, k:k + 1]),
                        reads=[b_ps[bank], b_const], writes=[b_hT[s][k]])
                else:
                    T.op(dve, lambda e, src=src, dst=dst, k=k: e.tensor_scalar(
                        out=dst, in0=src, scalar1=lngT[:, l, idx, k:k + 1], scalar2=lnbT[:, l, idx, k:k + 1],
                        op0=ALU.mult, op1=ALU.add),
                        reads=[b_ps[bank], b_const], writes=[b_hT[s][k]])

        def load_lp(l):
            T.dma(pool, s_lp, lambda e: [e.dma_start(out=ba_bc[:], in_=b_a[l].partition_broadcast(128)),
                                         e.dma_start(out=wa2[:], in_=w_a2[l])], writes=[b_lp], n=2)

        def load_ln(l, idx):
            def f(e):
                return [e.dma_start(out=lng[:], in_=ln_g[l, idx].partition_broadcast(128)),
                        e.dma_start(out=lnb[:], in_=ln_b[l, idx].partition_broadcast(128))]
            T.dma(pool, s_ln, f, writes=[b_ln], n=2)

        def layer_norm_all(tile, l, idx):
            ns, L = tile["ns"], tile["L"]
            T.tag = "ln"
            for s in range(ns):
                transpose_to_hT(s, L, aff=(l, idx))
            for s in range(ns):
                hs = h[0:L, s, :]
                T.op(pool, lambda e, hs=hs: e.tensor_tensor(out=hs, in0=hs, in1=lng[0:L, :], op=ALU.mult),
                     reads=[b_h[s][0], b_h[s][1], b_ln], writes=[b_h[s][0], b_h[s][1]])
                T.op(pool, lambda e, hs=hs: e.tensor_tensor(out=hs, in0=hs, in1=lnb[0:L, :], op=ALU.add),
                     reads=[b_h[s][0], b_h[s][1], b_ln], writes=[b_h[s][0], b_h[s][1]])

        def ln_front(s, L):
            if True:
                i = s
                hs = h[0:L, s, :]
                T.op(dve, lambda e, i=i, s=s: e.bn_stats(out=lst[i][0:L, 0, :], in_=h[0:L, s, 0:512]), reads=[b_h[s][0]], writes=[b_lst[i]])
                T.op(dve, lambda e, i=i, s=s: e.bn_stats(out=lst[i][0:L, 1, :], in_=h[0:L, s, 512:1024]), reads=[b_h[s][1]], writes=[b_lst[i]])
                T.op(dve, lambda e, i=i: e.bn_aggr(out=lmv[i][0:L, :], in_=lst[i][0:L, :, :]), reads=[b_lst[i]], writes=[b_lmv[i]])
                T.op(dve, lambda e, i=i: e.tensor_scalar_add(out=lve[i][0:L, :], in0=lmv[i][0:L, 1:2], scalar1=LN_EPS),
                     reads=[b_lmv[i]], writes=[b_lve[i]])
                rstd_pow(lrs[i][0:L, :], lve[i][0:L, :], L, 1, b_lve[i], b_lrs[i])
                T.op(dve, lambda e, i=i: e.scalar_tensor_tensor(out=lnm[i][0:L, :], in0=lmv[i][0:L, 0:1], scalar=-1.0, in1=lrs[i][0:L, :],
                                                                 op0=ALU.mult, op1=ALU.mult),
                     reads=[b_lmv[i], b_lrs[i]], writes=[b_lnm[i]])
                T.op(act, lambda e, i=i, hs=hs: e.activation(out=hb[i][0:L, :], in_=hs, func=AF.Identity, scale=lrs[i][0:L, :], bias=lnm[i][0:L, :]),
                     reads=[b_h[s][0], b_h[s][1], b_lrs[i], b_lnm[i]], writes=[b_hb[i]])
                T.op(act, lambda e, i=i, hs=hs: e.activation(out=hs, in_=hs, func=AF.Identity, scale=lrs[i][0:L, :], bias=lnm[i][0:L, :]),
                     reads=[b_h[s][0], b_h[s][1], b_lrs[i], b_lnm[i]], writes=[b_h[s][0], b_h[s][1]])

        def residual_add(s, L, hf, pbank, first):
            hh = h[0:L, s, hf * 512:(hf + 1) * 512]
            if first:
                T.op(dve, lambda e: e.scalar_tensor_tensor(out=hh, in0=hh, scalar=DN_ALPHA, in1=ps[0:L, pbank, :],
                                                            op0=ALU.mult, op1=ALU.add),
                     reads=[b_h[s][hf], b_ps[pbank]], writes=[b_h[s][hf]])
            else:
                T.op(dve, lambda e: e.tensor_tensor(out=hh, in0=hh, in1=ps[0:L, pbank, :], op=ALU.add),
                     reads=[b_h[s][hf], b_ps[pbank]], writes=[b_h[s][hf]])

        def ffn(tile, nu, nd, l):
            T.tag = "ffn"
            ns, L = tile["ns"], tile["L"]
            ntok = tile["ntok"]
            wu_v = wsrc(nu, l).rearrange("(k p) n -> p k n", p=128)
            wd_v = wsrc(nd, l).rearrange("(c p) n -> p c n", p=128)
            du, dd = b_w[nu][l], b_w[nd][l]
            groups = [(0, 512), (512, 512), (1024, 512), (1536, 512), (2048, 512), (2560, 256)]
            for gi, (f0, fw) in enumerate(groups):
                nch = fw // 128
                wa_v, wa_b = wload(wu_v[:, :, f0:f0 + fw], 8, fw, du)
                wb_v, wb_b = wload(wu_v[:, :, DFF + f0:DFF + f0 + fw], 8, fw, du)
                wd_s, wd_b = wload(wd_v[:, f0 // 128:f0 // 128 + nch, :], nch, 1024, dd)
                gb = gi % 2
                for c in range(nch):
                    pa, pb = palloc(), palloc()

                    def fup(e, w_v=wa_v, bank=pa, c=c):
                        r = None
                        for k in range(8):
                            r = e.matmul(ps[:, bank, 0:ntok], lhsT=w_v[:, k, c * 128:(c + 1) * 128],
                                         rhs=hT[:, k, 0:ntok], start=(k == 0), stop=(k == 7))
                        return r
                    T.op(pe, fup, reads=[wa_b] + flat(b_hT[0:ns]), writes=[b_ps[pa]])

                    def fupb(e, w_v=wb_v, bank=pb, c=c):
                        r = None
                        for k in range(8):
                            r = e.matmul(ps[:, bank, 0:ntok], lhsT=w_v[:, k, c * 128:(c + 1) * 128],
                                         rhs=hT[:, k, 0:ntok], start=(k == 0), stop=(k == 7))
                        return r
                    T.op(pe, fupb, reads=[wb_b] + flat(b_hT[0:ns]), writes=[b_ps[pb]])
                    si = c % 2
                    T.op(act, lambda e, pa=pa, si=si: e.activation(out=satmp[si][:, 0:ntok], in_=ps[:, pa, 0:ntok], func=AF.Silu),
                         reads=[b_ps[pa]], writes=[b_sa[si]])
                    T.op(dve, lambda e, pb=pb, si=si, c=c, gb=gb: e.scalar_tensor_tensor(
                        out=gT[gb][:, c, 0:ntok], in0=satmp[si][:, 0:ntok], scalar=0.5, in1=ps[:, pb, 0:ntok],
                        op0=ALU.mult, op1=ALU.mult),
                        reads=[b_sa[si], b_ps[pb]], writes=[b_gT[gb][c]])
                for s in range(ns):
                    for hf in range(2):
                        py = palloc()

                        def fdn(e, s=s, hf=hf, py=py, nch=nch, gb=gb, wd_s=wd_s):
                            r = None
                            for c in range(nch):
                                r = e.matmul(ps[0:L, py, :], lhsT=gT[gb][:, c, s * 128:s * 128 + L],
                                             rhs=wd_s[:, c, hf * 512:(hf + 1) * 512], start=(c == 0), stop=(c == nch - 1))
                            return r
                        T.op(pe, fdn, reads=[wd_b] + b_gT[gb][0:nch], writes=[b_ps[py]])
                        residual_add(s, L, hf, py, first=(gi == 0))
                    if gi == len(groups) - 1:
                        ln_front(s, L)
                        T.tag = "ffn"

        def core_A(s, L, dec_ap, dec_buf, S, b_S, is_ret, vt, b_vt):
            pt, pa, pk0, pk1 = 0, 1, 2, 3
            po0, po1 = (4, 5) if s % 2 == 0 else (6, 7)
            T.op(pool, lambda e: e.tensor_tensor(out=Sbf[:], in0=S[:], in1=dec_ap.rearrange("p (h o) -> p h o", o=1).to_broadcast([128, 4, 256]),
                                                  op=ALU.mult),
                 reads=[b_S, dec_buf], writes=[b_Sbf])

            def ftr(e):
                r = None
                for hh in range(4):
                    r = e.transpose(psb[:, pt, hh * 128:hh * 128 + L], qt[0:L, s, hh * 128:(hh + 1) * 128], idb[0:L, 0:L])
                for hh in range(4):
                    r = e.transpose(psb[:, pt, 512 + hh * 128:512 + hh * 128 + L], kt[0:L, s, hh * 128:(hh + 1) * 128], idb[0:L, 0:L])
                return r
            T.op(pe, ftr, reads=[b_qt[s], b_kt[s], b_const], writes=[b_ps[pt]])
            src = psb[:, pt, :].rearrange("p (g t) -> p g t", g=8)[:, :, 0:L]
            if not is_ret:
                T.op(act, lambda e: e.copy(out=qkT[:, :, 0:L], in_=src), reads=[b_ps[pt]], writes=[b_qkT])
            else:
                T.op(dve, lambda e: e.tensor_tensor(out=qkT[:, :, 0:L], in0=src, in1=c1T[:, :, LOFF[L]:LOFF[L] + L], op=ALU.mult),
                     reads=[b_ps[pt], b_const], writes=[b_qkT])

            def fatt(e):
                r = None
                for hh in range(4):
                    r = e.matmul(ps[0:L, pa, hh * 128:hh * 128 + L], lhsT=qkT[:, 4 + hh, 0:L], rhs=qkT[:, hh, 0:L],
                                 start=True, stop=True)
                return r
            T.op(pe, fatt, reads=[b_qkT], writes=[b_ps[pa]])
            T.op(dve, lambda e: e.tensor_tensor(
                out=attb[0:L, :, 0:L], in0=ps[0:L, pa, :].rearrange("p (h t) -> p h t", h=4)[:, :, 0:L],
                in1=mask[0:L, 0:L].rearrange("p (o t) -> p o t", o=1).to_broadcast([L, 4, L]), op=ALU.mult),
                reads=[b_ps[pa], b_const], writes=[b_attb])

            def fo(e):
                r = None
                for hh in range(4):
                    bank = po0 if hh < 2 else po1
                    oo = ps[0:L, bank, (hh % 2) * 256:(hh % 2 + 1) * 256]
                    e.matmul(oo, lhsT=attb[0:L, hh, 0:L], rhs=vt[0:L, s, hh * 256:(hh + 1) * 256], start=True, stop=False)
                    r = e.matmul(oo, lhsT=qkT[:, hh, 0:L], rhs=Sbf[:, hh, :], start=False, stop=True)
                return r
            T.op(pe, fo, reads=[b_attb, b_vt[s][0], b_vt[s][1], b_qkT, b_Sbf], writes=[b_ps[po0], b_ps[po1]])

            def fkv(e):
                r = None
                for hh in range(4):
                    bank = pk0 if hh < 2 else pk1
                    r = e.matmul(ps[:, bank, (hh % 2) * 256:(hh % 2 + 1) * 256], lhsT=kt[0:L, s, hh * 128:(hh + 1) * 128],
                                 rhs=vt[0:L, s, hh * 256:(hh + 1) * 256], start=True, stop=True)
                return r
            T.op(pe, fkv, reads=[b_kt[s], b_vt[s][0], b_vt[s][1]], writes=[b_ps[pk0], b_ps[pk1]])

        def core_A2(s, dec_ap, dec_buf, S, b_S):
            pk0, pk1 = 2, 3
            for hh in range(4):
                bank = pk0 if hh < 2 else pk1
                T.op(dve, lambda e, hh=hh, bank=bank: e.scalar_tensor_tensor(
                    out=S[:, hh, :], in0=S[:, hh, :], scalar=dec_ap[:, hh:hh + 1],
                    in1=ps[:, bank, (hh % 2) * 256:(hh % 2 + 1) * 256], op0=ALU.mult, op1=ALU.add),
                    reads=[b_S, dec_buf, b_ps[bank]], writes=[b_S])

        def core_B(s, L, gt, b_gt):
            po0, po1 = (4, 5) if s % 2 == 0 else (6, 7)
            for hh in range(4):
                bank = po0 if hh < 2 else po1
                T.op(dve, lambda e, hh=hh, bank=bank: e.bn_stats(out=hst[0:L, hh, :], in_=ps[0:L, bank, (hh % 2) * 256:(hh % 2 + 1) * 256]),
                     reads=[b_ps[bank]], writes=[b_hst])
            for hh in range(4):
                T.op(dve, lambda e, hh=hh: e.bn_aggr(out=hmv[0:L, hh, :], in_=hst[0:L, hh:hh + 1, :]),
                     reads=[b_hst], writes=[b_hmv])
            T.op(dve, lambda e: e.tensor_scalar_add(out=hve[0:L, :], in0=hmv[0:L, :, 1], scalar1=GN_EPS),
                 reads=[b_hmv], writes=[b_hve])
            rstd_pow(hrs[0:L, :], hve[0:L, :], L, 4, b_hve, b_hrs)
            T.op(act, lambda e: e.activation(out=on[0:L, 0, :], in_=gt[0:L, s, 0:256], func=AF.Identity, scale=hrs[0:L, 0:1]),
                 reads=[b_gt[s][0], b_hrs], writes=[b_on])
            for hh in range(1, 4):
                T.op(act, lambda e, hh=hh: e.activation(out=on[0:L, hh, :], in_=gt[0:L, s, hh * 256:(hh + 1) * 256], func=AF.Identity,
                                                         scale=hrs[0:L, hh:hh + 1]),
                     reads=[b_gt[s][hh // 2], b_hrs, b_on], writes=[b_on])
            for hh in range(4):
                bank = po0 if hh < 2 else po1
                T.op(dve, lambda e, hh=hh, bank=bank: e.scalar_tensor_tensor(
                    out=yin[0:L, hh * 256:(hh + 1) * 256], in0=ps[0:L, bank, (hh % 2) * 256:(hh % 2 + 1) * 256],
                    scalar=hmv[0:L, hh, 0:1], in1=on[0:L, hh, :], op0=ALU.subtract, op1=ALU.mult),
                    reads=[b_ps[bank], b_hmv, b_on, b_yin] if hh else [b_ps[bank], b_hmv, b_on], writes=[b_yin])

        def core_B2(s, L, gn_ap, yT, b_yT):
            py = 2

            def fty(e):
                r = None
                for c in range(8):
                    r = e.transpose(psb[:, py, c * 128:c * 128 + L], yin[0:L, c * 128:(c + 1) * 128], idb[0:L, 0:L])
                return r
            T.op(pe, fty, reads=[b_yin, b_const], writes=[b_ps[py]])
            T.op(dve, lambda e: e.tensor_tensor(
                out=yT[:, :, s * 128:s * 128 + L], in0=psb[:, py, :].rearrange("p (c t) -> p c t", c=8)[:, :, 0:L],
                in1=gn_ap.rearrange("p (c o) -> p c o", o=1).to_broadcast([128, 8, L]), op=ALU.mult),
                reads=[b_ps[py], b_const], writes=[b_yT[s]])

        def mixer_cores(ns, L, dec_fn, S, b_S, is_ret, gn_ap, yT, b_yT, vt, b_vt, gt, b_gt, fillers=()):
            fillers = list(fillers)
            for s in range(ns):
                dec_ap, dec_buf = dec_fn(s)
                core_A(s, L, dec_ap, dec_buf, S, b_S, is_ret, vt, b_vt)
                if s >= 1:
                    core_B(s - 1, L, gt, b_gt)
                core_A2(s, dec_ap, dec_buf, S, b_S)
                if fillers:
                    fillers.pop(0)()
                if s >= 1:
                    core_B2(s - 1, L, gn_ap, yT, b_yT)
            core_B(ns - 1, L, gt, b_gt)
            while fillers:
                fillers.pop(0)()
            core_B2(ns - 1, L, gn_ap, yT, b_yT)

        def proj_tok(s, L, w_v, w_b, bank=None):
            pz = palloc() if bank is None else bank

            def f(e):
                r = None
                for k in range(8):
                    r = e.matmul(ps[0:L, pz, :], lhsT=hT[:, k, s * 128:s * 128 + L], rhs=w_v[:, k, :],
                                 start=(k == 0), stop=(k == 7))
                return r
            T.op(pe, f, reads=[w_b] + b_hT[s], writes=[b_ps[pz]])
            return pz

        def rotary(s, L, pz, out_ap, out_bufs):
            T.op(act, lambda e: e.copy(out=zc[0:L, :], in_=ps[0:L, pz, :]), reads=[b_ps[pz]], writes=[b_zc])
            z4 = zc[0:L, :].rearrange("p (h w j) -> p h w j", h=4, w=2)
            a4 = za[0:L, :].rearrange("p (h w j) -> p h w j", h=4, w=2)
            t4 = ztmp[0:L, :].rearrange("p (h w j) -> p h w j", h=4, w=2)
            cos_b = cosT[0:L, s, :].rearrange("p (o j) -> p o j", o=1).to_broadcast([L, 8, 64])
            sin_b = sinT[0:L, s, :].rearrange("p (o j) -> p o j", o=1).to_broadcast([L, 4, 64])
            nsin_b = nsinT[0:L, s, :].rearrange("p (o j) -> p o j", o=1).to_broadcast([L, 4, 64])
            T.op(dve, lambda e: e.tensor_tensor(out=za[0:L, :].rearrange("p (g j) -> p g j", g=8),
                                                 in0=zc[0:L, :].rearrange("p (g j) -> p g j", g=8), in1=cos_b, op=ALU.mult),
                 reads=[b_zc, b_cs], writes=[b_za])
            T.op(pool, lambda e: e.tensor_tensor(out=t4[:, :, 0, :], in0=z4[:, :, 1, :], in1=nsin_b, op=ALU.mult),
                 reads=[b_zc, b_cs], writes=[b_ztmp])
            T.op(pool, lambda e: e.tensor_tensor(out=t4[:, :, 1, :], in0=z4[:, :, 0, :], in1=sin_b, op=ALU.mult),
                 reads=[b_zc, b_cs], writes=[b_ztmp])
            T.op(dve, lambda e: e.tensor_tensor(out=out_ap, in0=za[0:L, :], in1=ztmp[0:L, :], op=ALU.add),
                 reads=[b_za, b_ztmp], writes=out_bufs)

        def mixers(tile, l):
            ns, L, ntok = tile["ns"], tile["L"], tile["ntok"]
            T.tag = "mix.gla_qk"
            li = LIDX[L]
            W = wsrc("w_in", l).rearrange("(k p) n -> p k n", p=128)
            wdeps = [b_w["w_in"][l]]
            wag_v, wag_b = wload(W[:, :, 3072:3088], 8, 16, wdeps[0])
            pg = palloc()

            def fag(e):
                r = None
                for k in range(8):
                    r = e.matmul(ps[0:16, pg, 0:ntok], lhsT=wag_v[:, k, :], rhs=hT[:, k, 0:ntok], start=(k == 0), stop=(k == 7))
                return r
            T.op(pe, fag, reads=[wag_b] + flat(b_hT[0:ns]), writes=[b_ps[pg]])
            T.op(act, lambda e: e.copy(out=agT[:, 0:ntok], in_=ps[0:16, pg, 0:ntok]), reads=[b_ps[pg]], writes=[b_agT])
            if MIX_STOP < 2:
                return
            wq_v, wq_b = wload(W[:, :, 0:512], 8, 512, wdeps[0])
            wk_v, wk_b = wload(W[:, :, 512:1024], 8, 512, wdeps[0])
            for s in range(ns):
                px = palloc()
                T.op(pe, lambda e, s=s, px=px: e.matmul(ps[0:L, px, :], lhsT=agT[:, s * 128:s * 128 + L], rhs=wa2[:, :],
                                                         start=True, stop=True),
                     reads=[b_agT, b_lp], writes=[b_ps[px]])
                T.op(dve, lambda e, px=px: e.tensor_tensor(out=xb[0:L, :], in0=ps[0:L, px, :], in1=ba_bc[0:L, :], op=ALU.add),
                     reads=[b_ps[px], b_lp], writes=[b_xb])
                T.op(act, lambda e: e.activation(out=ee[0:L, :], in_=xb[0:L, :], func=AF.Exp, scale=-1.0),
                     reads=[b_xb], writes=[b_ee])
                T.op(act, lambda e: e.activation(out=ltok[0:L, :], in_=ee[0:L, :], func=AF.Ln, bias=1.0),
                     reads=[b_ee], writes=[b_ltok])
                pd, pbl = palloc(), palloc()
                T.op(pe, lambda e, pd=pd: e.matmul(ps[0:L, pd, :], lhsT=tri[0:L, 0:L], rhs=ltok[0:L, :], start=True, stop=True),
                     reads=[b_ltok, b_const], writes=[b_ps[pd]])

                def fbl(e, pbl=pbl):
                    r = None
                    for hh in range(4):
                        r = e.matmul(ps[:, pbl, hh:hh + 1], lhsT=ltok[0:L, hh * 128:(hh + 1) * 128], rhs=neg16[0:L, 0:1],
                                     start=True, stop=True)
                    return r
                T.op(pe, fbl, reads=[b_ltok, b_const], writes=[b_ps[pbl]])
                T.op(act, lambda e, pd=pd: e.activation(out=E1[0:L, :], in_=ps[0:L, pd, :], func=AF.Exp,
                                                         bias=float(math.log(128.0 ** -0.5)), scale=1.0),
                     reads=[b_ps[pd]], writes=[b_E1])
                T.op(act, lambda e, pd=pd: e.activation(out=E2[0:L, :], in_=ps[0:L, pd, :], func=AF.Exp, scale=-1.0),
                     reads=[b_ps[pd]], writes=[b_E2])
                T.op(act, lambda e, pbl=pbl, s=s: e.activation(out=decg[:, s, :], in_=ps[:, pbl, 0:4], func=AF.Exp),
                     reads=[b_ps[pbl]], writes=[b_decg[s]])
                pq = proj_tok(s, L, wq_v, wq_b)
                T.op(dve, lambda e, pq=pq, s=s: e.tensor_tensor(out=qt[0:L, s, :], in0=ps[0:L, pq, :], in1=E1[0:L, :], op=ALU.mult),
                     reads=[b_ps[pq], b_E1], writes=[b_qt[s]])
                pk = proj_tok(s, L, wk_v, wk_b)
                T.op(dve, lambda e, pk=pk, s=s: e.tensor_tensor(out=kt[0:L, s, :], in0=ps[0:L, pk, :], in1=E2[0:L, :], op=ALU.mult),
                     reads=[b_ps[pk], b_E2], writes=[b_kt[s]])

            def v_piece(c0, j, vt, b_vt, banks=None):
                w_v, w_b = wload(W[:, :, c0 + 512 * j:c0 + 512 * (j + 1)], 8, 512, wdeps[0])
                for s in range(ns):
                    pz = proj_tok(s, L, w_v, w_b, None if banks is None else banks[s % 2])
                    T.op(act, lambda e, pz=pz, s=s: e.copy(out=vt[0:L, s, 512 * j:512 * (j + 1)], in_=ps[0:L, pz, :]),
                         reads=[b_ps[pz]], writes=[b_vt[s][j]])

            def g_piece(c0, j, gt, b_gt, banks=None):
                w_v, w_b = wload(W[:, :, c0 + 512 * j:c0 + 512 * (j + 1)], 8, 512, wdeps[0])
                for s in range(ns):
                    pz = proj_tok(s, L, w_v, w_b, None if banks is None else banks[s % 2])
                    T.op(act, lambda e, pz=pz, s=s: e.activation(out=gt[0:L, s, 512 * j:512 * (j + 1)], in_=ps[0:L, pz, :],
                                                                  func=AF.Silu),
                         reads=[b_ps[pz]], writes=[b_gt[s][j]])

            def vg_pieces(c0v, c0g, vt, b_vt, gt, b_gt):
                for j in range(2):
                    v_piece(c0v, j, vt, b_vt)
                for j in range(2):
                    g_piece(c0g, j, gt, b_gt)

            if MIX_STOP < 3:
                return
            T.tag = "mix.gla_vg"
            vg_pieces(1024, 2048, vv, b_vv, sgate, b_sg)
            T.tag = "mix.gla_core"
            if MIX_STOP < 4:
                return
            tile["state_load"](l, "g")
            fill = [lambda j=j: v_piece(4112, j, vv2, b_vv2, banks=(0, 1)) for j in range(2)] + \
                   [lambda j=j: g_piece(5136, j, sgate2, b_sg2, banks=(0, 1)) for j in range(2)]
            mixer_cores(ns, L, lambda s: (decg[:, s, :], b_decg[s]), Sg, b_Sg, False, gng[:, l, :], yinTa, b_yTa,
                        vv, b_vv, sgate, b_sg, fillers=fill)
            tile["state_store"](l, "g")
            if MIX_STOP < 5:
                return
            T.tag = "mix.ret_qk"
            wq_v, wq_b = wload(W[:, :, 3088:3600], 8, 512, wdeps[0])
            wk_v, wk_b = wload(W[:, :, 3600:4112], 8, 512, wdeps[0])
            for s in range(ns):
                pq = proj_tok(s, L, wq_v, wq_b)
                rotary(s, L, pq, qt[0:L, s, :], [b_qt[s]])
                pk = proj_tok(s, L, wk_v, wk_b)
                rotary(s, L, pk, rot[0:L, :], [b_rot])
                T.op(dve, lambda e, s=s: e.tensor_tensor(
                    out=kt[0:L, s, :].rearrange("p (h d) -> p h d", h=4), in0=rot[0:L, :].rearrange("p (h d) -> p h d", h=4),
                    in1=c2[0:L, li, :].rearrange("p (h o) -> p h o", o=1).to_broadcast([L, 4, 128]), op=ALU.mult),
                    reads=[b_rot, b_const], writes=[b_kt[s]])
            if MIX_STOP < 6:
                return
            T.tag = "mix.ret_core"
            tile["state_load"](l, "r")
            mixer_cores(ns, L, lambda s: (decr[:, li, :], b_const), Sr, b_Sr, True, gnr[:, l, :], yinTb, b_yTb,
                        vv2, b_vv2, sgate2, b_sg2)
            tile["state_store"](l, "r")
            if MIX_STOP < 7:
                return
            T.tag = "mix.merge"
            woa = wsrc("w_oa", l).rearrange("(k p) n -> p k n", p=128)
            wob = wsrc("w_ob", l).rearrange("(k p) n -> p k n", p=128)
            for cg in range(2):
                a_v, a_b = wload(woa[:, :, 512 * cg:512 * (cg + 1)], 8, 512, b_w["w_oa"][l])
                b_v, b_b = wload(wob[:, :, 512 * cg:512 * (cg + 1)], 8, 512, b_w["w_ob"][l])
                ma_v, ma_b = wload(W[:, :, 6160 + 512 * cg:6160 + 512 * (cg + 1)], 8, 512, wdeps[0])
                mb_v, mb_b = wload(W[:, :, 7184 + 512 * cg:7184 + 512 * (cg + 1)], 8, 512, wdeps[0])
                for c4 in range(4):
                    c = 4 * cg + c4
                    banks = []
                    for (w_v, w_b, src, srcb) in ((a_v, a_b, yinTa, b_yTa), (b_v, b_b, yinTb, b_yTb),
                                                  (ma_v, ma_b, hT, b_hT), (mb_v, mb_b, hT, b_hT)):
                        pz = palloc()

                        def f(e, w_v=w_v, src=src, pz=pz, c4=c4):
                            r = None
                            for k in range(8):
                                r = e.matmul(ps[:, pz, 0:ntok], lhsT=w_v[:, k, c4 * 128:(c4 + 1) * 128], rhs=src[:, k, 0:ntok],
                                             start=(k == 0), stop=(k == 7))
                            return r
                        T.op(pe, f, reads=[w_b] + flat(srcb[0:ns]), writes=[b_ps[pz]])
                        banks.append(pz)
                    pya, pyb, pma, pmb = banks
                    T.op(act, lambda e, pma=pma, c=c: e.activation(out=sgA[:, 0:ntok], in_=ps[:, pma, 0:ntok], func=AF.Sigmoid,
                                                                    bias=bm[:, l, c:c + 1], scale=1.0),
                         reads=[b_ps[pma], b_const], writes=[b_sgA])
                    T.op(act, lambda e, pmb=pmb, c=c: e.activation(out=sgB[:, 0:ntok], in_=ps[:, pmb, 0:ntok], func=AF.Sigmoid,
                                                                    bias=bm[:, l, 8 + c:9 + c], scale=1.0),
                         reads=[b_ps[pmb], b_const], writes=[b_sgB])
                    T.op(dve, lambda e, pya=pya: e.tensor_tensor(out=t1[:, 0:ntok], in0=sgA[:, 0:ntok], in1=ps[:, pya, 0:ntok], op=ALU.mult),
                         reads=[b_sgA, b_ps[pya]], writes=[b_t1])
                    T.op(dve, lambda e, pyb=pyb: e.tensor_tensor(out=t2[:, 0:ntok], in0=sgB[:, 0:ntok], in1=ps[:, pyb, 0:ntok], op=ALU.mult),
                         reads=[b_sgB, b_ps[pyb]], writes=[b_t2])
                    T.op(dve, lambda e, c=c: e.tensor_tensor(out=mT[:, c, 0:ntok], in0=t1[:, 0:ntok], in1=t2[:, 0:ntok], op=ALU.add),
                         reads=[b_t1, b_t2], writes=[b_mT[c]])
            if MIX_STOP < 8:
                return
            T.tag = "mix.outproj"
            wo = wsrc("w_o", l).rearrange("(k p) n -> p k n", p=128)
            wo_p = [wload(wo[:, :, 512 * hf:512 * (hf + 1)], 8, 512, b_w["w_o"][l]) for hf in range(2)]
            for s in range(ns):
                for hf in range(2):
                    po = palloc()

                    def f(e, s=s, hf=hf, po=po):
                        r = None
                        for k in range(8):
                            r = e.matmul(ps[0:L, po, :], lhsT=mT[:, k, s * 128:s * 128 + L], rhs=wo_p[hf][0][:, k, :],
                                         start=(k == 0), stop=(k == 7))
                        return r
                    T.op(pe, f, reads=[wo_p[hf][1]] + b_mT, writes=[b_ps[po]])
                    residual_add(s, L, hf, po, first=True)
                ln_front(s, L)

        def run_tile(tile):
            ns, L = tile["ns"], tile["L"]
            T.dma(pool, s_x, lambda e: e.dma_start(out=h[0:L, 0:ns, :], in_=tile["x"].rearrange("(s p) d -> p s d", p=L)),
                  writes=[b for s in range(ns) for b in b_h[s]])
            pos0 = tile["pos0"]

            def fcs(e):
                return [e.dma_start(out=dst[0:L, 0:ns, :], in_=src[pos0:pos0 + ns * L, :].rearrange("(s p) j -> p s j", p=L))
                        for dst, src in ((cosT, c_cos), (sinT, c_sin), (nsinT, c_nsin))]
            T.dma(pool, s_cs, fcs, writes=[b_cs], n=3)
            if DEBUG_STOP >= 1:
                for s in range(ns):
                    transpose_to_hT(s, L)
            for l in range(depth):
                if DEBUG_STOP < 2:
                    break
                convert_layer(l + 1)
                load_lp(l)
                load_ln(l, 0)
                ffn(tile, "w1u", "w1d", l)
                if DEBUG_STOP < 3:
                    break
                layer_norm_all(tile, l, 0)
                if DEBUG_STOP < 4:
                    break
                load_ln(l, 1)
                mixers(tile, l)
                if DEBUG_STOP < 5:
                    break
                layer_norm_all(tile, l, 1)
                load_ln(l, 2)
                ffn(tile, "w2u", "w2d", l)
                layer_norm_all(tile, l, 2)
            if tile["y"] is not None:
                T.dma(pool, s_y, lambda e: e.dma_start(out=tile["y"].rearrange("(s p) d -> p s d", p=L), in_=h[0:L, 0:ns, :]),
                      reads=[b for s in range(ns) for b in b_h[s]], is_out=True)

        def make_state_fns(src_g, src_r, dst_g, dst_r, out_g, out_r):
            def load(l, which):
                S, b_S, sem = (Sg, b_Sg, s_stl_g) if which == "g" else (Sr, b_Sr, s_stl_r)
                src = (src_g if which == "g" else src_r)
                if src is None:
                    T.op(dve, lambda e: e.memset(S[:].rearrange("p h e -> p (h e)"), 0.0), writes=[b_S])
                    return
                ap, bb = src(l)
                T.dma(pool, sem, lambda e: e.dma_start(out=S[:], in_=ap.rearrange("h d e -> d h e")),
                      reads=([bb] if bb is not None else []), writes=[b_S])

            def store(l, which):
                if NOSTORE:
                    return
                S, b_S, sem = (Sg, b_Sg, s_sts_g) if which == "g" else (Sr, b_Sr, s_sts_r)
                ap, bb = (dst_g if which == "g" else dst_r)(l)
                is_out = out_g if which == "g" else out_r
                T.dma(pool, sem, lambda e: e.dma_start(out=ap.rearrange("h d e -> d h e"), in_=S[:]),
                      reads=[b_S], writes=([bb] if bb is not None else []), is_out=is_out)
            return load, store

        load_consts()
        convert_layer(0)
        if with_meta:
            ld, stf = make_state_fns(None, None, lambda l: (smeta_g[l], b_smg[l]), lambda l: (smeta_r[l], b_smr[l]), False, False)
            run_tile(dict(ns=1, L=16, ntok=16, x=meta, y=None, pos0=0, state_load=ld, state_store=stf))
        if with_sample:
            ld, stf = make_state_fns(lambda l: (sg_in[l], None), lambda l: (sr_in[l], None),
                                     lambda l: (gs[l], None), lambda l: (rs[l], None), True, True)
            run_tile(dict(ns=1, L=32, ntok=32, x=xs, y=ys, pos0=N_META + PAST_LEN, state_load=ld, state_store=stf))
        for q in range(n_seq):
            for k in range(tiles_per_seq):
                first, last = (k == 0), (k == tiles_per_seq - 1)
                if first:
                    sgf = (lambda l: (smeta_g[l], b_smg[l])) if with_meta else None
                    srf = (lambda l: (smeta_r[l], b_smr[l])) if with_meta else None
                else:
                    sgf = lambda l: (scr_g[l], b_scg[l])
                    srf = lambda l: (scr_r[l], b_scr[l])
                if last:
                    dgf = lambda l, q=q: (gp[l, q], None)
                    drf = lambda l, q=q: (rp[l, q], None)
                else:
                    dgf = lambda l: (scr_g[l], b_scg[l])
                    drf = lambda l: (scr_r[l], b_scr[l])
                ld, stf = make_state_fns(sgf, srf, dgf, drf, last, last)
                run_tile(dict(ns=nsub, L=128, ntok=NT, x=xp[q, k * NT:(k + 1) * NT, :], y=yp[q, k * NT:(k + 1) * NT, :],
                              pos0=N_META + k * NT, state_load=ld, state_store=stf))
        T.finish()
        T.replay()
        global LAST_TRACKER
        LAST_TRACKER = T
    return nc


def _const_inputs():
    idf = np.eye(128, dtype=np.float32)
    s_idx = np.arange(128)[:, None]
    t_idx = np.arange(128)[None, :]
    mask = (t_idx >= s_idx).astype(np.float32)
    tri = (s_idx > t_idx).astype(np.float32) / np.float32(16.0)
    half = 64
    inv = (np.float32(10000.0) ** (-np.arange(half, dtype=np.float32) / np.float32(half))).astype(np.float32)
    pos = np.arange(N_META + 2048, dtype=np.float32)
    ang = (pos[:, None] * inv[None, :]).astype(np.float32)
    cos = np.cos(ang).astype(np.float32)
    sin = np.sin(ang).astype(np.float32)
    log_gamma = np.log1p(-(2.0 ** (-5.0 - np.arange(4, dtype=np.float64))))
    c1T = np.ones((8, 176), np.float32)
    c2 = np.zeros((128, 3, 4), np.float32)
    decr = np.zeros((3, 4), np.float32)
    for L in LVARS:
        t = np.arange(L, dtype=np.float64)
        c1T[0:4, LOFF[L]:LOFF[L] + L] = np.exp((t[None, :] - (L - 1.0)) * log_gamma[:, None])
        c2[:L, LIDX[L], :] = np.exp((L - 1.0 - t)[:, None] * log_gamma[None, :]) * (128.0 ** -0.5)
        decr[LIDX[L], :] = np.exp(L * log_gamma)
    return dict(c_idf=idf, c_mask=mask, c_tri=tri, c_cos=cos, c_sin=sin, c_nsin=(-sin).astype(np.float32),
                c_c1T=c1T, c_c2=c2, c_decr=decr)


def run_cores(inputs, n_cores, n_seq, seq_len, depth, nsub, with_meta=True, with_sample=True, trace=False):
    f = lambda a: np.ascontiguousarray(np.asarray(a, dtype=np.float32))
    nc = build_program(n_seq, seq_len, depth, nsub, with_meta, with_sample)
    consts = _const_inputs()
    shared = dict(
        meta=f(inputs["meta"]), ln_g=f(inputs["ln_g"][:depth]), ln_b=f(inputs["ln_b"][:depth]),
        w1u=f(inputs["w_ffn1_up"][:depth]), w1d=f(inputs["w_ffn1_down"][:depth]), w_in=f(inputs["w_in"][:depth]),
        w_a2=f(inputs["w_alpha2"][:depth]), b_a=f(inputs["b_alpha"][:depth]),
        bm_t=f(np.asarray(inputs["b_merge"][:depth]).reshape(depth, 16, 128).transpose(2, 0, 1)),
        gng_t=f(np.asarray(inputs["gn_gla"][:depth]).reshape(depth, 8, 128).transpose(2, 0, 1)),
        gnr_t=f(np.asarray(inputs["gn_ret"][:depth]).reshape(depth, 8, 128).transpose(2, 0, 1)),
        lng_t=f(np.asarray(inputs["ln_g"][:depth]).reshape(depth, 3, 8, 128).transpose(3, 0, 1, 2)),
        lnb_t=f(np.asarray(inputs["ln_b"][:depth]).reshape(depth, 3, 8, 128).transpose(3, 0, 1, 2)),
        w_oa=f(inputs["w_o_gla"][:depth]), w_ob=f(inputs["w_o_ret"][:depth]), w_o=f(inputs["w_out"][:depth]),
        w2u=f(inputs["w_ffn2_up"][:depth]), w2d=f(inputs["w_ffn2_down"][:depth]),
    )
    shared.update(consts)
    xpr = np.asarray(inputs["x_prompt"])
    xsm = np.asarray(inputs["x_sample"])
    sgl = np.asarray(inputs["state_gla"])
    srt = np.asarray(inputs["state_ret"])
    in_maps = []
    for c in range(n_cores):
        m = dict(shared)
        m["xp"] = f(xpr[c * n_seq:(c + 1) * n_seq, :seq_len])
        m["xs"] = f(xsm[c])
        m["sg_in"] = f(sgl[:depth, c])
        m["sr_in"] = f(srt[:depth, c])
        in_maps.append(m)
    res = run_bass_kernel_spmd(nc, in_maps, core_ids=list(range(n_cores)), trace=trace)
    R = res.results
    y_prompt = np.concatenate([r["yp"] for r in R], axis=0)
    y_sample = np.stack([r["ys"] for r in R], axis=0)
    gla_p = np.concatenate([r["gp"] for r in R], axis=1)
    ret_p = np.concatenate([r["rp"] for r in R], axis=1)
    gla_s = np.stack([r["gs"] for r in R], axis=1)
    ret_s = np.stack([r["rs"] for r in R], axis=1)
    return (y_prompt, y_sample, gla_p, ret_p, gla_s, ret_s), res


def kernel(**inputs):
    outs, _ = run_cores(inputs, n_cores=8, n_seq=4, seq_len=2048, depth=DEPTH, nsub=4)
    return tuple(np.ascontiguousarray(o.astype(np.float32, copy=False)) for o in outs)
```

```python
from contextlib import ExitStack
import math
import numpy as np
import concourse.bass as bass
import concourse.mybir as mybir
from concourse.bass_utils import run_bass_kernel_spmd

F32 = mybir.dt.float32
BF16 = mybir.dt.bfloat16
AF = mybir.ActivationFunctionType
ALU = mybir.AluOpType

D = 1024
DFF = 2816
NIN = 8208
DEPTH = 4
N_META = 16
PAST_LEN = 1024
LN_EPS = 1e-5
GN_EPS = 1e-5
DN_ALPHA = (2.0 * DEPTH) ** 0.25
LVARS = (128, 16, 32)
LOFF = {128: 0, 16: 128, 32: 144}
LIDX = {128: 0, 16: 1, 32: 2}
NSLOT = 8
DEBUG_STOP = 99
MIX_STOP = 99
CORE_STOP = 99
NOSTORE = 0
SAME_SYNC = True
PROFILE_LOG = False
LAST_TRACKER = None


class Sem:
    __slots__ = ("h", "val")

    def __init__(self, h):
        self.h = h
        self.val = 0


class Buf:
    __slots__ = ("w", "r", "name", "x")

    def __init__(self, name="", x=False):
        self.w = None
        self.r = {}
        self.name = name
        self.x = x


class Eng:
    def __init__(self, name, sem, same_sync):
        self.name = name
        self.sem = sem
        self.waited = {}
        self.prog = []
        self.same_sync = same_sync


class Tracker:
    def __init__(self, nc, stack):
        self.nc = nc
        self.stack = stack
        self.pe = Eng("pe", self.new_sem("s_pe"), False)
        self.act = Eng("act", self.new_sem("s_act"), SAME_SYNC)
        self.dve = Eng("dve", self.new_sem("s_dve"), SAME_SYNC)
        self.pool = Eng("pool", self.new_sem("s_pool"), SAME_SYNC)
        self.sp = Eng("sp", self.new_sem("s_sp"), False)
        self.out_sems = []
        self.tag = ""
        self.pe_log = []

    def new_sem(self, name):
        return Sem(self.stack.enter_context(self.nc.semaphore(name)))

    def _deps(self, eng, reads, writes):
        deps = {}
        for b in reads:
            if b.w is not None:
                s, v = b.w
                if deps.get(s, 0) < v:
                    deps[s] = v
            if b.x:
                for s, v in b.r.items():
                    if deps.get(s, 0) < v:
                        deps[s] = v
        for b in writes:
            if b.w is not None:
                s, v = b.w
                if deps.get(s, 0) < v:
                    deps[s] = v
            for s, v in b.r.items():
                if deps.get(s, 0) < v:
                    deps[s] = v
        for s, v in deps.items():
            if s is eng.sem and not eng.same_sync:
                continue
            if eng.waited.get(s, 0) < v:
                eng.prog.append((0, s, v))
                eng.waited[s] = v

    def op(self, eng, fn, reads=(), writes=()):
        self._deps(eng, reads, writes)
        s = eng.sem
        s.val += 1
        tok = (s, s.val)
        eng.prog.append((1, fn, s, 1, self.tag))
        for b in writes:
            b.w = tok
            b.r = {}
        for b in reads:
            b.r[s] = s.val
        return tok

    def dma(self, qeng, sem, fn, reads=(), writes=(), n=1, is_out=False):
        self._deps(qeng, reads, writes)
        sem.val += 16 * n
        tok = (sem, sem.val)
        qeng.prog.append((1, fn, sem, 16))
        for b in writes:
            b.w = tok
            b.r = {}
        for b in reads:
            b.r[sem] = sem.val
        if is_out and sem not in self.out_sems:
            self.out_sems.append(sem)
        return tok

    def finish(self):
        for s in self.out_sems:
            self.sp.prog.append((0, s, s.val))

    def replay(self):
        def run(prog, e):
            for it in prog:
                if it[0] == 0:
                    e.wait_ge(it[1].h, it[2])
                else:
                    if PROFILE_LOG and len(it) > 4 and prog is self.pe.prog:
                        n0 = self.nc.n_instructions() if callable(self.nc.n_instructions) else self.nc.n_instructions
                    r = it[1](e)
                    if PROFILE_LOG and len(it) > 4 and prog is self.pe.prog:
                        n1 = self.nc.n_instructions() if callable(self.nc.n_instructions) else self.nc.n_instructions
                        self.pe_log.append((it[4], n1 - n0))
                    if isinstance(r, (list, tuple)):
                        for x in r:
                            x.then_inc(it[2].h, it[3])
                    else:
                        r.then_inc(it[2].h, it[3])

        with self.nc.Block() as block:
            @block.tensor
            def _(e):
                run(self.pe.prog, e)

            @block.scalar
            def _(e):
                run(self.act.prog, e)

            @block.vector
            def _(e):
                run(self.dve.prog, e)

            @block.gpsimd
            def _(e):
                run(self.pool.prog, e)

            @block.sync
            def _(e):
                run(self.sp.prog, e)


def build_program(n_seq, seq_len, depth, nsub, with_meta=True, with_sample=True):
    NT = 128 * nsub
    assert seq_len % NT == 0
    tiles_per_seq = seq_len // NT
    nc = bass.Bass("TRN2", target_bir_lowering=False)

    def din(name, shape):
        return nc.dram_tensor(name, list(shape), F32, kind="ExternalInput").ap()

    def dout(name, shape):
        return nc.dram_tensor(name, list(shape), F32, kind="ExternalOutput").ap()

    def dscr(name, shape):
        return nc.dram_tensor(name, list(shape), F32).ap()

    xp = din("xp", [n_seq, seq_len, D])
    xs = din("xs", [32, D])
    sg_in = din("sg_in", [depth, 4, 128, 256])
    sr_in = din("sr_in", [depth, 4, 128, 256])
    meta = din("meta", [N_META, D])
    ln_g = din("ln_g", [depth, 3, D])
    ln_b = din("ln_b", [depth, 3, D])
    w1u = din("w1u", [depth, D, 2 * DFF])
    w1d = din("w1d", [depth, DFF, D])
    w_in = din("w_in", [depth, D, NIN])
    w_a2 = din("w_a2", [depth, 16, 512])
    b_a = din("b_a", [depth, 512])
    bm_t = din("bm_t", [128, depth, 16])
    gng_t = din("gng_t", [128, depth, 8])
    gnr_t = din("gnr_t", [128, depth, 8])
    w_oa = din("w_oa", [depth, D, D])
    w_ob = din("w_ob", [depth, D, D])
    w_o = din("w_o", [depth, D, D])
    w2u = din("w2u", [depth, D, 2 * DFF])
    w2d = din("w2d", [depth, DFF, D])
    c_idf = din("c_idf", [128, 128])
    c_mask = din("c_mask", [128, 128])
    c_tri = din("c_tri", [128, 128])
    c_cos = din("c_cos", [N_META + 2048, 64])
    c_sin = din("c_sin", [N_META + 2048, 64])
    c_nsin = din("c_nsin", [N_META + 2048, 64])
    c_c1T = din("c_c1T", [8, 176])
    lng_t = din("lng_t", [128, depth, 3, 8])
    lnb_t = din("lnb_t", [128, depth, 3, 8])
    c_c2 = din("c_c2", [128, 3, 4])
    c_decr = din("c_decr", [3, 4])

    yp = dout("yp", [n_seq, seq_len, D])
    ys = dout("ys", [32, D])
    gp = dout("gp", [depth, n_seq, 4, 128, 256])
    rp = dout("rp", [depth, n_seq, 4, 128, 256])
    gs = dout("gs", [depth, 4, 128, 256])
    rs = dout("rs", [depth, 4, 128, 256])

    WSH = {"w1u": (D, 2 * DFF), "w1d": (DFF, D), "w_in": (D, NIN), "w_oa": (D, D), "w_ob": (D, D), "w_o": (D, D),
           "w2u": (D, 2 * DFF), "w2d": (DFF, D)}
    wfp = {"w1u": w1u, "w1d": w1d, "w_in": w_in, "w_oa": w_oa, "w_ob": w_ob, "w_o": w_o, "w2u": w2u, "w2d": w2d}
    wsc = {k: nc.dram_tensor(k + "_bf", [depth, r, c], BF16).ap() for k, (r, c) in WSH.items()}

    def wsrc(name, l):
        return wsc[name][l]

    smeta_g = dscr("smeta_g", [depth, 4, 128, 256])
    smeta_r = dscr("smeta_r", [depth, 4, 128, 256])
    scr_g = dscr("scr_g", [depth, 4, 128, 256])
    scr_r = dscr("scr_r", [depth, 4, 128, 256])

    with ExitStack() as st:
        T = Tracker(nc, st)
        pe, act, dve, pool, sp = T.pe, T.act, T.dve, T.pool, T.sp

        def sb(name, shape, dt=F32):
            return st.enter_context(nc.sbuf_tensor(name, list(shape), dt))

        idf = sb("idf", [128, 128])
        idb = sb("idb", [128, 128], BF16)
        mask = sb("mask", [128, 128])
        tri = sb("tri", [128, 128])
        neg16 = sb("neg16", [128, 1])
        mhalf = sb("mhalf", [128, 8])
        c1T = sb("c1T", [128, 8, 176])
        c2 = sb("c2", [128, 3, 4])
        decr = sb("decr", [128, 3, 4])
        ba_bc = sb("ba_bc", [128, 512])
        wa2 = sb("wa2", [16, 512], BF16)
        G = [sb(f"G{i}", [128, 512]) for i in range(5)]
        bm = sb("bm", [128, depth, 16])
        gng = sb("gng", [128, depth, 8])
        gnr = sb("gnr", [128, depth, 8])
        lng = sb("lng", [128, D])
        lnb = sb("lnb", [128, D])
        cosT = sb("cosT", [128, nsub, 64])
        sinT = sb("sinT", [128, nsub, 64])
        nsinT = sb("nsinT", [128, nsub, 64])
        h = sb("h", [128, nsub, D])
        hT = sb("hT", [128, 8, NT], BF16)
        ring = [sb(f"ring{i}", [128, 4096], BF16) for i in range(NSLOT)]
        satmp = [G[0], G[1]]
        agT = sb("agT", [16, NT], BF16)
        xb, ee, ltok, E1, E2 = G
        decg = sb("decg", [128, nsub, 4])
        qt = sb("qt", [128, nsub, 512], BF16)
        kt = sb("kt", [128, nsub, 512], BF16)
        assert nsub == 4, "buffer aliasing below assumes 512-token tiles"
        vv = sb("vv", [128, nsub, 1024], BF16)
        sgate = sb("sgate", [128, nsub, 1024], BF16)
        vv2 = sb("vv2", [128, nsub, 1024], BF16)
        sgate2 = sb("sgate2", [128, nsub, 1024], BF16)
        gT = [sgate[:, 2 * i:2 * i + 2, :].rearrange("p s (j t) -> p (s j) t", j=2) for i in range(2)]
        mT = vv[:, :, :].rearrange("p s (j t) -> p (s j) t", j=2)
        zc, za, ztmp, rot = G[0:4]
        Sbf = sb("Sbf", [128, 4, 256], BF16)
        attb = sb("attb", [128, 4, 128], BF16)
        on = sb("on", [128, 4, 256])
        yin = sb("yin", [128, 1024], BF16)
        hst = sb("hst", [128, 4, 6])
        hmv = sb("hmv", [128, 4, 2])
        hve = sb("hve", [128, 4])
        hrs = sb("hrs", [128, 4])
        yinTa = sb("yinTa", [128, 8, NT], BF16)
        yinTb = sb("yinTb", [128, 8, NT], BF16)
        sgA, sgB, t1, t2 = G[0:4]
        Sg = sb("Sg", [128, 4, 256])
        Sr = sb("Sr", [128, 4, 256])
        lst = [sb(f"lst{i}", [128, 2, 6]) for i in range(nsub)]
        lmv = [sb(f"lmv{i}", [128, 2]) for i in range(nsub)]
        lve = [sb(f"lve{i}", [128, 1]) for i in range(nsub)]
        lrs = [sb(f"lrs{i}", [128, 1]) for i in range(nsub)]
        lnm = [sb(f"lnm{i}", [128, 1]) for i in range(nsub)]
        hb = [sb(f"hb{i}", [128, D], BF16) for i in range(nsub)]
        lngT = sb("lngT", [128, depth, 3, 8])
        lnbT = sb("lnbT", [128, depth, 3, 8])
        hnm = sb("hnm", [128, 4])
        qkT = sb("qkT", [128, 8, 128], BF16)

        ps = st.enter_context(nc.psum_tensor("ps", [128, 8, 512], F32))
        psb = ps.bitcast(BF16)

        b_const = Buf("const")
        b_w = {k: [Buf(f"w_{k}{l}") for l in range(depth)] for k in WSH}
        b_lp = Buf("lp")
        b_ln = Buf("lnp")
        b_cs = Buf("cossin")
        b_h = [[Buf(f"h{s}{j}") for j in range(2)] for s in range(nsub)]
        b_hT = [[Buf(f"hT{s}_{k}") for k in range(8)] for s in range(nsub)]

        def flat(xs):
            out = []
            for x in xs:
                if isinstance(x, list):
                    out.extend(x)
                else:
                    out.append(x)
            return out
        b_ring = [Buf(f"ring{i}") for i in range(NSLOT)]
        b_G = [Buf(f"G{i}") for i in range(5)]
        b_sa = [b_G[0], b_G[1]]
        b_ps = [Buf(f"ps{i}", x=True) for i in range(8)]
        b_agT = Buf("agT")
        b_xb, b_ee, b_ltok, b_E1, b_E2 = b_G
        b_decg = [Buf(f"decg{s}") for s in range(nsub)]
        b_qt = [Buf(f"qt{s}") for s in range(nsub)]
        b_kt = [Buf(f"kt{s}") for s in range(nsub)]
        b_vv = [[Buf(f"vv{s}{j}") for j in range(2)] for s in range(nsub)]
        b_sg = [[Buf(f"sg{s}{j}") for j in range(2)] for s in range(nsub)]
        b_vv2 = [[Buf(f"vw{s}{j}") for j in range(2)] for s in range(nsub)]
        b_sg2 = [[Buf(f"sh{s}{j}") for j in range(2)] for s in range(nsub)]
        b_gT = [[b_sg[2 * i + c // 2][c % 2] for c in range(4)] for i in range(2)]
        b_zc, b_za, b_ztmp, b_rot = b_G[0:4]
        b_Sbf, b_qTt, b_kTt, b_attb, b_on, b_yin = Buf("Sbf"), Buf("qTt"), Buf("kTt"), Buf("attb"), Buf("on"), Buf("yin")
        b_hst, b_hmv, b_hve, b_hrs = Buf("hst"), Buf("hmv"), Buf("hve"), Buf("hrs")
        b_yTa = [Buf(f"yTa{s}") for s in range(nsub)]
        b_yTb = [Buf(f"yTb{s}") for s in range(nsub)]
        b_mT = [b_vv[c // 2][c % 2] for c in range(8)]
        b_sgA, b_sgB, b_t1, b_t2 = b_G[0:4]
        b_Sg, b_Sr = Buf("Sg"), Buf("Sr")
        b_lst = [Buf(f"lst{i}") for i in range(nsub)]
        b_lmv = [Buf(f"lmv{i}") for i in range(nsub)]
        b_lve = [Buf(f"lve{i}") for i in range(nsub)]
        b_lrs = [Buf(f"lrs{i}") for i in range(nsub)]
        b_lnm = [Buf(f"lnm{i}") for i in range(nsub)]
        b_hb = [Buf(f"hb{i}") for i in range(nsub)]
        b_hnm, b_qkT = Buf("hnm"), Buf("qkT")
        b_smg = [Buf(f"smg{l}") for l in range(depth)]
        b_smr = [Buf(f"smr{l}") for l in range(depth)]
        b_scg = [Buf(f"scg{l}") for l in range(depth)]
        b_scr = [Buf(f"scr{l}") for l in range(depth)]

        s_ring = [T.new_sem(f"d_ring{i}") for i in range(NSLOT)]
        s_const = T.new_sem("d_const")
        s_cv = {k: [T.new_sem(f"d_cv_{k}{l}") for l in range(depth)] for k in WSH}
        s_lp = T.new_sem("d_lp")
        s_x = T.new_sem("d_x")
        s_y = T.new_sem("d_y")
        s_ln = T.new_sem("d_ln")
        s_cs = T.new_sem("d_cs")
        s_stl_g, s_stl_r = T.new_sem("d_stlg"), T.new_sem("d_stlr")
        s_sts_g, s_sts_r = T.new_sem("d_stsg"), T.new_sem("d_stsr")

        pstate = {"i": 0}

        def palloc():
            i = pstate["i"]
            pstate["i"] = (i + 1) % 8
            return i

        rstate = {"i": 0}

        def wload(src, a, b, dep):
            i = rstate["i"]
            rstate["i"] = (i + 1) % NSLOT
            view = ring[i][:, 0:a * b].rearrange("p (a b) -> p a b", a=a)
            T.dma(sp, s_ring[i], lambda e, view=view, src=src: e.dma_start(out=view, in_=src),
                  reads=[dep], writes=[b_ring[i]])
            return view, b_ring[i]

        def load_consts():
            def f(e):
                r = [
                    e.dma_start(out=idf[:], in_=c_idf[:, :]),
                    e.dma_start(out=mask[:], in_=c_mask[:, :]),
                    e.dma_start(out=tri[:], in_=c_tri[:, :]),
                    e.dma_start(out=c1T[:], in_=c_c1T.partition_broadcast(128)),
                    e.dma_start(out=c2[:], in_=c_c2[:, :, :]),
                    e.dma_start(out=decr[:], in_=c_decr.partition_broadcast(128)),
                    e.dma_start(out=bm[:], in_=bm_t[:, :, :]),
                    e.dma_start(out=gng[:], in_=gng_t[:, :, :]),
                    e.dma_start(out=gnr[:], in_=gnr_t[:, :, :]),
                    e.dma_start(out=lngT[:], in_=lng_t[:, :, :, :]),
                    e.dma_start(out=lnbT[:], in_=lnb_t[:, :, :, :]),
                ]
                return r
            T.dma(pool, s_const, f, writes=[b_const], n=11)
            T.op(dve, lambda e: e.tensor_copy(out=idb[:], in_=idf[:]), reads=[b_const], writes=[b_const])
            T.op(dve, lambda e: e.memset(neg16[:], -1.0 / 16.0), writes=[b_const])
            T.op(dve, lambda e: e.memset(mhalf[:], -0.5), writes=[b_const])

        converted = set()

        def convert_layer(l):
            if l >= depth or l in converted:
                return
            converted.add(l)
            if True:
                for name in ("w1u", "w1d", "w_in", "w_oa", "w_ob", "w_o", "w2u", "w2d"):
                    src, dst = wfp[name][l], wsc[name][l]
                    nchunk = WSH[name][0] // 128

                    def f(e, src=src, dst=dst, nchunk=nchunk):
                        return [e.dma_start(out=dst[c * 128:(c + 1) * 128, :], in_=src[c * 128:(c + 1) * 128, :])
                                for c in range(nchunk)]
                    T.dma(pool, s_cv[name][l], f, writes=[b_w[name][l]], n=nchunk)

        def rstd_pow(out_ap, in_ap, nrow, ncol, rb, wb):
            T.op(pool, lambda e: e.tensor_tensor(out=out_ap, in0=in_ap, in1=mhalf[0:nrow, 0:ncol], op=ALU.pow),
                 reads=[rb, b_const], writes=[wb])

        def transpose_to_hT(s, L, aff=None):
            p0, p1 = palloc(), palloc()
            if aff is None:
                def f(e):
                    r = None
                    for k in range(8):
                        bank = p0 if k < 4 else p1
                        r = e.transpose(ps[:, bank, (k % 4) * 128:(k % 4) * 128 + L],
                                        h[0:L, s, k * 128:(k + 1) * 128], idf[0:L, 0:L])
                    return r
                T.op(pe, f, reads=[b_h[s][0], b_h[s][1], b_const], writes=[b_ps[p0], b_ps[p1]])
            else:
                i = s

                def f(e):
                    r = None
                    for k in range(8):
                        bank = p0 if k < 4 else p1
                        r = e.transpose(psb[:, bank, (k % 4) * 128:(k % 4) * 128 + L],
                                        hb[i][0:L, k * 128:(k + 1) * 128], idb[0:L, 0:L])
                    return r
                T.op(pe, f, reads=[b_hb[i], b_const], writes=[b_ps[p0], b_ps[p1]])
            if aff is None:
                T.op(act, lambda e: e.copy(out=hT[:, 0:4, s * 128:s * 128 + L],
                                            in_=ps[:, p0, :].rearrange("p (k t) -> p k t", k=4)[:, :, 0:L]),
                     reads=[b_ps[p0]], writes=b_hT[s][0:4])
                T.op(dve, lambda e: e.tensor_copy(out=hT[:, 4:8, s * 128:s * 128 + L],
                                                   in_=ps[:, p1, :].rearrange("p (k t) -> p k t", k=4)[:, :, 0:L]),
                     reads=[b_ps[p1]], writes=b_hT[s][4:8])
                return
            l, idx = aff
            for k in range(8):
                bank = p0 if k < 4 else p1
                src = psb[:, bank, (k % 4) * 128:(k % 4) * 128 + L]
                dst = hT[:, k, s * 128:s * 128 + L]
                if k < 4:
                    T.op(act, lambda e, src=src, dst=dst, k=k: e.activation(
                        out=dst, in_=src, func=AF.Identity, scale=lngT[:, l, idx, k:k + 1], bias=lnbT[:, l, idx, k:k + 1]),
                        reads=[b_ps[bank], b_const], writes=[b_hT[s][k]])
                else:
                    T.op(dve, lambda e, src=src, dst=dst, k=k: e.tensor_scalar(
                        out=dst, in0=src, scalar1=lngT[:, l, idx, k:k + 1], scalar2=lnbT[:, l, idx, k:k + 1],
                        op0=ALU.mult, op1=ALU.add),
                        reads=[b_ps[bank], b_const], writes=[b_hT[s][k]])

        def load_lp(l):
            T.dma(pool, s_lp, lambda e: [e.dma_start(out=ba_bc[:], in_=b_a[l].partition_broadcast(128)),
                                         e.dma_start(out=wa2[:], in_=w_a2[l])], writes=[b_lp], n=2)

        def load_ln(l, idx):
            def f(e):
                return [e.dma_start(out=lng[:], in_=ln_g[l, idx].partition_broadcast(128)),
                        e.dma_start(out=lnb[:], in_=ln_b[l, idx].partition_broadcast(128))]
            T.dma(pool, s_ln, f, writes=[b_ln], n=2)

        def layer_norm_all(tile, l, idx):
            ns, L = tile["ns"], tile["L"]
            T.tag = "ln"
            for s in range(ns):
                transpose_to_hT(s, L, aff=(l, idx))
            for s in range(ns):
                hs = h[0:L, s, :]
                T.op(pool, lambda e, hs=hs: e.tensor_tensor(out=hs, in0=hs, in1=lng[0:L, :], op=ALU.mult),
                     reads=[b_h[s][0], b_h[s][1], b_ln], writes=[b_h[s][0], b_h[s][1]])
                T.op(pool, lambda e, hs=hs: e.tensor_tensor(out=hs, in0=hs, in1=lnb[0:L, :], op=ALU.add),
                     reads=[b_h[s][0], b_h[s][1], b_ln], writes=[b_h[s][0], b_h[s][1]])

        def ln_front(s, L):
            if True:
                i = s
                hs = h[0:L, s, :]
                T.op(dve, lambda e, i=i, s=s: e.bn_stats(out=lst[i][0:L, 0, :], in_=h[0:L, s, 0:512]), reads=[b_h[s][0]], writes=[b_lst[i]])
                T.op(dve, lambda e, i=i, s=s: e.bn_stats(out=lst[i][0:L, 1, :], in_=h[0:L, s, 512:1024]), reads=[b_h[s][1]], writes=[b_lst[i]])
                T.op(dve, lambda e, i=i: e.bn_aggr(out=lmv[i][0:L, :], in_=lst[i][0:L, :, :]), reads=[b_lst[i]], writes=[b_lmv[i]])
                T.op(dve, lambda e, i=i: e.tensor_scalar_add(out=lve[i][0:L, :], in0=lmv[i][0:L, 1:2], scalar1=LN_EPS),
                     reads=[b_lmv[i]], writes=[b_lve[i]])
                rstd_pow(lrs[i][0:L, :], lve[i][0:L, :], L, 1, b_lve[i], b_lrs[i])
                T.op(dve, lambda e, i=i: e.scalar_tensor_tensor(out=lnm[i][0:L, :], in0=lmv[i][0:L, 0:1], scalar=-1.0, in1=lrs[i][0:L, :],
                                                                 op0=ALU.mult, op1=ALU.mult),
                     reads=[b_lmv[i], b_lrs[i]], writes=[b_lnm[i]])
                T.op(act, lambda e, i=i, hs=hs: e.activation(out=hb[i][0:L, :], in_=hs, func=AF.Identity, scale=lrs[i][0:L, :], bias=lnm[i][0:L, :]),
                     reads=[b_h[s][0], b_h[s][1], b_lrs[i], b_lnm[i]], writes=[b_hb[i]])
                T.op(act, lambda e, i=i, hs=hs: e.activation(out=hs, in_=hs, func=AF.Identity, scale=lrs[i][0:L, :], bias=lnm[i][0:L, :]),
                     reads=[b_h[s][0], b_h[s][1], b_lrs[i], b_lnm[i]], writes=[b_h[s][0], b_h[s][1]])

        def residual_add(s, L, hf, pbank, first):
            hh = h[0:L, s, hf * 512:(hf + 1) * 512]
            if first:
                T.op(dve, lambda e: e.scalar_tensor_tensor(out=hh, in0=hh, scalar=DN_ALPHA, in1=ps[0:L, pbank, :],
                                                            op0=ALU.mult, op1=ALU.add),
                     reads=[b_h[s][hf], b_ps[pbank]], writes=[b_h[s][hf]])
            else:
                T.op(dve, lambda e: e.tensor_tensor(out=hh, in0=hh, in1=ps[0:L, pbank, :], op=ALU.add),
                     reads=[b_h[s][hf], b_ps[pbank]], writes=[b_h[s][hf]])

        def ffn(tile, nu, nd, l):
            T.tag = "ffn"
            ns, L = tile["ns"], tile["L"]
            ntok = tile["ntok"]
            wu_v = wsrc(nu, l).rearrange("(k p) n -> p k n", p=128)
            wd_v = wsrc(nd, l).rearrange("(c p) n -> p c n", p=128)
            du, dd = b_w[nu][l], b_w[nd][l]
            groups = [(0, 512), (512, 512), (1024, 512), (1536, 512), (2048, 512), (2560, 256)]
            for gi, (f0, fw) in enumerate(groups):
                nch = fw // 128
                wa_v, wa_b = wload(wu_v[:, :, f0:f0 + fw], 8, fw, du)
                wb_v, wb_b = wload(wu_v[:, :, DFF + f0:DFF + f0 + fw], 8, fw, du)
                wd_s, wd_b = wload(wd_v[:, f0 // 128:f0 // 128 + nch, :], nch, 1024, dd)
                gb = gi % 2
                for c in range(nch):
                    pa, pb = palloc(), palloc()

                    def fup(e, w_v=wa_v, bank=pa, c=c):
                        r = None
                        for k in range(8):
                            r = e.matmul(ps[:, bank, 0:ntok], lhsT=w_v[:, k, c * 128:(c + 1) * 128],
                                         rhs=hT[:, k, 0:ntok], start=(k == 0), stop=(k == 7))
                        return r
                    T.op(pe, fup, reads=[wa_b] + flat(b_hT[0:ns]), writes=[b_ps[pa]])

                    def fupb(e, w_v=wb_v, bank=pb, c=c):
                        r = None
                        for k in range(8):
                            r = e.matmul(ps[:, bank, 0:ntok], lhsT=w_v[:, k, c * 128:(c + 1) * 128],
                                         rhs=hT[:, k, 0:ntok], start=(k == 0), stop=(k == 7))
                        return r
                    T.op(pe, fupb, reads=[wb_b] + flat(b_hT[0:ns]), writes=[b_ps[pb]])
                    si = c % 2
                    T.op(act, lambda e, pa=pa, si=si: e.activation(out=satmp[si][:, 0:ntok], in_=ps[:, pa, 0:ntok], func=AF.Silu),
                         reads=[b_ps[pa]], writes=[b_sa[si]])
                    T.op(dve, lambda e, pb=pb, si=si, c=c, gb=gb: e.scalar_tensor_tensor(
                        out=gT[gb][:, c, 0:ntok], in0=satmp[si][:, 0:ntok], scalar=0.5, in1=ps[:, pb, 0:ntok],
                        op0=ALU.mult, op1=ALU.mult),
                        reads=[b_sa[si], b_ps[pb]], writes=[b_gT[gb][c]])
                for s in range(ns):
                    for hf in range(2):
                        py = palloc()

                        def fdn(e, s=s, hf=hf, py=py, nch=nch, gb=gb, wd_s=wd_s):
                            r = None
                            for c in range(nch):
                                r = e.matmul(ps[0:L, py, :], lhsT=gT[gb][:, c, s * 128:s * 128 + L],
                                             rhs=wd_s[:, c, hf * 512:(hf + 1) * 512], start=(c == 0), stop=(c == nch - 1))
                            return r
                        T.op(pe, fdn, reads=[wd_b] + b_gT[gb][0:nch], writes=[b_ps[py]])
                        residual_add(s, L, hf, py, first=(gi == 0))
                    if gi == len(groups) - 1:
                        ln_front(s, L)
                        T.tag = "ffn"

        def core_A(s, L, dec_ap, dec_buf, S, b_S, is_ret, vt, b_vt):
            pt, pa, pk0, pk1 = 0, 1, 2, 3
            po0, po1 = (4, 5) if s % 2 == 0 else (6, 7)
            T.op(pool, lambda e: e.tensor_tensor(out=Sbf[:], in0=S[:], in1=dec_ap.rearrange("p (h o) -> p h o", o=1).to_broadcast([128, 4, 256]),
                                                  op=ALU.mult),
                 reads=[b_S, dec_buf], writes=[b_Sbf])

            def ftr(e):
                r = None
                for hh in range(4):
                    r = e.transpose(psb[:, pt, hh * 128:hh * 128 + L], qt[0:L, s, hh * 128:(hh + 1) * 128], idb[0:L, 0:L])
                for hh in range(4):
                    r = e.transpose(psb[:, pt, 512 + hh * 128:512 + hh * 128 + L], kt[0:L, s, hh * 128:(hh + 1) * 128], idb[0:L, 0:L])
                return r
            T.op(pe, ftr, reads=[b_qt[s], b_kt[s], b_const], writes=[b_ps[pt]])
            src = psb[:, pt, :].rearrange("p (g t) -> p g t", g=8)[:, :, 0:L]
            if not is_ret:
                T.op(act, lambda e: e.copy(out=qkT[:, :, 0:L], in_=src), reads=[b_ps[pt]], writes=[b_qkT])
            else:
                T.op(dve, lambda e: e.tensor_tensor(out=qkT[:, :, 0:L], in0=src, in1=c1T[:, :, LOFF[L]:LOFF[L] + L], op=ALU.mult),
                     reads=[b_ps[pt], b_const], writes=[b_qkT])

            def fatt(e):
                r = None
                for hh in range(4):
                    r = e.matmul(ps[0:L, pa, hh * 128:hh * 128 + L], lhsT=qkT[:, 4 + hh, 0:L], rhs=qkT[:, hh, 0:L],
                                 start=True, stop=True)
                return r
            T.op(pe, fatt, reads=[b_qkT], writes=[b_ps[pa]])
            T.op(dve, lambda e: e.tensor_tensor(
                out=attb[0:L, :, 0:L], in0=ps[0:L, pa, :].rearrange("p (h t) -> p h t", h=4)[:, :, 0:L],
                in1=mask[0:L, 0:L].rearrange("p (o t) -> p o t", o=1).to_broadcast([L, 4, L]), op=ALU.mult),
                reads=[b_ps[pa], b_const], writes=[b_attb])

            def fo(e):
                r = None
                for hh in range(4):
                    bank = po0 if hh < 2 else po1
                    oo = ps[0:L, bank, (hh % 2) * 256:(hh % 2 + 1) * 256]
                    e.matmul(oo, lhsT=attb[0:L, hh, 0:L], rhs=vt[0:L, s, hh * 256:(hh + 1) * 256], start=True, stop=False)
                    r = e.matmul(oo, lhsT=qkT[:, hh, 0:L], rhs=Sbf[:, hh, :], start=False, stop=True)
                return r
            T.op(pe, fo, reads=[b_attb, b_vt[s][0], b_vt[s][1], b_qkT, b_Sbf], writes=[b_ps[po0], b_ps[po1]])

            def fkv(e):
                r = None
                for hh in range(4):
                    bank = pk0 if hh < 2 else pk1
                    r = e.matmul(ps[:, bank, (hh % 2) * 256:(hh % 2 + 1) * 256], lhsT=kt[0:L, s, hh * 128:(hh + 1) * 128],
                                 rhs=vt[0:L, s, hh * 256:(hh + 1) * 256], start=True, stop=True)
                return r
            T.op(pe, fkv, reads=[b_kt[s], b_vt[s][0], b_vt[s][1]], writes=[b_ps[pk0], b_ps[pk1]])

        def core_A2(s, dec_ap, dec_buf, S, b_S):
            pk0, pk1 = 2, 3
            for hh in range(4):
                bank = pk0 if hh < 2 else pk1
                T.op(dve, lambda e, hh=hh, bank=bank: e.scalar_tensor_tensor(
                    out=S[:, hh, :], in0=S[:, hh, :], scalar=dec_ap[:, hh:hh + 1],
                    in1=ps[:, bank, (hh % 2) * 256:(hh % 2 + 1) * 256], op0=ALU.mult, op1=ALU.add),
                    reads=[b_S, dec_buf, b_ps[bank]], writes=[b_S])

        def core_B(s, L, gt, b_gt):
            po0, po1 = (4, 5) if s % 2 == 0 else (6, 7)
            for hh in range(4):
                bank = po0 if hh < 2 else po1
                T.op(dve, lambda e, hh=hh, bank=bank: e.bn_stats(out=hst[0:L, hh, :], in_=ps[0:L, bank, (hh % 2) * 256:(hh % 2 + 1) * 256]),
                     reads=[b_ps[bank]], writes=[b_hst])
            for hh in range(4):
                T.op(dve, lambda e, hh=hh: e.bn_aggr(out=hmv[0:L, hh, :], in_=hst[0:L, hh:hh + 1, :]),
                     reads=[b_hst], writes=[b_hmv])
            T.op(dve, lambda e: e.tensor_scalar_add(out=hve[0:L, :], in0=hmv[0:L, :, 1], scalar1=GN_EPS),
                 reads=[b_hmv], writes=[b_hve])
            rstd_pow(hrs[0:L, :], hve[0:L, :], L, 4, b_hve, b_hrs)
            T.op(act, lambda e: e.activation(out=on[0:L, 0, :], in_=gt[0:L, s, 0:256], func=AF.Identity, scale=hrs[0:L, 0:1]),
                 reads=[b_gt[s][0], b_hrs], writes=[b_on])
            for hh in range(1, 4):
                T.op(act, lambda e, hh=hh: e.activation(out=on[0:L, hh, :], in_=gt[0:L, s, hh * 256:(hh + 1) * 256], func=AF.Identity,
                                                         scale=hrs[0:L, hh:hh + 1]),
                     reads=[b_gt[s][hh // 2], b_hrs, b_on], writes=[b_on])
            for hh in range(4):
                bank = po0 if hh < 2 else po1
                T.op(dve, lambda e, hh=hh, bank=bank: e.scalar_tensor_tensor(
                    out=yin[0:L, hh * 256:(hh + 1) * 256], in0=ps[0:L, bank, (hh % 2) * 256:(hh % 2 + 1) * 256],
                    scalar=hmv[0:L, hh, 0:1], in1=on[0:L, hh, :], op0=ALU.subtract, op1=ALU.mult),
                    reads=[b_ps[bank], b_hmv, b_on, b_yin] if hh else [b_ps[bank], b_hmv, b_on], writes=[b_yin])

        def core_B2(s, L, gn_ap, yT, b_yT):
            py = 2

            def fty(e):
                r = None
                for c in range(8):
                    r = e.transpose(psb[:, py, c * 128:c * 128 + L], yin[0:L, c * 128:(c + 1) * 128], idb[0:L, 0:L])
                return r
            T.op(pe, fty, reads=[b_yin, b_const], writes=[b_ps[py]])
            T.op(dve, lambda e: e.tensor_tensor(
                out=yT[:, :, s * 128:s * 128 + L], in0=psb[:, py, :].rearrange("p (c t) -> p c t", c=8)[:, :, 0:L],
                in1=gn_ap.rearrange("p (c o) -> p c o", o=1).to_broadcast([128, 8, L]), op=ALU.mult),
                reads=[b_ps[py], b_const], writes=[b_yT[s]])

        def mixer_cores(ns, L, dec_fn, S, b_S, is_ret, gn_ap, yT, b_yT, vt, b_vt, gt, b_gt, fillers=()):
            fillers = list(fillers)
            for s in range(ns):
                dec_ap, dec_buf = dec_fn(s)
                core_A(s, L, dec_ap, dec_buf, S, b_S, is_ret, vt, b_vt)
                if s >= 1:
                    core_B(s - 1, L, gt, b_gt)
                core_A2(s, dec_ap, dec_buf, S, b_S)
                if fillers:
                    fillers.pop(0)()
                if s >= 1:
                    core_B2(s - 1, L, gn_ap, yT, b_yT)
            core_B(ns - 1, L, gt, b_gt)
            while fillers:
                fillers.pop(0)()
            core_B2(ns - 1, L, gn_ap, yT, b_yT)

        def proj_tok(s, L, w_v, w_b, bank=None):
            pz = palloc() if bank is None else bank

            def f(e):
                r = None
                for k in range(8):
                    r = e.matmul(ps[0:L, pz, :], lhsT=hT[:, k, s * 128:s * 128 + L], rhs=w_v[:, k, :],
                                 start=(k == 0), stop=(k == 7))
                return r
            T.op(pe, f, reads=[w_b] + b_hT[s], writes=[b_ps[pz]])
            return pz

        def rotary(s, L, pz, out_ap, out_bufs):
            T.op(act, lambda e: e.copy(out=zc[0:L, :], in_=ps[0:L, pz, :]), reads=[b_ps[pz]], writes=[b_zc])
            z4 = zc[0:L, :].rearrange("p (h w j) -> p h w j", h=4, w=2)
            a4 = za[0:L, :].rearrange("p (h w j) -> p h w j", h=4, w=2)
            t4 = ztmp[0:L, :].rearrange("p (h w j) -> p h w j", h=4, w=2)
            cos_b = cosT[0:L, s, :].rearrange("p (o j) -> p o j", o=1).to_broadcast([L, 8, 64])
            sin_b = sinT[0:L, s, :].rearrange("p (o j) -> p o j", o=1).to_broadcast([L, 4, 64])
            nsin_b = nsinT[0:L, s, :].rearrange("p (o j) -> p o j", o=1).to_broadcast([L, 4, 64])
            T.op(dve, lambda e: e.tensor_tensor(out=za[0:L, :].rearrange("p (g j) -> p g j", g=8),
                                                 in0=zc[0:L, :].rearrange("p (g j) -> p g j", g=8), in1=cos_b, op=ALU.mult),
                 reads=[b_zc, b_cs], writes=[b_za])
            T.op(pool, lambda e: e.tensor_tensor(out=t4[:, :, 0, :], in0=z4[:, :, 1, :], in1=nsin_b, op=ALU.mult),
                 reads=[b_zc, b_cs], writes=[b_ztmp])
            T.op(pool, lambda e: e.tensor_tensor(out=t4[:, :, 1, :], in0=z4[:, :, 0, :], in1=sin_b, op=ALU.mult),
                 reads=[b_zc, b_cs], writes=[b_ztmp])
            T.op(dve, lambda e: e.tensor_tensor(out=out_ap, in0=za[0:L, :], in1=ztmp[0:L, :], op=ALU.add),
                 reads=[b_za, b_ztmp], writes=out_bufs)

        def mixers(tile, l):
            ns, L, ntok = tile["ns"], tile["L"], tile["ntok"]
            T.tag = "mix.gla_qk"
            li = LIDX[L]
            W = wsrc("w_in", l).rearrange("(k p) n -> p k n", p=128)
            wdeps = [b_w["w_in"][l]]
            wag_v, wag_b = wload(W[:, :, 3072:3088], 8, 16, wdeps[0])
            pg = palloc()

            def fag(e):
                r = None
                for k in range(8):
                    r = e.matmul(ps[0:16, pg, 0:ntok], lhsT=wag_v[:, k, :], rhs=hT[:, k, 0:ntok], start=(k == 0), stop=(k == 7))
                return r
            T.op(pe, fag, reads=[wag_b] + flat(b_hT[0:ns]), writes=[b_ps[pg]])
            T.op(act, lambda e: e.copy(out=agT[:, 0:ntok], in_=ps[0:16, pg, 0:ntok]), reads=[b_ps[pg]], writes=[b_agT])
            if MIX_STOP < 2:
                return
            wq_v, wq_b = wload(W[:, :, 0:512], 8, 512, wdeps[0])
            wk_v, wk_b = wload(W[:, :, 512:1024], 8, 512, wdeps[0])
            def gla_part1(s):
                px = palloc()
                T.op(pe, lambda e, s=s, px=px: e.matmul(ps[0:L, px, :], lhsT=agT[:, s * 128:s * 128 + L], rhs=wa2[:, :],
                                                         start=True, stop=True),
                     reads=[b_agT, b_lp], writes=[b_ps[px]])
                T.op(dve, lambda e, px=px: e.tensor_tensor(out=xb[0:L, :], in0=ps[0:L, px, :], in1=ba_bc[0:L, :], op=ALU.add),
                     reads=[b_ps[px], b_lp], writes=[b_xb])
                T.op(act, lambda e: e.activation(out=ee[0:L, :], in_=xb[0:L, :], func=AF.Exp, scale=-1.0),
                     reads=[b_xb], writes=[b_ee])
                T.op(act, lambda e: e.activation(out=ltok[0:L, :], in_=ee[0:L, :], func=AF.Ln, bias=1.0),
                     reads=[b_ee], writes=[b_ltok])

            def gla_part2(s):
                pd, pbl = palloc(), palloc()
                T.op(pe, lambda e, pd=pd: e.matmul(ps[0:L, pd, :], lhsT=tri[0:L, 0:L], rhs=ltok[0:L, :], start=True, stop=True),
                     reads=[b_ltok, b_const], writes=[b_ps[pd]])

                def fbl(e, pbl=pbl):
                    r = None
                    for hh in range(4):
                        r = e.matmul(ps[:, pbl, hh:hh + 1], lhsT=ltok[0:L, hh * 128:(hh + 1) * 128], rhs=neg16[0:L, 0:1],
                                     start=True, stop=True)
                    return r
                T.op(pe, fbl, reads=[b_ltok, b_const], writes=[b_ps[pbl]])
                T.op(act, lambda e, pd=pd: e.activation(out=E1[0:L, :], in_=ps[0:L, pd, :], func=AF.Exp,
                                                         bias=float(math.log(128.0 ** -0.5)), scale=1.0),
                     reads=[b_ps[pd]], writes=[b_E1])
                T.op(act, lambda e, pd=pd: e.activation(out=E2[0:L, :], in_=ps[0:L, pd, :], func=AF.Exp, scale=-1.0),
                     reads=[b_ps[pd]], writes=[b_E2])
                T.op(act, lambda e, pbl=pbl, s=s: e.activation(out=decg[:, s, :], in_=ps[:, pbl, 0:4], func=AF.Exp),
                     reads=[b_ps[pbl]], writes=[b_decg[s]])

            def gla_part3(s):
                pq = proj_tok(s, L, wq_v, wq_b)
                T.op(dve, lambda e, pq=pq, s=s: e.tensor_tensor(out=qt[0:L, s, :], in0=ps[0:L, pq, :], in1=E1[0:L, :], op=ALU.mult),
                     reads=[b_ps[pq], b_E1], writes=[b_qt[s]])
                pk = proj_tok(s, L, wk_v, wk_b)
                T.op(dve, lambda e, pk=pk, s=s: e.tensor_tensor(out=kt[0:L, s, :], in0=ps[0:L, pk, :], in1=E2[0:L, :], op=ALU.mult),
                     reads=[b_ps[pk], b_E2], writes=[b_kt[s]])

            for s in range(ns):
                gla_part1(s)
                if s > 0:
                    gla_part3(s - 1)
                gla_part2(s)
            gla_part3(ns - 1)

            def v_piece(c0, j, vt, b_vt, banks=None):
                w_v, w_b = wload(W[:, :, c0 + 512 * j:c0 + 512 * (j + 1)], 8, 512, wdeps[0])
                for s in range(ns):
                    pz = proj_tok(s, L, w_v, w_b, None if banks is None else banks[s % 2])
                    T.op(act, lambda e, pz=pz, s=s: e.copy(out=vt[0:L, s, 512 * j:512 * (j + 1)], in_=ps[0:L, pz, :]),
                         reads=[b_ps[pz]], writes=[b_vt[s][j]])

            def g_piece(c0, j, gt, b_gt, banks=None):
                w_v, w_b = wload(W[:, :, c0 + 512 * j:c0 + 512 * (j + 1)], 8, 512, wdeps[0])
                for s in range(ns):
                    pz = proj_tok(s, L, w_v, w_b, None if banks is None else banks[s % 2])
                    T.op(act, lambda e, pz=pz, s=s: e.activation(out=gt[0:L, s, 512 * j:512 * (j + 1)], in_=ps[0:L, pz, :],
                                                                  func=AF.Silu),
                         reads=[b_ps[pz]], writes=[b_gt[s][j]])

            def vg_pieces(c0v, c0g, vt, b_vt, gt, b_gt):
                for j in range(2):
                    v_piece(c0v, j, vt, b_vt)
                for j in range(2):
                    g_piece(c0g, j, gt, b_gt)

            if MIX_STOP < 3:
                return
            T.tag = "mix.gla_vg"
            vg_pieces(1024, 2048, vv, b_vv, sgate, b_sg)
            T.tag = "mix.gla_core"
            if MIX_STOP < 4:
                return
            tile["state_load"](l, "g")
            fill = [lambda j=j: v_piece(4112, j, vv2, b_vv2, banks=(0, 1)) for j in range(2)] + \
                   [lambda j=j: g_piece(5136, j, sgate2, b_sg2, banks=(0, 1)) for j in range(2)]
            mixer_cores(ns, L, lambda s: (decg[:, s, :], b_decg[s]), Sg, b_Sg, False, gng[:, l, :], yinTa, b_yTa,
                        vv, b_vv, sgate, b_sg, fillers=fill)
            tile["state_store"](l, "g")
            if MIX_STOP < 5:
                return
            T.tag = "mix.ret_qk"
            wq_v, wq_b = wload(W[:, :, 3088:3600], 8, 512, wdeps[0])
            wk_v, wk_b = wload(W[:, :, 3600:4112], 8, 512, wdeps[0])
            for s in range(ns):
                pq = proj_tok(s, L, wq_v, wq_b)
                rotary(s, L, pq, qt[0:L, s, :], [b_qt[s]])
                pk = proj_tok(s, L, wk_v, wk_b)
                rotary(s, L, pk, rot[0:L, :], [b_rot])
                T.op(dve, lambda e, s=s: e.tensor_tensor(
                    out=kt[0:L, s, :].rearrange("p (h d) -> p h d", h=4), in0=rot[0:L, :].rearrange("p (h d) -> p h d", h=4),
                    in1=c2[0:L, li, :].rearrange("p (h o) -> p h o", o=1).to_broadcast([L, 4, 128]), op=ALU.mult),
                    reads=[b_rot, b_const], writes=[b_kt[s]])
            if MIX_STOP < 6:
                return
            T.tag = "mix.ret_core"
            tile["state_load"](l, "r")
            mixer_cores(ns, L, lambda s: (decr[:, li, :], b_const), Sr, b_Sr, True, gnr[:, l, :], yinTb, b_yTb,
                        vv2, b_vv2, sgate2, b_sg2)
            tile["state_store"](l, "r")
            if MIX_STOP < 7:
                return
            T.tag = "mix.merge"
            woa = wsrc("w_oa", l).rearrange("(k p) n -> p k n", p=128)
            wob = wsrc("w_ob", l).rearrange("(k p) n -> p k n", p=128)
            for cg in range(2):
                a_v, a_b = wload(woa[:, :, 512 * cg:512 * (cg + 1)], 8, 512, b_w["w_oa"][l])
                b_v, b_b = wload(wob[:, :, 512 * cg:512 * (cg + 1)], 8, 512, b_w["w_ob"][l])
                ma_v, ma_b = wload(W[:, :, 6160 + 512 * cg:6160 + 512 * (cg + 1)], 8, 512, wdeps[0])
                mb_v, mb_b = wload(W[:, :, 7184 + 512 * cg:7184 + 512 * (cg + 1)], 8, 512, wdeps[0])
                for c4 in range(4):
                    c = 4 * cg + c4
                    banks = []
                    for (w_v, w_b, src, srcb) in ((a_v, a_b, yinTa, b_yTa), (b_v, b_b, yinTb, b_yTb),
                                                  (ma_v, ma_b, hT, b_hT), (mb_v, mb_b, hT, b_hT)):
                        pz = palloc()

                        def f(e, w_v=w_v, src=src, pz=pz, c4=c4):
                            r = None
                            for k in range(8):
                                r = e.matmul(ps[:, pz, 0:ntok], lhsT=w_v[:, k, c4 * 128:(c4 + 1) * 128], rhs=src[:, k, 0:ntok],
                                             start=(k == 0), stop=(k == 7))
                            return r
                        T.op(pe, f, reads=[w_b] + flat(srcb[0:ns]), writes=[b_ps[pz]])
                        banks.append(pz)
                    pya, pyb, pma, pmb = banks
                    T.op(act, lambda e, pma=pma, c=c: e.activation(out=sgA[:, 0:ntok], in_=ps[:, pma, 0:ntok], func=AF.Sigmoid,
                                                                    bias=bm[:, l, c:c + 1], scale=1.0),
                         reads=[b_ps[pma], b_const], writes=[b_sgA])
                    T.op(act, lambda e, pmb=pmb, c=c: e.activation(out=sgB[:, 0:ntok], in_=ps[:, pmb, 0:ntok], func=AF.Sigmoid,
                                                                    bias=bm[:, l, 8 + c:9 + c], scale=1.0),
                         reads=[b_ps[pmb], b_const], writes=[b_sgB])
                    T.op(dve, lambda e, pya=pya: e.tensor_tensor(out=t1[:, 0:ntok], in0=sgA[:, 0:ntok], in1=ps[:, pya, 0:ntok], op=ALU.mult),
                         reads=[b_sgA, b_ps[pya]], writes=[b_t1])
                    T.op(dve, lambda e, pyb=pyb: e.tensor_tensor(out=t2[:, 0:ntok], in0=sgB[:, 0:ntok], in1=ps[:, pyb, 0:ntok], op=ALU.mult),
                         reads=[b_sgB, b_ps[pyb]], writes=[b_t2])
                    T.op(dve, lambda e, c=c: e.tensor_tensor(out=mT[:, c, 0:ntok], in0=t1[:, 0:ntok], in1=t2[:, 0:ntok], op=ALU.add),
                         reads=[b_t1, b_t2], writes=[b_mT[c]])
            if MIX_STOP < 8:
                return
            T.tag = "mix.outproj"
            wo = wsrc("w_o", l).rearrange("(k p) n -> p k n", p=128)
            wo_p = [wload(wo[:, :, 512 * hf:512 * (hf + 1)], 8, 512, b_w["w_o"][l]) for hf in range(2)]
            for s in range(ns):
                for hf in range(2):
                    po = palloc()

                    def f(e, s=s, hf=hf, po=po):
                        r = None
                        for k in range(8):
                            r = e.matmul(ps[0:L, po, :], lhsT=mT[:, k, s * 128:s * 128 + L], rhs=wo_p[hf][0][:, k, :],
                                         start=(k == 0), stop=(k == 7))
                        return r
                    T.op(pe, f, reads=[wo_p[hf][1]] + b_mT, writes=[b_ps[po]])
                    residual_add(s, L, hf, po, first=True)
                ln_front(s, L)

        def run_tile(tile):
            ns, L = tile["ns"], tile["L"]
            T.dma(pool, s_x, lambda e: e.dma_start(out=h[0:L, 0:ns, :], in_=tile["x"].rearrange("(s p) d -> p s d", p=L)),
                  writes=[b for s in range(ns) for b in b_h[s]])
            pos0 = tile["pos0"]

            def fcs(e):
                return [e.dma_start(out=dst[0:L, 0:ns, :], in_=src[pos0:pos0 + ns * L, :].rearrange("(s p) j -> p s j", p=L))
                        for dst, src in ((cosT, c_cos), (sinT, c_sin), (nsinT, c_nsin))]
            T.dma(pool, s_cs, fcs, writes=[b_cs], n=3)
            if DEBUG_STOP >= 1:
                for s in range(ns):
                    transpose_to_hT(s, L)
            for l in range(depth):
                if DEBUG_STOP < 2:
                    break
                convert_layer(l + 1)
                load_lp(l)
                load_ln(l, 0)
                ffn(tile, "w1u", "w1d", l)
                if DEBUG_STOP < 3:
                    break
                layer_norm_all(tile, l, 0)
                if DEBUG_STOP < 4:
                    break
                load_ln(l, 1)
                mixers(tile, l)
                if DEBUG_STOP < 5:
                    break
                layer_norm_all(tile, l, 1)
                load_ln(l, 2)
                ffn(tile, "w2u", "w2d", l)
                layer_norm_all(tile, l, 2)
            if tile["y"] is not None:
                T.dma(pool, s_y, lambda e: e.dma_start(out=tile["y"].rearrange("(s p) d -> p s d", p=L), in_=h[0:L, 0:ns, :]),
                      reads=[b for s in range(ns) for b in b_h[s]], is_out=True)

        def make_state_fns(src_g, src_r, dst_g, dst_r, out_g, out_r):
            def load(l, which):
                S, b_S, sem = (Sg, b_Sg, s_stl_g) if which == "g" else (Sr, b_Sr, s_stl_r)
                src = (src_g if which == "g" else src_r)
                if src is None:
                    T.op(dve, lambda e: e.memset(S[:].rearrange("p h e -> p (h e)"), 0.0), writes=[b_S])
                    return
                ap, bb = src(l)
                T.dma(pool, sem, lambda e: e.dma_start(out=S[:], in_=ap.rearrange("h d e -> d h e")),
                      reads=([bb] if bb is not None else []), writes=[b_S])

            def store(l, which):
                if NOSTORE:
                    return
                S, b_S, sem = (Sg, b_Sg, s_sts_g) if which == "g" else (Sr, b_Sr, s_sts_r)
                ap, bb = (dst_g if which == "g" else dst_r)(l)
                is_out = out_g if which == "g" else out_r
                T.dma(pool, sem, lambda e: e.dma_start(out=ap.rearrange("h d e -> d h e"), in_=S[:]),
                      reads=[b_S], writes=([bb] if bb is not None else []), is_out=is_out)
            return load, store

        load_consts()
        convert_layer(0)
        if with_meta:
            ld, stf = make_state_fns(None, None, lambda l: (smeta_g[l], b_smg[l]), lambda l: (smeta_r[l], b_smr[l]), False, False)
            run_tile(dict(ns=1, L=16, ntok=16, x=meta, y=None, pos0=0, state_load=ld, state_store=stf))
        if with_sample:
            ld, stf = make_state_fns(lambda l: (sg_in[l], None), lambda l: (sr_in[l], None),
                                     lambda l: (gs[l], None), lambda l: (rs[l], None), True, True)
            run_tile(dict(ns=1, L=32, ntok=32, x=xs, y=ys, pos0=N_META + PAST_LEN, state_load=ld, state_store=stf))
        for q in range(n_seq):
            for k in range(tiles_per_seq):
                first, last = (k == 0), (k == tiles_per_seq - 1)
                if first:
                    sgf = (lambda l: (smeta_g[l], b_smg[l])) if with_meta else None
                    srf = (lambda l: (smeta_r[l], b_smr[l])) if with_meta else None
                else:
                    sgf = lambda l: (scr_g[l], b_scg[l])
                    srf = lambda l: (scr_r[l], b_scr[l])
                if last:
                    dgf = lambda l, q=q: (gp[l, q], None)
                    drf = lambda l, q=q: (rp[l, q], None)
                else:
                    dgf = lambda l: (scr_g[l], b_scg[l])
                    drf = lambda l: (scr_r[l], b_scr[l])
                ld, stf = make_state_fns(sgf, srf, dgf, drf, last, last)
                run_tile(dict(ns=nsub, L=128, ntok=NT, x=xp[q, k * NT:(k + 1) * NT, :], y=yp[q, k * NT:(k + 1) * NT, :],
                              pos0=N_META + k * NT, state_load=ld, state_store=stf))
        T.finish()
        T.replay()
        global LAST_TRACKER
        LAST_TRACKER = T
    return nc


def _const_inputs():
    idf = np.eye(128, dtype=np.float32)
    s_idx = np.arange(128)[:, None]
    t_idx = np.arange(128)[None, :]
    mask = (t_idx >= s_idx).astype(np.float32)
    tri = (s_idx > t_idx).astype(np.float32) / np.float32(16.0)
    half = 64
    inv = (np.float32(10000.0) ** (-np.arange(half, dtype=np.float32) / np.float32(half))).astype(np.float32)
    pos = np.arange(N_META + 2048, dtype=np.float32)
    ang = (pos[:, None] * inv[None, :]).astype(np.float32)
    cos = np.cos(ang).astype(np.float32)
    sin = np.sin(ang).astype(np.float32)
    log_gamma = np.log1p(-(2.0 ** (-5.0 - np.arange(4, dtype=np.float64))))
    c1T = np.ones((8, 176), np.float32)
    c2 = np.zeros((128, 3, 4), np.float32)
    decr = np.zeros((3, 4), np.float32)
    for L in LVARS:
        t = np.arange(L, dtype=np.float64)
        c1T[0:4, LOFF[L]:LOFF[L] + L] = np.exp((t[None, :] - (L - 1.0)) * log_gamma[:, None])
        c2[:L, LIDX[L], :] = np.exp((L - 1.0 - t)[:, None] * log_gamma[None, :]) * (128.0 ** -0.5)
        decr[LIDX[L], :] = np.exp(L * log_gamma)
    return dict(c_idf=idf, c_mask=mask, c_tri=tri, c_cos=cos, c_sin=sin, c_nsin=(-sin).astype(np.float32),
                c_c1T=c1T, c_c2=c2, c_decr=decr)


def run_cores(inputs, n_cores, n_seq, seq_len, depth, nsub, with_meta=True, with_sample=True, trace=False):
    f = lambda a: np.ascontiguousarray(np.asarray(a, dtype=np.float32))
    nc = build_program(n_seq, seq_len, depth, nsub, with_meta, with_sample)
    consts = _const_inputs()
    shared = dict(
        meta=f(inputs["meta"]), ln_g=f(inputs["ln_g"][:depth]), ln_b=f(inputs["ln_b"][:depth]),
        w1u=f(inputs["w_ffn1_up"][:depth]), w1d=f(inputs["w_ffn1_down"][:depth]), w_in=f(inputs["w_in"][:depth]),
        w_a2=f(inputs["w_alpha2"][:depth]), b_a=f(inputs["b_alpha"][:depth]),
        bm_t=f(np.asarray(inputs["b_merge"][:depth]).reshape(depth, 16, 128).transpose(2, 0, 1)),
        gng_t=f(np.asarray(inputs["gn_gla"][:depth]).reshape(depth, 8, 128).transpose(2, 0, 1)),
        gnr_t=f(np.asarray(inputs["gn_ret"][:depth]).reshape(depth, 8, 128).transpose(2, 0, 1)),
        lng_t=f(np.asarray(inputs["ln_g"][:depth]).reshape(depth, 3, 8, 128).transpose(3, 0, 1, 2)),
        lnb_t=f(np.asarray(inputs["ln_b"][:depth]).reshape(depth, 3, 8, 128).transpose(3, 0, 1, 2)),
        w_oa=f(inputs["w_o_gla"][:depth]), w_ob=f(inputs["w_o_ret"][:depth]), w_o=f(inputs["w_out"][:depth]),
        w2u=f(inputs["w_ffn2_up"][:depth]), w2d=f(inputs["w_ffn2_down"][:depth]),
    )
    shared.update(consts)
    xpr = np.asarray(inputs["x_prompt"])
    xsm = np.asarray(inputs["x_sample"])
    sgl = np.asarray(inputs["state_gla"])
    srt = np.asarray(inputs["state_ret"])
    in_maps = []
    for c in range(n_cores):
        m = dict(shared)
        m["xp"] = f(xpr[c * n_seq:(c + 1) * n_seq, :seq_len])
        m["xs"] = f(xsm[c])
        m["sg_in"] = f(sgl[:depth, c])
        m["sr_in"] = f(srt[:depth, c])
        in_maps.append(m)
    res = run_bass_kernel_spmd(nc, in_maps, core_ids=list(range(n_cores)), trace=trace)
    R = res.results
    y_prompt = np.concatenate([r["yp"] for r in R], axis=0)
    y_sample = np.stack([r["ys"] for r in R], axis=0)
    gla_p = np.concatenate([r["gp"] for r in R], axis=1)
    ret_p = np.concatenate([r["rp"] for r in R], axis=1)
    gla_s = np.stack([r["gs"] for r in R], axis=1)
    ret_s = np.stack([r["rs"] for r in R], axis=1)
    return (y_prompt, y_sample, gla_p, ret_p, gla_s, ret_s), res


def kernel(**inputs):
    outs, _ = run_cores(inputs, n_cores=8, n_seq=4, seq_len=2048, depth=DEPTH, nsub=4)
    return tuple(np.ascontiguousarray(o.astype(np.float32, copy=False)) for o in outs)
```
